# Optimizing a Trainium2 kernel written in Bass

```python
import jax
import jax.numpy as jnp
from jax import lax
import numpy as np

D_MODEL = 1024
BATCH = 8
SEQ = 4096
DEPTH = 2

GRID_W = 64
CTX_LEN = 256
N_MOD = 9
D_FF = 2816
RMS_EPS = 1e-6
NEG_INF = -1e30

GLA_HEADS = 4
GLA_DK = 128
GLA_DV = 256
GLA_LR = 16
GLA_TAU = 16.0
GLA_CHUNK = 64

NAT_HEADS = 16
NAT_HD = 64
WIN_R = 8
WIN_C = 16

GQA_HEADS = 8
GQA_KV_HEADS = 2
GQA_HD = 128
Q_BLOCK = 128
ROPE_BASE = 10000.0

N_BRANCH = 3
BRANCH_W = GLA_HEADS * GLA_DV

IN_SPLITS = (
    GLA_HEADS * GLA_DK,
    GLA_HEADS * GLA_DK,
    GLA_HEADS * GLA_DV,
    GLA_HEADS * GLA_DV,
    2 * GLA_LR,
    NAT_HEADS * NAT_HD,
    NAT_HEADS * NAT_HD,
    NAT_HEADS * NAT_HD,
    GQA_HEADS * GQA_HD,
    GQA_KV_HEADS * GQA_HD,
    GQA_KV_HEADS * GQA_HD,
    D_MODEL, D_MODEL, D_MODEL,
)
IN_WIDTH = sum(IN_SPLITS)

kernel_name = 'hybrid_gla_natten_gqa_macaron_dit'


def rms_norm(x, g):
    xf = x.astype(jnp.float32)
    y = xf * lax.rsqrt(jnp.mean(xf * xf, axis=-1, keepdims=True) + RMS_EPS)
    return (y * g.astype(jnp.float32)).astype(x.dtype)


def modulate(h, shift, scale):
    return h * (1.0 + scale) + shift


def swiglu(h, w_up, w_down):
    a, b = jnp.split(h @ w_up, 2, axis=-1)
    return (jax.nn.silu(a) * b) @ w_down


def to_heads(t, n):
    b, l, _ = t.shape
    return t.reshape(b, l, n, -1).transpose(0, 2, 1, 3)


def merge_heads(t):
    b, n, l, d = t.shape
    return t.transpose(0, 2, 1, 3).reshape(b, l, n * d)


def flip(t):
    return jnp.flip(t, axis=2)


def rope_1d(x, pos):
    half = x.shape[-1] // 2
    freqs = ROPE_BASE ** (-jnp.arange(half, dtype=jnp.float32) / half)
    ang = pos.astype(jnp.float32)[:, None] * freqs
    cos, sin = jnp.cos(ang), jnp.sin(ang)
    xf = x.astype(jnp.float32)
    x1, x2 = xf[..., :half], xf[..., half:]
    return jnp.concatenate([x1 * cos - x2 * sin, x2 * cos + x1 * sin], axis=-1).astype(x.dtype)


def rope_2d(x, rows, cols):
    half = x.shape[-1] // 2
    return jnp.concatenate([rope_1d(x[..., :half], rows), rope_1d(x[..., half:], cols)], axis=-1)


def grouped_attention(q, k, v):
    b, h, l, d = q.shape
    kvh = k.shape[1]
    nb = l // Q_BLOCK
    qb = jnp.moveaxis(q.reshape(b, kvh, h // kvh, nb, Q_BLOCK, d), 3, 0)

    def block(qi):
        s = jnp.einsum('bkgqd,bksd->bkgqs', qi, k).astype(jnp.float32)
        p = jax.nn.softmax(s, axis=-1).astype(v.dtype)
        return jnp.einsum('bkgqs,bksd->bkgqd', p, v)

    o = lax.map(block, qb)
    return jnp.moveaxis(o, 0, 3).reshape(b, h, l, d)


def neighbourhood_attention(q, k, v, k_ctx, v_ctx, rpb):
    b, n, l, d = q.shape
    rows = l // GRID_W
    kr = min(WIN_R, rows)
    qg = q.reshape(b, n, rows, GRID_W, d)
    kg = k.reshape(b, n, rows, GRID_W, d)
    vg = v.reshape(b, n, rows, GRID_W, d)
    col = jnp.arange(GRID_W)
    c_start = jnp.clip(col - WIN_C // 2, 0, GRID_W - WIN_C)
    col_mask = (col[None, :] >= c_start[:, None]) & (col[None, :] < c_start[:, None] + WIN_C)
    dc = jnp.clip(col[None, :] - col[:, None] + WIN_C - 1, 0, 2 * WIN_C - 2)
    rpb_c = rpb.astype(jnp.float32)[:, :, dc]

    def row_block(r):
        r_start = jnp.clip(r - kr // 2, 0, rows - kr)
        k_s = lax.dynamic_slice_in_dim(kg, r_start, kr, axis=2)
        v_s = lax.dynamic_slice_in_dim(vg, r_start, kr, axis=2)
        q_r = lax.dynamic_index_in_dim(qg, r, axis=2, keepdims=False)
        dr = r_start + jnp.arange(kr) - r + WIN_R - 1
        bias = rpb_c[:, dr].transpose(0, 2, 1, 3)
        s = jnp.einsum('bhqd,bhikd->bhqik', q_r, k_s).astype(jnp.float32) + bias
        s = jnp.where(col_mask[:, None, :], s, NEG_INF).reshape(b, n, GRID_W, kr * GRID_W)
        s_ctx = jnp.einsum('bhqd,bhcd->bhqc', q_r, k_ctx).astype(jnp.float32)
        p = jax.nn.softmax(jnp.concatenate([s, s_ctx], axis=-1), axis=-1).astype(v.dtype)
        p_lat = p[..., :kr * GRID_W].reshape(b, n, GRID_W, kr, GRID_W)
        return (jnp.einsum('bhqik,bhikd->bhqd', p_lat, v_s)
                + jnp.einsum('bhqc,bhcd->bhqd', p[..., kr * GRID_W:], v_ctx))

    o = lax.map(row_block, jnp.arange(rows))
    return jnp.moveaxis(o, 0, 2).reshape(b, n, l, d)


def gla_chunk_scan(q, k, v, log_a, s0):
    b, n, l, _ = q.shape
    dv = v.shape[-1]
    nc = l // GLA_CHUNK

    def chunks(t):
        return t.astype(jnp.float32).reshape(b, n, nc, GLA_CHUNK, t.shape[-1])

    qc, kc, vc = chunks(q), chunks(k), chunks(v)
    g = jnp.cumsum(chunks(log_a), axis=3)
    g_last = g[:, :, :, -1, :]
    q_e = qc * jnp.exp(g)
    k_in = kc * jnp.exp(-g)
    k_end = kc * jnp.exp(g_last[:, :, :, None, :] - g)
    lower = jnp.tril(jnp.ones((GLA_CHUNK, GLA_CHUNK), dtype=bool))
    att = jnp.where(lower, jnp.einsum('bhnid,bhnjd->bhnij', q_e, k_in), 0.0)
    o_intra = jnp.einsum('bhnij,bhnjv->bhniv', att, vc)

    def step(state, inp):
        q_t, k_t, v_t, gl_t = inp
        o_t = jnp.einsum('bhcd,bhdv->bhcv', q_t, state)
        state = jnp.exp(gl_t)[..., None] * state + jnp.einsum('bhcd,bhcv->bhdv', k_t, v_t)
        return state, o_t

    xs = tuple(jnp.moveaxis(t, 2, 0) for t in (q_e, k_end, vc, g_last))
    _, o_inter = lax.scan(step, s0.astype(jnp.float32), xs)
    return (o_intra + jnp.moveaxis(o_inter, 0, 2)).reshape(b, n, l, dv)


def gla_final_state(k, v, log_a):
    g = jnp.cumsum(log_a.astype(jnp.float32), axis=2)
    k_dec = k.astype(jnp.float32) * jnp.exp(g[:, :, -1:, :] - g)
    return jnp.einsum('bhld,bhlv->bhdv', k_dec, v.astype(jnp.float32))


def gla_bidir(q, k, v, la_f, la_b, s_f, s_b):
    o_f = gla_chunk_scan(q, k, v, la_f, s_f)
    o_b = flip(gla_chunk_scan(flip(q), flip(k), flip(v), flip(la_b), s_b))
    return o_f + o_b


def gla_log_decay(fg, w2, b2, direction):
    z = fg[..., direction * GLA_LR:(direction + 1) * GLA_LR] @ w2[direction] + b2[direction]
    return to_heads(jax.nn.log_sigmoid(z.astype(jnp.float32)) / GLA_TAU, GLA_HEADS)


def gla_branch(p, pc, fg_w2, fg_b, gain, need_ctx):
    k = to_heads(p[1], GLA_HEADS)
    v = to_heads(p[2], GLA_HEADS)
    kc = to_heads(pc[1], GLA_HEADS)
    vc = to_heads(pc[2], GLA_HEADS)
    lac_f = gla_log_decay(pc[4], fg_w2, fg_b, 0)
    lac_b = gla_log_decay(pc[4], fg_w2, fg_b, 1)
    s_f = gla_final_state(kc, vc, lac_f)
    s_b = gla_final_state(flip(kc), flip(vc), flip(lac_b))
    q = to_heads(p[0], GLA_HEADS) * GLA_DK ** -0.5
    o = gla_bidir(q, k, v, gla_log_decay(p[4], fg_w2, fg_b, 0), gla_log_decay(p[4], fg_w2, fg_b, 1), s_f, s_b)
    y = merge_heads(rms_norm(o, gain).astype(p[3].dtype)) * jax.nn.silu(p[3])
    if not need_ctx:
        return y, None
    qc = to_heads(pc[0], GLA_HEADS) * GLA_DK ** -0.5
    zero = jnp.zeros_like(s_f)
    oc = gla_bidir(qc, kc, vc, lac_f, lac_b, zero, zero)
    return y, merge_heads(rms_norm(oc, gain).astype(pc[3].dtype)) * jax.nn.silu(pc[3])


def nat_branch(p, pc, q_gain, k_gain, rpb, need_ctx):
    k = rms_norm(to_heads(p[1], NAT_HEADS), k_gain)
    v = to_heads(p[2], NAT_HEADS)
    kc = rms_norm(to_heads(pc[1], NAT_HEADS), k_gain)
    vc = to_heads(pc[2], NAT_HEADS)
    q = rms_norm(to_heads(p[0], NAT_HEADS), q_gain) * NAT_HD ** -0.5
    y = merge_heads(neighbourhood_attention(q, k, v, kc, vc, rpb))
    if not need_ctx:
        return y, None
    qc = rms_norm(to_heads(pc[0], NAT_HEADS), q_gain) * NAT_HD ** -0.5
    return y, merge_heads(grouped_attention(qc, kc, vc))


def gqa_branch(p, pc, q_gain, k_gain, rows, cols, need_ctx):
    k = rope_2d(rms_norm(to_heads(p[1], GQA_KV_HEADS), k_gain), rows, cols)
    v = to_heads(p[2], GQA_KV_HEADS)
    kc = rms_norm(to_heads(pc[1], GQA_KV_HEADS), k_gain)
    vc = to_heads(pc[2], GQA_KV_HEADS)
    q = rope_2d(rms_norm(to_heads(p[0], GQA_HEADS), q_gain), rows, cols) * GQA_HD ** -0.5
    y = merge_heads(grouped_attention(q, jnp.concatenate([kc, k], axis=2), jnp.concatenate([vc, v], axis=2)))
    if not need_ctx:
        return y, None
    qc = rms_norm(to_heads(pc[0], GQA_HEADS), q_gain) * GQA_HD ** -0.5
    return y, merge_heads(grouped_attention(qc, kc, vc))


def token_mixer(u, uc, rows, cols, w_in, gla_fg_w2, gla_fg_b, gla_norm_g, nat_q_norm, nat_k_norm,
                nat_rpb, gqa_q_norm, gqa_k_norm, w_branch, w_out, need_ctx):
    split_at = [int(s) for s in np.cumsum(IN_SPLITS)[:-1]]
    p = jnp.split(u @ w_in, split_at, axis=-1)
    pc = jnp.split(uc @ w_in, split_at, axis=-1)
    outs = (
        gla_branch(p[0:5], pc[0:5], gla_fg_w2, gla_fg_b, gla_norm_g, need_ctx),
        nat_branch(p[5:8], pc[5:8], nat_q_norm, nat_k_norm, nat_rpb, need_ctx),
        gqa_branch(p[8:11], pc[8:11], gqa_q_norm, gqa_k_norm, rows, cols, need_ctx),
    )

    def merge(ys, gates):
        z = jax.nn.sigmoid(gates[0]) * (ys[0] @ w_branch[0])
        for i in range(1, N_BRANCH):
            z = z + jax.nn.sigmoid(gates[i]) * (ys[i] @ w_branch[i])
        return z @ w_out

    y = merge([o[0] for o in outs], p[11:14])
    yc = merge([o[1] for o in outs], pc[11:14]) if need_ctx else None
    return y, yc


def setup_inputs(seed: int = 0) -> dict:
    key = jax.random.key(seed)
    ks = jax.random.split(key, 20)
    f32 = jnp.float32

    def dense(k, shape, fan_in):
        return jax.random.normal(k, shape, f32) * fan_in ** -0.5

    def gain(k, shape):
        return 1.0 + 0.1 * jax.random.normal(k, shape, f32)

    return {
        'x': jax.random.normal(ks[0], (BATCH, SEQ, D_MODEL), f32),
        'c': jax.random.normal(ks[1], (BATCH, D_MODEL), f32),
        'ctx': jax.random.normal(ks[2], (BATCH, CTX_LEN, D_MODEL), f32),
        'c_ctx': jax.random.normal(ks[3], (D_MODEL,), f32),
        'w_mod': dense(ks[4], (DEPTH, D_MODEL, N_MOD * D_MODEL), D_MODEL),
        'b_mod': 0.01 * jax.random.normal(ks[5], (DEPTH, N_MOD * D_MODEL), f32),
        'norm_g': gain(ks[6], (DEPTH, 3, D_MODEL)),
        'ffn_w_in': dense(ks[7], (DEPTH, 2, D_MODEL, 2 * D_FF), D_MODEL),
        'ffn_w_out': dense(ks[8], (DEPTH, 2, D_FF, D_MODEL), D_FF),
        'w_in': dense(ks[9], (DEPTH, D_MODEL, IN_WIDTH), D_MODEL),
        'gla_fg_w2': dense(ks[10], (DEPTH, 2, GLA_LR, GLA_HEADS * GLA_DK), GLA_LR),
        'gla_fg_b': 0.1 * jax.random.normal(ks[11], (DEPTH, 2, GLA_HEADS * GLA_DK), f32),
        'gla_norm_g': gain(ks[12], (DEPTH, GLA_DV)),
        'nat_q_norm': gain(ks[13], (DEPTH, NAT_HD)),
        'nat_k_norm': gain(ks[14], (DEPTH, NAT_HD)),
        'nat_rpb': 0.2 * jax.random.normal(ks[15], (DEPTH, NAT_HEADS, 2 * WIN_R - 1, 2 * WIN_C - 1), f32),
        'gqa_q_norm': gain(ks[16], (DEPTH, GQA_HD)),
        'gqa_k_norm': gain(ks[17], (DEPTH, GQA_HD)),
        'w_branch': dense(ks[18], (DEPTH, N_BRANCH, BRANCH_W, D_MODEL), BRANCH_W),
        'w_out': dense(ks[19], (DEPTH, D_MODEL, D_MODEL), D_MODEL),
    }


def reference(x, c, ctx, c_ctx, w_mod, b_mod, norm_g, ffn_w_in, ffn_w_out, w_in, gla_fg_w2, gla_fg_b,
              gla_norm_g, nat_q_norm, nat_k_norm, nat_rpb, gqa_q_norm, gqa_k_norm, w_branch, w_out):
    n_tok = x.shape[1]
    t = jnp.arange(n_tok, dtype=jnp.int32)
    rows = t // GRID_W
    cols = t % GRID_W
    h = ctx
    s_c = jax.nn.silu(c)
    s_cc = jax.nn.silu(c_ctx)
    for l in range(DEPTH):
        need_ctx = l < DEPTH - 1
        sh1, sc1, g1, sh2, sc2, g2, sh3, sc3, g3 = jnp.split((s_c @ w_mod[l] + b_mod[l])[:, None, :], N_MOD, axis=-1)
        ch1, cs1, cg1, ch2, cs2, cg2, ch3, cs3, cg3 = jnp.split(s_cc @ w_mod[l] + b_mod[l], N_MOD, axis=-1)
        x = x + 0.5 * g1 * swiglu(modulate(rms_norm(x, norm_g[l, 0]), sh1, sc1), ffn_w_in[l, 0], ffn_w_out[l, 0])
        h = h + 0.5 * cg1 * swiglu(modulate(rms_norm(h, norm_g[l, 0]), ch1, cs1), ffn_w_in[l, 0], ffn_w_out[l, 0])
        y, yc = token_mixer(
            modulate(rms_norm(x, norm_g[l, 1]), sh2, sc2), modulate(rms_norm(h, norm_g[l, 1]), ch2, cs2),
            rows, cols, w_in[l], gla_fg_w2[l], gla_fg_b[l], gla_norm_g[l], nat_q_norm[l], nat_k_norm[l],
            nat_rpb[l], gqa_q_norm[l], gqa_k_norm[l], w_branch[l], w_out[l], need_ctx)
        x = x + g2 * y
        x = x + 0.5 * g3 * swiglu(modulate(rms_norm(x, norm_g[l, 2]), sh3, sc3), ffn_w_in[l, 1], ffn_w_out[l, 1])
        if need_ctx:
            h = h + cg2 * yc
            h = h + 0.5 * cg3 * swiglu(modulate(rms_norm(h, norm_g[l, 2]), ch3, cs3), ffn_w_in[l, 1], ffn_w_out[l, 1])
    return x
```

```python
import numpy as np
from contextlib import ExitStack
import concourse.bass as bass
import concourse.mybir as mybir
from concourse.bass_utils import run_bass_kernel_spmd

F32 = mybir.dt.float32
BF16 = mybir.dt.bfloat16
AF = mybir.ActivationFunctionType
ALU = mybir.AluOpType

NCTX = 256
NLAT = 4096
T = NCTX + NLAT
D = 1024
DFF = 2816
NCH = T // 128
EPS = 1e-6
C_GLA = 128

O_GQ, O_GK, O_GV, O_GG, O_FG = 0, 512, 1024, 2048, 3072
O_NQ, O_NK, O_NV = 3104, 4128, 5152
O_QQ, O_QK, O_QV = 6176, 7200, 7456
O_G0 = 7712
W_IN = 10784

NSLOT = 16
SLOT_DR0 = [d for d in range(-7, 7)] + [-5, 3]


class Buf:
    __slots__ = ("w", "r")

    def __init__(self):
        self.w = None
        self.r = {}


class Sched:
    def __init__(self, nc, es, n_dma_sems=14):
        self.nc = nc
        self.eng = {"pe": nc.tensor, "act": nc.scalar, "dve": nc.vector, "pool": nc.gpsimd, "sp": nc.sync}
        self.sem = {k: es.enter_context(nc.semaphore("sem_" + k)) for k in self.eng}
        self.cnt = {k: 0 for k in self.eng}
        self.seen = {k: {} for k in self.eng}
        self.dsem, self.dval, self.dnext = {}, {}, {}
        for q in ("sp", "pool"):
            self.dsem[q] = [es.enter_context(nc.semaphore(f"dsem_{q}{i}")) for i in range(n_dma_sems)]
            self.dval[q] = [0] * n_dma_sems
            self.dnext[q] = 0
        self.n_ins = 0

    def _semobj(self, key):
        if isinstance(key, str):
            return self.sem[key]
        return self.dsem[key[0]][key[1]]

    def _wait(self, e, tok, raw=False):
        if tok is None:
            return
        key, val = tok
        if key == e and e == "pe":
            return
        if self.seen[e].get(key, 0) >= val:
            return
        self.seen[e][key] = val
        self.eng[e].wait_ge(self._semobj(key), val)

    def _deps(self, e, reads, writes):
        for b in reads:
            self._wait(e, b.w, raw=True)
        for b in writes:
            self._wait(e, b.w)
            for tok in b.r.values():
                self._wait(e, tok)

    def _mark(self, tok, reads, writes):
        for b in reads:
            b.r[tok[0]] = tok
        for b in writes:
            b.w = tok
            b.r = {}

    def op(self, e, fn, reads=(), writes=()):
        self._deps(e, reads, writes)
        ins = fn(self.eng[e])
        self.cnt[e] += 1
        ins.then_inc(self.sem[e], 1)
        self._mark((e, self.cnt[e]), reads, writes)
        self.n_ins += 1

    def dma(self, q, out, in_, reads=(), writes=()):
        i = self.dnext[q]
        self.dnext[q] = (i + 1) % len(self.dsem[q])
        key = (q, i)
        if self.dval[q][i] > 0:
            self._wait(q, (key, self.dval[q][i]))
        self._deps(q, reads, writes)
        self.dval[q][i] += 16
        self.eng[q].dma_start(out=out, in_=in_).then_inc(self.dsem[q][i], 16)
        self._mark((key, self.dval[q][i]), reads, writes)
        self.n_ins += 1

    def barrier(self, engines=None):
        for e in (engines or self.eng):
            for o in self.eng:
                if o != e and self.cnt[o] > 0:
                    self._wait(e, (o, self.cnt[o]))
            for q in self.dsem:
                for i, v in enumerate(self.dval[q]):
                    if v > 0:
                        self._wait(e, ((q, i), v))


class Tl:
    __slots__ = ("t", "b")

    def __init__(self, t):
        self.t = t
        self.b = Buf()


class Ring:
    def __init__(self, items):
        self.items = items
        self.i = 0

    def next(self):
        x = self.items[self.i]
        self.i = (self.i + 1) % len(self.items)
        return x


def build_program(phases=None, dump=()):
    nc = bass.Bass("TRN2", target_bir_lowering=False)

    def din(name, shape, dt=F32):
        return nc.dram_tensor(name, list(shape), dt, kind="ExternalInput").ap()

    def dscr(name, shape, dt):
        kind = "ExternalOutput" if name in dump else "Internal"
        return nc.dram_tensor(name, list(shape), dt, kind=kind).ap()

    x_d = din("x", [NLAT, D])
    ctx_d = din("ctx", [NCTX, D])
    cc_d = din("cc", [16, 128])
    w_mod = din("w_mod", [2, D, 9 * D])
    b_mod = din("b_mod", [2, 72, 128])
    norm_g = din("norm_g", [2, 24, 128])
    ffn_wi = din("ffn_w_in", [2, 2, D, 2 * DFF])
    ffn_wo = din("ffn_w_out", [2, 2, DFF, D])
    w_in = din("w_in", [2, D, W_IN])
    fg_w2 = din("gla_fg_w2", [2, 2, 16, 512])
    fg_b = din("gla_fg_b", [2, 2, 1, 512])
    gla_g = din("gla_norm_g", [2, 2, 128])
    nat_qg = din("nat_q_norm", [2, 1, 64])
    nat_kg = din("nat_k_norm", [2, 1, 64])
    gqa_qg = din("gqa_q_norm", [2, 1, 128])
    gqa_kg = din("gqa_k_norm", [2, 1, 128])
    w_br = din("w_branch", [2, 3, D, D])
    w_o = din("w_out", [2, D, D])
    natb = din("natb", [2, 2, 128, NSLOT * 512])
    natm = din("natm", [128, NSLOT * 64])
    ropec = din("ropec", [128, T])
    ropes = din("ropes", [128, T])
    ident_d = din("ident", [128, 128])
    perm_d = din("perm", [128, 128])
    tri_d = din("tri", [4, 128, 128])
    msk_d = din("gmask", [2, 128, 128])

    out_d = nc.dram_tensor("out", [NLAT, D], F32, kind="ExternalOutput").ap()

    XT = dscr("XT", [D, T], F32)
    GQT = dscr("GQT", [512, T], BF16)
    GKT = dscr("GKT", [512, T], BF16)
    GGT = dscr("GGT", [D, T], BF16)
    SPF = dscr("SPF", [T, 512], F32)
    SPB = dscr("SPB", [T, 512], F32)
    GKt = dscr("GKt", [T, 512], BF16)
    GVt = dscr("GVt", [T, 1024], BF16)
    NQT = dscr("NQT", [D, T], BF16)
    NKT = dscr("NKT", [D, T], BF16)
    NVt = dscr("NVt", [T, D], BF16)
    QQT = dscr("QQT", [D, T], BF16)
    QKT = dscr("QKT", [256, T], BF16)
    QVt = dscr("QVt", [T, 256], BF16)
    SGT = dscr("SGT", [3, D, T], BF16)
    YT = dscr("YT", [3, D, T], BF16)
    UTD = dscr("UTD", [D, T], BF16)

    es = ExitStack()
    with es:
        S = Sched(nc, es)

        uid = [0]

        def uname(name):
            uid[0] += 1
            return f"s{uid[0]}_{name}"

        def gsb(name, shape, dt):
            return Tl(es.enter_context(nc.sbuf_tensor(uname(name), list(shape), dt)))

        PS = [Tl(es.enter_context(nc.psum_tensor(f"ps{i}", [128, 512], F32))) for i in range(8)]

        ident = gsb("ident", [128, 128], F32)
        permb = gsb("permb", [128, 128], BF16)
        onesD = gsb("onesD", [128, 128], BF16)
        ones128 = gsb("ones128", [128, 128], BF16)
        ones256 = gsb("ones256", [128, 128], BF16)
        blk64 = gsb("blk64", [128, 128], BF16)
        ones1 = gsb("ones1", [128, 128], BF16)
        onesf = gsb("onesf", [1, 128], F32)
        DER = [[gsb(f"der{l}{k}", [128, 9, 8], F32) for k in range(2)] for l in range(2)]
        GQG = [gsb(f"gqg{l}", [128, 2], F32) for l in range(2)]
        NG = [gsb(f"ng{l}", [128, 2], F32) for l in range(2)]
        GLG = [gsb(f"glg{l}", [128, 2], F32) for l in range(2)]

        S.dma("sp", ident.t[:], ident_d, writes=[ident.b])
        S.dma("pool", permb.t[:], perm_d, writes=[permb.b])
        S.op("dve", lambda e: e.memset(onesD.t[:], 1.0 / 1024), writes=[onesD.b])
        S.op("dve", lambda e: e.memset(ones128.t[:], 1.0 / 128), writes=[ones128.b])
        S.op("dve", lambda e: e.memset(ones256.t[:], 1.0 / 256), writes=[ones256.b])
        S.op("dve", lambda e: e.memset(ones1.t[:], 1.0), writes=[ones1.b])
        S.op("dve", lambda e: e.memset(onesf.t[:], 1.0), writes=[onesf.b])
        S.op("dve", lambda e: e.memset(blk64.t[:], 0.0), writes=[blk64.b])
        S.op("dve", lambda e: e.memset(blk64.t[0:64, 0:64], 1.0 / 64), writes=[blk64.b])
        S.op("dve", lambda e: e.memset(blk64.t[64:128, 64:128], 1.0 / 64), writes=[blk64.b])

        ps_ring = Ring(PS)

        def mm(out, lhsT, rhs, start, stop, reads, writes):
            S.op("pe", lambda e: e.matmul(out, lhsT, rhs, start=start, stop=stop), reads=reads, writes=writes)

        def phase_end(name):
            S.barrier()

        def tiles(n, with_ctx=True):
            res = []
            if with_ctx:
                res.append((0, NCTX, 1))
            for t0 in range(NCTX, T, n):
                res.append((t0, n, 0))
            return res

        def phase_prep():
            with ExitStack() as ph:
                def sb(name, shape, dt):
                    return Tl(ph.enter_context(nc.sbuf_tensor(uname(name), list(shape), dt)))

                rows = sb("rows", [128, 128], F32)
                cT = sb("cT", [128, 16], F32)
                sT = sb("sT", [128, 16], F32)
                bT = [sb(f"bT{l}", [128, 72], F32) for l in range(2)]
                gT = [sb(f"gT{l}", [128, 24], F32) for l in range(2)]
                tmpv = sb("tmpv", [128, 4], F32)

                def loadT(dst_ap, dst_buf, src_rows, R, extra_src=None):
                    S.dma("sp", rows.t[0:R, :], src_rows, writes=[rows.b])
                    p = ps_ring.next()
                    mm(p.t[:, 0:R], rows.t[0:R, :], ident.t[0:R, 0:R], True, True, [rows.b, ident.b], [p.b])
                    S.op("dve", lambda e: e.tensor_copy(dst_ap, p.t[:, 0:R]), reads=[p.b], writes=[dst_buf])

                loadT(cT.t[:], cT.b, cc_d, 16)
                S.op("act", lambda e: e.activation(out=sT.t[:], in_=cT.t[:], func=AF.Silu), reads=[cT.b], writes=[sT.b])
                for l in range(2):
                    loadT(bT[l].t[:], bT[l].b, b_mod[l], 72)
                    loadT(gT[l].t[:], gT[l].b, norm_g[l], 24)
                    loadT(GLG[l].t[:], GLG[l].b, gla_g[l], 2)
                    loadT(tmpv.t[:, 0:1], tmpv.b, gqa_qg[l], 1)
                    S.op("dve", lambda e: e.tensor_scalar(GQG[l].t[:, 0:1], tmpv.t[:, 0:1], 128.0 ** -0.5, 0.0, ALU.mult, ALU.add),
                         reads=[tmpv.b], writes=[GQG[l].b])
                    loadT(GQG[l].t[:, 1:2], GQG[l].b, gqa_kg[l], 1)
                    S.dma("sp", rows.t[0:1, 0:64], nat_qg[l], writes=[rows.b])
                    S.dma("sp", rows.t[0:1, 64:128], nat_qg[l], writes=[rows.b])
                    p = ps_ring.next()
                    mm(p.t[:, 0:1], rows.t[0:1, :], ident.t[0:1, 0:1], True, True, [rows.b, ident.b], [p.b])
                    S.op("dve", lambda e: e.tensor_scalar(NG[l].t[:, 0:1], p.t[:, 0:1], 0.125, 0.0, ALU.mult, ALU.add),
                         reads=[p.b], writes=[NG[l].b])
                    S.dma("sp", rows.t[0:1, 0:64], nat_kg[l], writes=[rows.b])
                    S.dma("sp", rows.t[0:1, 64:128], nat_kg[l], writes=[rows.b])
                    p = ps_ring.next()
                    mm(p.t[:, 0:1], rows.t[0:1, :], ident.t[0:1, 0:1], True, True, [rows.b, ident.b], [p.b])
                    S.op("dve", lambda e: e.tensor_copy(NG[l].t[:, 1:2], p.t[:, 0:1]), reads=[p.b], writes=[NG[l].b])

                wm_ring = Ring([sb(f"wm{i}", [128, 8, 1152], F32) for i in range(2)])
                for l in range(2):
                    pm = ps_ring.next()
                    src = w_mod[l].rearrange("(k p) n -> p k n", p=128)
                    for g in range(8):
                        wm = wm_ring.next()
                        for k in range(8):
                            S.dma("sp", wm.t[:, k, :], src[:, k, g * 1152:(g + 1) * 1152], writes=[wm.b])
                        for cc in range(9):
                            nn = g * 9 + cc
                            for k in range(8):
                                mm(pm.t[:, nn * 2:nn * 2 + 2], wm.t[:, k, cc * 128:(cc + 1) * 128],
                                   sT.t[:].rearrange("p (a k) -> p k a", a=2)[:, k, :], k == 0, k == 7,
                                   [wm.b, sT.b], [pm.b])
                    for kind in range(2):
                        der = DER[l][kind]
                        S.op("dve", lambda e: e.tensor_tensor(
                            der.t[:].rearrange("p j c -> p (j c)"),
                            pm.t[:, 0:144].rearrange("p (n a) -> p n a", a=2)[:, :, kind],
                            bT[l].t[:], ALU.add), reads=[pm.b, bT[l].b], writes=[der.b])
                        for s in range(3):
                            S.op("dve", lambda e: e.scalar_tensor_tensor(
                                der.t[:, 3 * s + 1, :], der.t[:, 3 * s + 1, :], 1.0, gT[l].t[:, s * 8:(s + 1) * 8],
                                ALU.add, ALU.mult), reads=[der.b, gT[l].b], writes=[der.b])
                            if s != 1:
                                S.op("dve", lambda e: e.tensor_scalar(der.t[:, 3 * s + 2, :], der.t[:, 3 * s + 2, :], 0.5, 0.0,
                                                                     ALU.mult, ALU.add), reads=[der.b], writes=[der.b])

                xin_ring = Ring([sb(f"xin{i}", [128, D], F32) for i in range(3)])
                xst_ring = Ring([sb(f"xst{i}", [128, 8, 512], F32) for i in range(2)])
                groups = [(0, 2)] + [(2 + 4 * g, 4) for g in range(8)]
                ev = 0
                for (c0, ncn) in groups:
                    st = xst_ring.next()
                    for j in range(ncn):
                        ch = c0 + j
                        xi = xin_ring.next()
                        srcx = ctx_d[ch * 128:(ch + 1) * 128, :] if ch < 2 else x_d[(ch - 2) * 128:(ch - 1) * 128, :]
                        S.dma("sp", xi.t[:], srcx, writes=[xi.b])
                        for half in range(2):
                            p = ps_ring.next()
                            for q in range(4):
                                c = half * 4 + q
                                S.op("pe", lambda e: e.transpose(p.t[:, q * 128:(q + 1) * 128], xi.t[:, c * 128:(c + 1) * 128],
                                                                 ident.t[:]), reads=[xi.b, ident.b], writes=[p.b])
                            eng = "act" if ev % 2 == 0 else "dve"
                            ev += 1
                            dst = st.t[:, half * 4:(half + 1) * 4, j * 128:(j + 1) * 128]
                            srcp = p.t[:].rearrange("p (q t) -> p q t", q=4)
                            if eng == "act":
                                S.op("act", lambda e: e.copy(dst, srcp), reads=[p.b], writes=[st.b])
                            else:
                                S.op("dve", lambda e: e.tensor_copy(dst, srcp), reads=[p.b], writes=[st.b])
                    n = ncn * 128
                    S.dma("sp", XT.rearrange("(c p) t -> p c t", p=128)[:, :, c0 * 128:c0 * 128 + n], st.t[:, :, 0:n],
                          reads=[st.b])
            phase_end("prep")

        def norm_mod(l, kind, s, xt, n, sq, rstd_ring, tmp_ring, u_ap_fn, u_buf, aff="act", part="all"):
            der = DER[l][kind]
            if part in ("all", "sq"):
                S.op("act", lambda e: e.activation(out=sq.t[:, :, 0:n], in_=xt.t[:, :, 0:n], func=AF.Square),
                     reads=[xt.b], writes=[sq.b])
            if part == "sq":
                return
            p = ps_ring.next()
            for c in range(8):
                mm(p.t[:, 0:n], onesD.t[:], sq.t[:, c, 0:n], c == 0, c == 7, [onesD.b, sq.b], [p.b])
            rstd = rstd_ring.next()
            S.op("act", lambda e: e.activation(out=rstd.t[:, 0:n], in_=p.t[:, 0:n], func=AF.Ln, bias=EPS),
                 reads=[p.b], writes=[rstd.b])
            S.op("act", lambda e: e.activation(out=rstd.t[:, 0:n], in_=rstd.t[:, 0:n], func=AF.Exp, scale=-0.5),
                 reads=[rstd.b], writes=[rstd.b])
            for c in range(8):
                tmp = tmp_ring.next()
                S.op("dve", lambda e: e.scalar_tensor_tensor(tmp.t[:, 0:n], xt.t[:, c, 0:n], der.t[:, 3 * s + 1, c:c + 1],
                                                             rstd.t[:, 0:n], ALU.mult, ALU.mult),
                     reads=[xt.b, der.b, rstd.b], writes=[tmp.b])
                if aff == "act":
                    S.op("act", lambda e: e.activation(out=u_ap_fn(c), in_=tmp.t[:, 0:n], func=AF.Identity,
                                                       bias=der.t[:, 3 * s, c:c + 1]),
                         reads=[tmp.b, der.b], writes=[u_buf])
                else:
                    S.op("pool", lambda e: e.tensor_scalar(u_ap_fn(c), tmp.t[:, 0:n], 1.0, der.t[:, 3 * s, c:c + 1], ALU.mult, ALU.add),
                         reads=[tmp.b, der.b], writes=[u_buf])

        XTv = XT.rearrange("(c p) t -> p c t", p=128)

        def phase_ffn(l, i, with_ctx, name):
            NT = 256
            s = 0 if i == 0 else 2
            with ExitStack() as ph:
                def sb(nm, shape, dt):
                    return Tl(ph.enter_context(nc.sbuf_tensor(uname(nm), list(shape), dt)))
                WU = sb("WU", [128, 8, 2 * DFF], BF16)
                WD = sb("WD", [128, 22, D], BF16)
                BWU = [Buf() for _ in range(11)]
                BWD = [Buf() for _ in range(11)]
                srcu = ffn_wi[l, i].rearrange("(k p) n -> p k n", p=128)
                srcd = ffn_wo[l, i].rearrange("(k p) n -> p k n", p=128)
                order = [0, 5, 1, 6, 2, 7, 3, 8, 4, 9, 10]
                for r in order:
                    S.dma("pool", WU.t[:, :, r * 512:(r + 1) * 512], srcu[:, :, r * 512:(r + 1) * 512], writes=[BWU[r]])
                for r in range(11):
                    S.dma("pool", WD.t[:, 2 * r:2 * r + 2, :], srcd[:, 2 * r:2 * r + 2, :], writes=[BWD[r]])
                xt_ring = Ring([sb(f"xt{j}", [128, 8, NT], F32) for j in range(2)])
                sq = sb("sq", [128, 8, NT], BF16)
                rstd_ring = Ring([sb(f"rstd{j}", [128, NT], F32) for j in range(2)])
                tmp_ring = Ring([sb(f"tmp{j}", [128, NT], F32) for j in range(2)])
                u_ring = Ring([sb(f"u{j}", [128, 8, NT], BF16) for j in range(2)])
                sl_ring = Ring([sb(f"sl{j}", [128, NT], F32) for j in range(3)])
                g_ring = Ring([sb(f"g{j}", [128, 22, NT], BF16) for j in range(2)])
                u2 = sb("u2", [128, 8, NT], BF16) if i == 0 else None
                tmp_ring2 = Ring([sb(f"tmpb{j}", [128, NT], F32) for j in range(2)])
                UTDv = UTD.rearrange("(c p) t -> p c t", p=128)

                def do_norm2(ti, part):
                    t0, n, kind = tl[ti]
                    norm_mod(l, kind, 1, xts[ti], n, sq, rstd_ring, tmp_ring2, lambda c: u2.t[:, c, 0:n], u2.b, aff="pool", part=part)
                    if part != "sq":
                        S.dma("sp", UTDv[:, :, t0:t0 + n], u2.t[:, :, 0:n], reads=[u2.b])
                tl = tiles(NT, with_ctx)

                def load(ti):
                    t0, n, kind = tl[ti]
                    xt = xt_ring.next()
                    S.dma("sp", xt.t[:, :, 0:n], XTv[:, :, t0:t0 + n], writes=[xt.b])
                    return xt
                xts = [None] * len(tl)
                us = [None] * len(tl)
                xts[0] = load(0)

                def do_norm(ti, part):
                    t0, n, kind = tl[ti]
                    if part != "rest":
                        us[ti] = u_ring.next()
                    u = us[ti]
                    norm_mod(l, kind, s, xts[ti], n, sq, rstd_ring, tmp_ring, lambda c: u.t[:, c, 0:n], u.b, part=part)
                do_norm(0, "all")
                for ti, (t0, n, kind) in enumerate(tl):
                    xt = xts[ti]
                    u = us[ti]
                    g = g_ring.next()
                    for j in range(22):
                        if j == 2 and i == 0 and ti > 0:
                            do_norm2(ti - 1, "sq")
                        if j == 6:
                            if i == 0 and ti > 0:
                                do_norm2(ti - 1, "rest")
                            if ti + 1 < len(tl):
                                xts[ti + 1] = load(ti + 1)
                        if j == 15 and ti + 1 < len(tl):
                            do_norm(ti + 1, "sq")
                        p = ps_ring.next()
                        ra = (j * 128) // 512
                        rb = (DFF + j * 128) // 512
                        for k in range(8):
                            mm(p.t[:, 0:n], WU.t[:, k, j * 128:(j + 1) * 128], u.t[:, k, 0:n], k == 0, k == 7,
                               [BWU[ra], u.b], [p.b])
                        for k in range(8):
                            mm(p.t[:, 256:256 + n], WU.t[:, k, DFF + j * 128:DFF + (j + 1) * 128], u.t[:, k, 0:n], k == 0, k == 7,
                               [BWU[rb], u.b], [p.b])
                        sl = sl_ring.next()
                        S.op("act", lambda e: e.activation(out=sl.t[:, 0:n], in_=p.t[:, 0:n], func=AF.Silu),
                             reads=[p.b], writes=[sl.b])
                        S.op("dve", lambda e: e.tensor_tensor(g.t[:, j, 0:n], sl.t[:, 0:n], p.t[:, 256:256 + n], ALU.mult),
                             reads=[sl.b, p.b], writes=[g.b])
                    if ti + 1 < len(tl):
                        do_norm(ti + 1, "rest")
                    der = DER[l][kind]
                    for nn in range(8):
                        p = ps_ring.next()
                        for j in range(22):
                            mm(p.t[:, 0:n], WD.t[:, j, nn * 128:(nn + 1) * 128], g.t[:, j, 0:n], j == 0, j == 21,
                               [BWD[j // 2], g.b], [p.b])
                        S.op("dve", lambda e: e.scalar_tensor_tensor(xt.t[:, nn, 0:n], p.t[:, 0:n], der.t[:, 3 * s + 2, nn:nn + 1],
                                                                     xt.t[:, nn, 0:n], ALU.mult, ALU.add),
                             reads=[p.b, der.b, xt.b], writes=[xt.b])
                    S.dma("sp", XTv[:, :, t0:t0 + n], xt.t[:, :, 0:n], reads=[xt.b])
                if i == 0:
                    do_norm2(len(tl) - 1, "all")
            phase_end(name)

        def phase_inproj(l):
            tl = tiles(512, True)
            with ExitStack() as ph:
                def sb(nm, shape, dt):
                    return Tl(ph.enter_context(nc.sbuf_tensor(uname(nm), list(shape), dt)))
                UT = sb("UT", [128, 8, T], BF16)
                BUT = [Buf() for _ in tl]
                UTDv = UTD.rearrange("(c p) t -> p c t", p=128)
                for ti, (t0, n, kind) in enumerate(tl):
                    S.dma("sp", UT.t[:, :, t0:t0 + n], UTDv[:, :, t0:t0 + n], writes=[BUT[ti]])
                w_ring = Ring([sb(f"iw{j}", [128, 8, 1024], BF16) for j in range(2)])
                FG = [sb(f"fg{d}", [16, T], F32) for d in range(2)]
                W2 = [sb(f"w2{d}", [16, 512], F32) for d in range(2)]
                B2 = [sb(f"b2{d}", [1, 512], F32) for d in range(2)]
                st_ring = Ring([sb(f"ist{j}", [128, 512], BF16) for j in range(4)])
                sqb_ring = Ring([sb(f"isqb{j}", [128, 512], BF16) for j in range(2)])
                rs_ring = Ring([sb(f"irs{j}", [128, 512], F32) for j in range(2)])
                qn_ring = Ring([sb(f"iqn{j}", [128, 512], BF16) for j in range(2)])
                t1_ring = Ring([sb(f"it1{j}", [128, 512], F32) for j in range(2)])
                t2_ring = Ring([sb(f"it2{j}", [128, 512], F32) for j in range(2)])
                rc_ring = Ring([sb(f"irc{j}", [128, 512], F32) for j in range(2)])
                rsn_ring = Ring([sb(f"irsn{j}", [128, 512], F32) for j in range(2)])
                e_ring = Ring([sb(f"ie{j}", [128, 512], F32) for j in range(2)])
                sp_ring = Ring([sb(f"isp{j}", [128, 512], F32) for j in range(2)])
                wsrc = w_in[l].rearrange("(k p) n -> p k n", p=128)
                for d in range(2):
                    S.dma("sp", W2[d].t[:], fg_w2[l, d], writes=[W2[d].b])
                    S.dma("sp", B2[d].t[:], fg_b[l, d], writes=[B2[d].b])
                ev = [0]

                def evac(dst, src, reads, writes):
                    if ev[0] % 2 == 0:
                        S.op("act", lambda e: e.copy(dst, src), reads=reads, writes=writes)
                    else:
                        S.op("dve", lambda e: e.tensor_copy(dst, src), reads=reads, writes=writes)
                    ev[0] += 1

                def loadw(col0, ncols):
                    W = w_ring.next()
                    S.dma("pool", W.t[:, :, 0:ncols], wsrc[:, :, col0:col0 + ncols], writes=[W.b])
                    return W

                def fm_group(col0, nchunks, epi, m=128):
                    W = loadw(col0, nchunks * m)
                    for ti, (t0, n, kind) in enumerate(tl):
                        for ch in range(nchunks):
                            p = ps_ring.next()
                            for k in range(8):
                                mm(p.t[0:m, 0:n], W.t[:, k, ch * m:(ch + 1) * m], UT.t[:, k, t0:t0 + n], k == 0, k == 7,
                                   [W.b, BUT[ti]], [p.b])
                            epi(ch, p, t0, n, ti)

                def store_fm(dst, ch, st, t0, n):
                    S.dma("sp", dst[ch * 128:(ch + 1) * 128, t0:t0 + n], st.t[:, 0:n], reads=[st.b])

                def epi_copy(dst):
                    def f(ch, p, t0, n, ti):
                        st = st_ring.next()
                        evac(st.t[:, 0:n], p.t[:, 0:n], [p.b], [st.b])
                        store_fm(dst, ch, st, t0, n)
                    return f

                def epi_act(dst, func):
                    def f(ch, p, t0, n, ti):
                        st = st_ring.next()
                        S.op("act", lambda e: e.activation(out=st.t[:, 0:n], in_=p.t[:, 0:n], func=func), reads=[p.b], writes=[st.b])
                        store_fm(dst, ch, st, t0, n)
                    return f

                pA_ring = Ring([PS[0], PS[1], PS[2], PS[3]])
                pB_ring = Ring([PS[4], PS[5]])
                pC_ring = Ring([PS[6], PS[7]])
                qn_ring3 = Ring(qn_ring.items + [sb("iqn2", [128, 512], BF16)])

                def fm_group_pipe(col0, nchunks, stages):
                    W = loadw(col0, nchunks * 128)
                    items = []
                    for ti, (t0, n, kind) in enumerate(tl):
                        for ch in range(nchunks):
                            items.append({"ch": ch, "t0": t0, "n": n, "ti": ti})

                    def st0(it):
                        p = pA_ring.next()
                        n, t0 = it["n"], it["t0"]
                        for k in range(8):
                            mm(p.t[:, 0:n], W.t[:, k, it["ch"] * 128:(it["ch"] + 1) * 128], UT.t[:, k, t0:t0 + n], k == 0, k == 7,
                               [W.b, BUT[it["ti"]]], [p.b])
                        it["p"] = p
                    allst = [st0] + stages
                    ns = len(allst)
                    for step in range(len(items) + ns - 1):
                        for si in range(ns - 1, -1, -1):
                            i = step - si
                            if 0 <= i < len(items):
                                allst[si](items[i])

                def st_stats(ones_t):
                    def f(it):
                        p, n = it["p"], it["n"]
                        sqb = sqb_ring.next()
                        S.op("act", lambda e: e.activation(out=sqb.t[:, 0:n], in_=p.t[:, 0:n], func=AF.Square), reads=[p.b], writes=[sqb.b])
                        p2 = pB_ring.next()
                        mm(p2.t[:, 0:n], ones_t.t[:], sqb.t[:, 0:n], True, True, [ones_t.b, sqb.b], [p2.b])
                        it["p2"] = p2
                    return f

                def st_norm(gain_t, col, out_ring, dst=None, perm=False):
                    def f(it):
                        p, p2, n = it["p"], it["p2"], it["n"]
                        rs = rs_ring.next()
                        S.op("act", lambda e: e.activation(out=rs.t[:, 0:n], in_=p2.t[:, 0:n], func=AF.Ln, bias=EPS), reads=[p2.b], writes=[rs.b])
                        S.op("act", lambda e: e.activation(out=rs.t[:, 0:n], in_=rs.t[:, 0:n], func=AF.Exp, scale=-0.5), reads=[rs.b], writes=[rs.b])
                        o = out_ring.next()
                        S.op("dve", lambda e: e.scalar_tensor_tensor(o.t[:, 0:n], p.t[:, 0:n], gain_t.t[:, col:col + 1], rs.t[:, 0:n], ALU.mult, ALU.mult),
                             reads=[p.b, rs.b, gain_t.b], writes=[o.b])
                        it["o"] = o
                        if perm:
                            p3 = pC_ring.next()
                            mm(p3.t[:, 0:n], permb.t[:], o.t[:, 0:n], True, True, [permb.b, o.b], [p3.b])
                            it["p3"] = p3
                        else:
                            store_fm(dst, it["ch"], o, it["t0"], n)
                    return f

                rope_tiles = {}

                def get_rope(ti, t0, n):
                    if ti not in rope_tiles:
                        rc = rc_ring.next()
                        rsn = rsn_ring.next()
                        S.dma("sp", rc.t[:, 0:n], ropec[:, t0:t0 + n], writes=[rc.b])
                        S.dma("sp", rsn.t[:, 0:n], ropes[:, t0:t0 + n], writes=[rsn.b])
                        rope_tiles.clear()
                        rope_tiles[ti] = (rc, rsn)
                    return rope_tiles[ti]

                def st_rope(dst):
                    def f(it):
                        qn, p3, n, t0 = it["o"], it["p3"], it["n"], it["t0"]
                        rc, rsn = get_rope(it["ti"], t0, n)
                        t1 = t1_ring.next()
                        t2 = t2_ring.next()
                        S.op("pool", lambda e: e.tensor_tensor(t1.t[:, 0:n], qn.t[:, 0:n], rc.t[:, 0:n], ALU.mult),
                             reads=[qn.b, rc.b], writes=[t1.b])
                        S.op("dve", lambda e: e.tensor_tensor(t2.t[:, 0:n], p3.t[:, 0:n], rsn.t[:, 0:n], ALU.mult),
                             reads=[p3.b, rsn.b], writes=[t2.b])
                        st = st_ring.next()
                        S.op("dve", lambda e: e.tensor_tensor(st.t[:, 0:n], t1.t[:, 0:n], t2.t[:, 0:n], ALU.add),
                             reads=[t1.b, t2.b], writes=[st.b])
                        store_fm(dst, it["ch"], st, t0, n)
                    return f

                def tm_group(col0, ncols, dst):
                    W = loadw(col0, ncols)
                    w = min(512, ncols)
                    for cidx in range(NCH):
                        ti = 0 if cidx < 2 else 1 + (cidx - 2) // 4
                        for piece in range(ncols // w):
                            p = ps_ring.next()
                            for k in range(8):
                                mm(p.t[:, 0:w], UT.t[:, k, cidx * 128:(cidx + 1) * 128], W.t[:, k, piece * w:(piece + 1) * w],
                                   k == 0, k == 7, [W.b, BUT[ti]], [p.b])
                            st = st_ring.next()
                            evac(st.t[:, 0:w], p.t[:, 0:w], [p.b], [st.b])
                            S.dma("sp", dst[cidx * 128:(cidx + 1) * 128, piece * w:(piece + 1) * w], st.t[:, 0:w], reads=[st.b])

                fm_group(O_GQ, 4, epi_copy(GQT))
                fm_group(O_GK, 4, epi_copy(GKT))
                tm_group(O_GK, 512, GKt)
                tm_group(O_GV, 1024, GVt)
                fm_group(O_GG, 8, epi_act(GGT, AF.Silu))

                def epi_fg(ch, p, t0, n, ti):
                    evac(FG[ch].t[0:16, t0:t0 + n], p.t[0:16, 0:n], [p.b], [FG[ch].b])
                fm_group(O_FG, 2, epi_fg, m=16)
                for cidx in range(NCH):
                    for d in range(2):
                        p = ps_ring.next()
                        mm(p.t[:, :], FG[d].t[0:16, cidx * 128:(cidx + 1) * 128], W2[d].t[:], True, False, [FG[d].b, W2[d].b], [p.b])
                        mm(p.t[:, :], onesf.t[0:1, :], B2[d].t[:], False, True, [onesf.b, B2[d].b], [p.b])
                        ee = e_ring.next()
                        S.op("act", lambda e: e.activation(out=ee.t[:], in_=p.t[:], func=AF.Exp, scale=-1.0), reads=[p.b], writes=[ee.b])
                        spt = sp_ring.next()
                        S.op("act", lambda e: e.activation(out=spt.t[:], in_=ee.t[:], func=AF.Ln, bias=1.0), reads=[ee.b], writes=[spt.b])
                        S.dma("sp", (SPF if d == 0 else SPB)[cidx * 128:(cidx + 1) * 128, :], spt.t[:], reads=[spt.b])
                fm_group_pipe(O_NQ, 8, [st_stats(blk64), st_norm(NG[l], 0, st_ring, dst=NQT)])
                fm_group_pipe(O_NK, 8, [st_stats(blk64), st_norm(NG[l], 1, st_ring, dst=NKT)])
                tm_group(O_NV, 1024, NVt)
                fm_group_pipe(O_QQ, 8, [st_stats(ones128), st_norm(GQG[l], 0, qn_ring3, perm=True), st_rope(QQT)])
                fm_group_pipe(O_QK, 2, [st_stats(ones128), st_norm(GQG[l], 1, qn_ring3, perm=True), st_rope(QKT)])
                tm_group(O_QV, 256, QVt)
                for i in range(3):
                    fm_group(O_G0 + i * 1024, 8, epi_act(SGT[i], AF.Sigmoid))
            phase_end(f"inproj{l}")

        def phase_gqa(l):
            with ExitStack() as ph:
                def sb(nm, shape, dt):
                    return Tl(ph.enter_context(nc.sbuf_tensor(uname(nm), list(shape), dt)))
                KT = sb("qKT", [128, 2, T], BF16)
                V = sb("qV", [128, NCH, 256], BF16)
                S.dma("sp", KT.t[:], QKT.rearrange("(g p) t -> p g t", p=128), writes=[KT.b])
                S.dma("sp", V.t[:], QVt.rearrange("(s p) c -> p s c", p=128), writes=[V.b])
                q_ring = Ring([sb(f"qq{j}", [128, 512], BF16) for j in range(3)])
                pt_ring = Ring([sb(f"qpt{j}", [128, 512], BF16) for j in range(4)])
                rd_ring = Ring([sb(f"qrd{j}", [128, 512], F32) for j in range(2)])
                yo_ring = Ring([sb(f"qyo{j}", [128, 512], BF16) for j in range(3)])
                acc_ring = Ring([(PS[0], PS[1]), (PS[2], PS[3])])
                s_ring = Ring([PS[4], PS[5], PS[6], PS[7]])
                jobs = []
                for h in range(8):
                    if l == 0:
                        jobs.append((h, 0, NCTX, 2))
                    for t0 in range(NCTX, T, 512):
                        jobs.append((h, t0, 512, NCH))
                LOOK = 2
                items = []
                for ji, (h, t0, n, nkc) in enumerate(jobs):
                    for sc in range(nkc):
                        items.append((ji, sc))
                jst = {}
                pend = []

                def qk(ji, sc):
                    h, t0, n, nkc = jobs[ji]
                    g = h // 4
                    if sc == 0:
                        q = q_ring.next()
                        S.dma("sp", q.t[:, 0:n], QQT[h * 128:(h + 1) * 128, t0:t0 + n], writes=[q.b])
                        jst[ji] = {"q": q}
                    q = jst[ji]["q"]
                    ps_ = s_ring.next()
                    mm(ps_.t[:, 0:n], KT.t[:, g, sc * 128:(sc + 1) * 128], q.t[:, 0:n], True, True, [KT.b, q.b], [ps_.b])
                    return ps_

                def proc(ji, sc, ps_):
                    h, t0, n, nkc = jobs[ji]
                    g = h // 4
                    st = jst[ji]
                    if sc == 0:
                        st["po"], st["pd"] = acc_ring.next()
                    po, pd = st["po"], st["pd"]
                    pt = pt_ring.next()
                    S.op("act", lambda e: e.activation(out=pt.t[:, 0:n], in_=ps_.t[:, 0:n], func=AF.Exp), reads=[ps_.b], writes=[pt.b])
                    mm(po.t[:, 0:n], V.t[:, sc, g * 128:(g + 1) * 128], pt.t[:, 0:n], sc == 0, sc == nkc - 1, [V.b, pt.b], [po.b])
                    mm(pd.t[:, 0:n], ones1.t[:], pt.t[:, 0:n], sc == 0, sc == nkc - 1, [ones1.b, pt.b], [pd.b])
                    if sc == nkc - 1:
                        rd = rd_ring.next()
                        S.op("dve", lambda e: e.reciprocal(rd.t[:, 0:n], pd.t[:, 0:n]), reads=[pd.b], writes=[rd.b])
                        yo = yo_ring.next()
                        S.op("dve", lambda e: e.tensor_tensor(yo.t[:, 0:n], po.t[:, 0:n], rd.t[:, 0:n], ALU.mult), reads=[po.b, rd.b], writes=[yo.b])
                        S.dma("sp", YT[2][h * 128:(h + 1) * 128, t0:t0 + n], yo.t[:, 0:n], reads=[yo.b])
                        del jst[ji]

                for (ji, sc) in items:
                    pend.append((ji, sc, qk(ji, sc)))
                    if len(pend) > LOOK:
                        proc(*pend.pop(0))
                while pend:
                    proc(*pend.pop(0))
            phase_end(f"gqa{l}")

        def phase_nat(l):
            with ExitStack() as ph:
                def sb(nm, shape, dt):
                    return Tl(ph.enter_context(nc.sbuf_tensor(uname(nm), list(shape), dt)))
                KT = sb("nKT", [128, 4, T], BF16)
                QA = sb("nQA", [128, 4, T], BF16)
                QB = sb("nQB", [128, 4, T], BF16)
                S.op("pool", lambda e: e.memset(QA.t[64:128, :, :], 0.0), writes=[QA.b])
                S.op("pool", lambda e: e.memset(QB.t[0:64, :, :], 0.0), writes=[QB.b])
                V = sb("nV", [128, NCH, 512], BF16)
                TB = sb("nTB", [128, NSLOT, 512], BF16)
                identb = sb("nident", [128, 128], BF16)
                S.dma("pool", identb.t[:], ident_d, writes=[identb.b])
                MSK = sb("nMSK", [128, NSLOT, 64], F32)
                S.dma("sp", MSK.t[:], natm.rearrange("p (s q) -> p s q", q=64), writes=[MSK.b])
                sc_ring = Ring([sb(f"nsc{j}", [128, 512], F32) for j in range(2)])
                pt_ring = Ring([sb(f"npt{j}", [128, 512], BF16) for j in range(4)])
                rd_ring = Ring([sb(f"nrd{j}", [128, 512], F32) for j in range(3)])
                stg_ring = Ring([sb(f"nstg{j}", [128, 4, 512], BF16) for j in range(2)])
                acc_ring = Ring([(PS[0], PS[1]), (PS[2], PS[3])])
                s_ring = Ring([PS[4], PS[5], PS[6], PS[7]])
                for hg in range(2):
                    S.dma("sp", KT.t[:], NKT[hg * 512:(hg + 1) * 512, :].rearrange("(c p) t -> p c t", p=128), writes=[KT.b])
                    qsrc = NQT[hg * 512:(hg + 1) * 512, :].rearrange("(c p) t -> p c t", p=128)
                    S.dma("sp", QA.t[0:64, :, :], qsrc[0:64], writes=[QA.b])
                    S.dma("sp", QB.t[64:128, :, :], qsrc[64:128], writes=[QB.b])
                    S.dma("sp", V.t[:], NVt[:, hg * 512:(hg + 1) * 512].rearrange("(s p) c -> p s c", p=128), writes=[V.b])
                    S.dma("pool", TB.t[:].rearrange("p s c -> p (s c)"), natb[l, hg], writes=[TB.b])
                    for sl in range(NSLOT):
                        S.op("dve", lambda e: e.tensor_tensor(
                            TB.t[:, sl, :].rearrange("p (h q) -> p h q", h=8), TB.t[:, sl, :].rearrange("p (h q) -> p h q", h=8),
                            MSK.t[:, sl, :].unsqueeze(1).broadcast_to([128, 8, 64]), ALU.add),
                            reads=[TB.b, MSK.b], writes=[TB.b])
                    blocks = []
                    if l == 0:
                        blocks.append((0, [(b * 64, [(0, None), (1, None)]) for b in range(4)]))
                    for rg in range(8):
                        rows_ = []
                        for r in range(rg * 8, rg * 8 + 8):
                            rs = min(max(r - 4, 0), 56)
                            ch = []
                            if rs % 2 == 0:
                                for j in range(4):
                                    m = rs // 2 + j
                                    ch.append((2 + m, 2 * m - r + 7))
                            else:
                                m0 = (rs - 1) // 2
                                for j in range(5):
                                    m = m0 + j
                                    dr0 = 2 * m - r
                                    slot = 14 if j == 0 else (15 if j == 4 else dr0 + 7)
                                    ch.append((2 + m, slot))
                            ch += [(0, None), (1, None)]
                            rows_.append((NCTX + r * 64, ch))
                        blocks.append((NCTX + rg * 512, rows_))
                    import os as _os
                    _lim = int(_os.environ.get("NAT_LIM", "99"))
                    LOOK = 3
                    jobs = []
                    for (bt0, rows_) in blocks[:_lim]:
                        for ri, (q0, chunks) in enumerate(rows_):
                            jobs.append((bt0, q0, chunks, ri == 0, ri == len(rows_) - 1, len(rows_) * 64))
                    items = [(ji, ci) for ji, jb in enumerate(jobs) for ci in range(len(jb[2]))]
                    jst = {}
                    cur_stg = [None]

                    def qk(ji, ci):
                        bt0, q0, chunks, first, lastrow, nb = jobs[ji]
                        chunk, slot = chunks[ci]
                        ps_ = s_ring.next()
                        if slot is not None:
                            S.op("pe", lambda e: e.matmul(ps_.t[:, :], identb.t[:], TB.t[:, slot, :], start=True, stop=False, skip_group_check=True),
                                 reads=[identb.b, TB.b], writes=[ps_.b])
                        for hh in range(8):
                            cc, half = hh // 2, hh % 2
                            Qh = QA if half == 0 else QB
                            S.op("pe", lambda e: e.matmul(ps_.t[:, hh * 64:(hh + 1) * 64], KT.t[:, cc, chunk * 128:(chunk + 1) * 128],
                                                          Qh.t[:, cc, q0:q0 + 64], start=(slot is None), stop=True, skip_group_check=True),
                                 reads=[KT.b, Qh.b], writes=[ps_.b])
                        return ps_

                    def proc(ji, ci, ps_):
                        bt0, q0, chunks, first, lastrow, nb = jobs[ji]
                        chunk, slot = chunks[ci]
                        last = len(chunks) - 1
                        if ci == 0:
                            jst[ji] = acc_ring.next()
                            for dfr in list(deferred):
                                if dfr[2] is jst[ji][0]:
                                    dfr[1]()
                                    deferred.remove(dfr)
                            if first:
                                cur_stg[0] = stg_ring.next()
                        po, pd = jst[ji]
                        stg = cur_stg[0]
                        pt = pt_ring.next()
                        S.op("act", lambda e: e.activation(out=pt.t[:], in_=ps_.t[:], func=AF.Exp), reads=[ps_.b], writes=[pt.b])
                        for hh in range(8):
                            cc = hh // 2
                            S.op("pe", lambda e: e.matmul(po.t[:, hh * 64:(hh + 1) * 64], V.t[:, chunk, cc * 128:(cc + 1) * 128],
                                                          pt.t[:, hh * 64:(hh + 1) * 64], start=(ci == 0 and hh == 0),
                                                          stop=(ci == last and hh == 7), skip_group_check=True),
                                 reads=[V.b, pt.b], writes=[po.b])
                        mm(pd.t[:, :], ones1.t[:], pt.t[:], ci == 0, ci == last, [ones1.b, pt.b], [pd.b])
                        if ci == last:
                            rd = rd_ring.next()

                            def fin_a(pd=pd, rd=rd):
                                S.op("act", lambda e: e.activation(out=rd.t[:], in_=pd.t[:, :], func=AF.Ln), reads=[pd.b], writes=[rd.b])
                                S.op("act", lambda e: e.activation(out=rd.t[:], in_=rd.t[:], func=AF.Exp, scale=-1.0), reads=[rd.b], writes=[rd.b])

                            def fin_b(po=po, rd=rd, stg=stg, q0=q0, bt0=bt0, lastrow=lastrow, nb=nb):
                                o0 = q0 - bt0
                                for hf in range(2):
                                    S.op("dve", lambda e: e.tensor_tensor(
                                        stg.t[hf * 64:(hf + 1) * 64, :, o0:o0 + 64],
                                        po.t[hf * 64:(hf + 1) * 64, :].rearrange("p (c h q) -> p c h q", c=4, h=2)[:, :, hf, :],
                                        rd.t[hf * 64:(hf + 1) * 64, :].rearrange("p (c h q) -> p c h q", c=4, h=2)[:, :, hf, :], ALU.mult),
                                        reads=[po.b, rd.b], writes=[stg.b])
                                if lastrow:
                                    S.dma("sp", YT[1][hg * 512:(hg + 1) * 512, bt0:bt0 + nb].rearrange("(c p) t -> p c t", p=128),
                                          stg.t[:, :, 0:nb], reads=[stg.b])
                            deferred.append([2, fin_a, po])
                            deferred.append([4, fin_b, po])
                            del jst[ji]
                        for dfr in list(deferred):
                            dfr[0] -= 1
                            if dfr[0] < 0:
                                dfr[1]()
                                deferred.remove(dfr)

                    deferred = []
                    pend = []
                    for (ji, ci) in items:
                        pend.append((ji, ci, qk(ji, ci)))
                        if len(pend) > LOOK:
                            proc(*pend.pop(0))
                    while pend:
                        proc(*pend.pop(0))
                    for dfr in deferred:
                        dfr[1]()
            phase_end(f"nat{l}")

        def phase_gla(l):
            with ExitStack() as ph:
                def sb(nm, shape, dt):
                    return Tl(ph.enter_context(nc.sbuf_tensor(uname(nm), list(shape), dt)))
                TRI = sb("gTRI", [128, 4, 128], F32)
                MSKG = sb("gMSK", [128, 2, 128], F32)
                S.dma("sp", TRI.t[:], tri_d.rearrange("a p t -> p a t"), writes=[TRI.b])
                S.dma("sp", MSKG.t[:], msk_d.rearrange("a p t -> p a t"), writes=[MSKG.b])
                qT = sb("gqT", [128, T], BF16)
                kT = sb("gkT", [128, T], BF16)
                ktok = sb("gktok", [128, NCH, 128], BF16)
                vtok = sb("gvtok", [128, NCH, 256], BF16)
                sp_ = [sb(f"gsp{d}", [128, NCH, 128], F32) for d in range(2)]
                O = sb("gO", [128, 2, T], F32)
                st32 = [sb(f"gst32{d}", [128, 256], F32) for d in range(2)]
                st16 = [sb(f"gst16{d}", [128, 256], BF16) for d in range(2)]
                Ep_ring = Ring([sb(f"gEp{j}", [128, 128], F32) for j in range(7)])
                En_ring = Ring([sb(f"gEn{j}", [128, 128], F32) for j in range(3)])
                Dk_ring = Ring([sb(f"gDk{j}", [128, 128], F32) for j in range(3)])
                qe_ring = Ring([sb(f"gqe{j}", [128, 128], BF16) for j in range(6)])
                kin_ring = Ring([sb(f"gkin{j}", [128, 128], BF16) for j in range(4)])
                kend_ring = Ring([sb(f"gkend{j}", [128, 128], BF16) for j in range(4)])
                att_ring = Ring([sb(f"gatt{j}", [128, 128], BF16) for j in range(4)])
                fsq_ring = Ring([sb(f"gfsq{j}", [128, 2, 512], BF16) for j in range(2)])
                frs_ring = Ring([sb(f"gfrs{j}", [128, 512], F32) for j in range(2)])
                fgg_ring = Ring([sb(f"gfgg{j}", [128, 2, 512], BF16) for j in range(2)])
                ftm_ring = Ring([sb(f"gftm{j}", [128, 512], F32) for j in range(2)])
                fy_ring = Ring([sb(f"gfy{j}", [128, 2, 512], BF16) for j in range(2)])
                orders = [list(range(NCH)), [1, 0] + list(range(NCH - 1, 1, -1))]
                gd_ring = Ring([PS[0], PS[1]])
                as_ring = Ring([PS[2], PS[3], PS[4]])
                o_ring = Ring([PS[5], PS[6], PS[7]])
                for h in range(4):
                    S.dma("sp", qT.t[:], GQT[h * 128:(h + 1) * 128, :], writes=[qT.b])
                    S.dma("sp", kT.t[:], GKT[h * 128:(h + 1) * 128, :], writes=[kT.b])
                    S.dma("sp", ktok.t[:], GKt[:, h * 128:(h + 1) * 128].rearrange("(s p) c -> p s c", p=128), writes=[ktok.b])
                    S.dma("sp", vtok.t[:], GVt[:, h * 256:(h + 1) * 256].rearrange("(s p) c -> p s c", p=128), writes=[vtok.b])
                    S.dma("sp", sp_[0].t[:], SPF[:, h * 128:(h + 1) * 128].rearrange("(s p) c -> p s c", p=128), writes=[sp_[0].b])
                    S.dma("sp", sp_[1].t[:], SPB[:, h * 128:(h + 1) * 128].rearrange("(s p) c -> p s c", p=128), writes=[sp_[1].b])
                    S.op("pool", lambda e: e.memset(O.t[:], 0.0), writes=[O.b])
                    for d in range(2):
                        S.op("pool", lambda e: e.memset(st32[d].t[:], 0.0), writes=[st32[d].b])
                        S.op("pool", lambda e: e.memset(st16[d].t[:], 0.0), writes=[st16[d].b])
                    items = []
                    for step in range(NCH):
                        for d in range(2):
                            items.append({"d": d, "c": orders[d][step]})

                    def g0(it):
                        d, c = it["d"], it["c"]
                        spc = sp_[d].t[:, c, :]
                        gd = gd_ring.next()
                        mm(gd.t[:, 0:128], spc, TRI.t[:, d, :], True, True, [sp_[d].b, TRI.b], [gd.b])
                        mm(gd.t[:, 128:256], TRI.t[:, 2 + d, :], spc, True, True, [sp_[d].b, TRI.b], [gd.b])
                        it["gd"] = gd

                    def g1(it):
                        gd = it["gd"]
                        Ep, En, Dk = Ep_ring.next(), En_ring.next(), Dk_ring.next()
                        S.op("act", lambda e: e.activation(out=Ep.t[:], in_=gd.t[:, 0:128], func=AF.Exp), reads=[gd.b], writes=[Ep.b])
                        S.op("act", lambda e: e.activation(out=En.t[:], in_=gd.t[:, 0:128], func=AF.Exp, scale=-1.0), reads=[gd.b], writes=[En.b])
                        S.op("act", lambda e: e.activation(out=Dk.t[:], in_=gd.t[:, 128:256], func=AF.Exp), reads=[gd.b], writes=[Dk.b])
                        it.update(Ep=Ep, En=En, Dk=Dk)

                    def g2(it):
                        c, Ep, En, Dk = it["c"], it["Ep"], it["En"], it["Dk"]
                        tk = slice(c * 128, (c + 1) * 128)
                        qe, kin, kend = qe_ring.next(), kin_ring.next(), kend_ring.next()
                        S.op("dve", lambda e: e.scalar_tensor_tensor(qe.t[:], qT.t[:, tk], 128.0 ** -0.5, Ep.t[:], ALU.mult, ALU.mult),
                             reads=[qT.b, Ep.b], writes=[qe.b])
                        S.op("pool", lambda e: e.tensor_tensor(kin.t[:], kT.t[:, tk], En.t[:], ALU.mult), reads=[kT.b, En.b], writes=[kin.b])
                        S.op("pool", lambda e: e.tensor_tensor(kend.t[:], ktok.t[:, c, :], Dk.t[:], ALU.mult), reads=[ktok.b, Dk.b], writes=[kend.b])
                        it.update(qe=qe, kin=kin, kend=kend)

                    def g3(it):
                        c = it["c"]
                        a_s = as_ring.next()
                        mm(a_s.t[:, 0:128], it["kin"].t[:], it["qe"].t[:], True, True, [it["kin"].b, it["qe"].b], [a_s.b])
                        mm(a_s.t[:, 128:384], it["kend"].t[:], vtok.t[:, c, :], True, True, [it["kend"].b, vtok.b], [a_s.b])
                        it["as"] = a_s

                    def g4(it):
                        d, a_s, Ep = it["d"], it["as"], it["Ep"]
                        att = att_ring.next()
                        S.op("dve", lambda e: e.tensor_tensor(att.t[:], a_s.t[:, 0:128], MSKG.t[:, d, :], ALU.mult), reads=[a_s.b, MSKG.b], writes=[att.b])
                        it["att"] = att

                    def g5(it):
                        d, c, qe, att, a_s, Ep = it["d"], it["c"], it["qe"], it["att"], it["as"], it["Ep"]
                        pO = o_ring.next()
                        for vc in range(2):
                            mm(pO.t[:, vc * 128:(vc + 1) * 128], vtok.t[:, c, vc * 128:(vc + 1) * 128], att.t[:], True, False,
                               [vtok.b, att.b], [pO.b])
                            mm(pO.t[:, vc * 128:(vc + 1) * 128], st16[d].t[:, vc * 128:(vc + 1) * 128], qe.t[:], False, True,
                               [st16[d].b, qe.b], [pO.b])
                        it["pO"] = pO
                        egl = Ep.t[:, 127:128] if d == 0 else Ep.t[:, 0:1]
                        S.op("dve", lambda e: e.scalar_tensor_tensor(st32[d].t[:], st32[d].t[:], egl, a_s.t[:, 128:384], ALU.mult, ALU.add),
                             reads=[st32[d].b, Ep.b, a_s.b], writes=[st32[d].b])

                    def g6(it):
                        d, c, pO = it["d"], it["c"], it["pO"]
                        tk = slice(c * 128, (c + 1) * 128)
                        S.op("act", lambda e: e.copy(st16[d].t[:], st32[d].t[:]), reads=[st32[d].b], writes=[st16[d].b])
                        S.op("dve", lambda e: e.tensor_tensor(O.t[:, :, tk], O.t[:, :, tk], pO.t[:, 0:256].rearrange("p (v t) -> p v t", v=2), ALU.add),
                             reads=[pO.b, O.b], writes=[O.b])

                    gst = [g0, g1, g2, g3, g4, g5, g6]
                    NS = len(gst)
                    for step in range(len(items) + NS - 1):
                        for si in range(NS - 1, -1, -1):
                            i = step - si
                            if 0 <= i < len(items):
                                gst[si](items[i])
                    for (t0, n, kind) in tiles(512, l == 0):
                        fsq = fsq_ring.next()
                        S.op("act", lambda e: e.activation(out=fsq.t[:, :, 0:n], in_=O.t[:, :, t0:t0 + n], func=AF.Square), reads=[O.b], writes=[fsq.b])
                        pM = ps_ring.next()
                        for vc in range(2):
                            mm(pM.t[:, 0:n], ones256.t[:], fsq.t[:, vc, 0:n], vc == 0, vc == 1, [ones256.b, fsq.b], [pM.b])
                        frs = frs_ring.next()
                        S.op("act", lambda e: e.activation(out=frs.t[:, 0:n], in_=pM.t[:, 0:n], func=AF.Ln, bias=EPS), reads=[pM.b], writes=[frs.b])
                        S.op("act", lambda e: e.activation(out=frs.t[:, 0:n], in_=frs.t[:, 0:n], func=AF.Exp, scale=-0.5), reads=[frs.b], writes=[frs.b])
                        fgg = fgg_ring.next()
                        S.dma("sp", fgg.t[:, :, 0:n], GGT[h * 256:(h + 1) * 256, t0:t0 + n].rearrange("(v p) t -> p v t", p=128), writes=[fgg.b])
                        fy = fy_ring.next()
                        for vc in range(2):
                            ftm = ftm_ring.next()
                            S.op("dve", lambda e: e.scalar_tensor_tensor(ftm.t[:, 0:n], O.t[:, vc, t0:t0 + n], GLG[l].t[:, vc:vc + 1], frs.t[:, 0:n],
                                                                         ALU.mult, ALU.mult), reads=[O.b, GLG[l].b, frs.b], writes=[ftm.b])
                            S.op("pool", lambda e: e.tensor_tensor(fy.t[:, vc, 0:n], ftm.t[:, 0:n], fgg.t[:, vc, 0:n], ALU.mult),
                                 reads=[ftm.b, fgg.b], writes=[fy.b])
                        S.dma("sp", YT[0][h * 256:(h + 1) * 256, t0:t0 + n].rearrange("(v p) t -> p v t", p=128), fy.t[:, :, 0:n], reads=[fy.b])
            phase_end(f"gla{l}")

        def phase_merge(l):
            with ExitStack() as ph:
                def sb(nm, shape, dt):
                    return Tl(ph.enter_context(nc.sbuf_tensor(uname(nm), list(shape), dt)))
                WB = [sb(f"mWB{i}", [128, 8, D], BF16) for i in range(3)]
                WO = sb("mWO", [128, 8, D], BF16)
                for i in range(3):
                    S.dma("pool", WB[i].t[:], w_br[l, i].rearrange("(k p) n -> p k n", p=128), writes=[WB[i].b])
                S.dma("pool", WO.t[:], w_o[l].rearrange("(k p) n -> p k n", p=128), writes=[WO.b])
                y_ring = Ring([sb(f"my{j}", [128, 8, 512], BF16) for j in range(4)])
                sg_ring = Ring([sb(f"msg{j}", [128, 8, 512], BF16) for j in range(4)])
                xt_ring = Ring([sb(f"mxt{j}", [128, 8, 512], F32) for j in range(2)])
                z_ring = Ring([sb(f"mz{j}", [128, 8, 512], BF16) for j in range(2)])
                ta_ring = Ring([sb(f"mta{j}", [128, 512], F32) for j in range(2)])
                tb_ring = Ring([sb(f"mtb{j}", [128, 512], F32) for j in range(2)])
                tc_ring = Ring([sb(f"mtc{j}", [128, 512], F32) for j in range(2)])
                for (t0, n, kind) in tiles(512, l == 0):
                    der = DER[l][kind]
                    ys, sgs = [], []
                    for i in range(3):
                        y = y_ring.next()
                        S.dma("sp", y.t[:, :, 0:n], YT[i].rearrange("(c p) t -> p c t", p=128)[:, :, t0:t0 + n], writes=[y.b])
                        sg = sg_ring.next()
                        S.dma("sp", sg.t[:, :, 0:n], SGT[i].rearrange("(c p) t -> p c t", p=128)[:, :, t0:t0 + n], writes=[sg.b])
                        ys.append(y)
                        sgs.append(sg)
                    xt = xt_ring.next()
                    S.dma("sp", xt.t[:, :, 0:n], XTv[:, :, t0:t0 + n], writes=[xt.b])
                    z = z_ring.next()
                    for nn in range(8):
                        pp = []
                        for i in range(3):
                            p = ps_ring.next()
                            for k in range(8):
                                mm(p.t[:, 0:n], WB[i].t[:, k, nn * 128:(nn + 1) * 128], ys[i].t[:, k, 0:n], k == 0, k == 7,
                                   [WB[i].b, ys[i].b], [p.b])
                            pp.append(p)
                        ta, tb, tc = ta_ring.next(), tb_ring.next(), tc_ring.next()
                        for (tt_, i) in ((ta, 0), (tb, 1), (tc, 2)):
                            S.op("dve", lambda e: e.tensor_tensor(tt_.t[:, 0:n], pp[i].t[:, 0:n], sgs[i].t[:, nn, 0:n], ALU.mult),
                                 reads=[pp[i].b, sgs[i].b], writes=[tt_.b])
                        S.op("pool", lambda e: e.tensor_tensor(ta.t[:, 0:n], ta.t[:, 0:n], tb.t[:, 0:n], ALU.add), reads=[ta.b, tb.b], writes=[ta.b])
                        S.op("pool", lambda e: e.tensor_tensor(z.t[:, nn, 0:n], ta.t[:, 0:n], tc.t[:, 0:n], ALU.add), reads=[ta.b, tc.b], writes=[z.b])
                    for nn in range(8):
                        p = ps_ring.next()
                        for k in range(8):
                            mm(p.t[:, 0:n], WO.t[:, k, nn * 128:(nn + 1) * 128], z.t[:, k, 0:n], k == 0, k == 7, [WO.b, z.b], [p.b])
                        S.op("dve", lambda e: e.scalar_tensor_tensor(xt.t[:, nn, 0:n], p.t[:, 0:n], der.t[:, 5, nn:nn + 1], xt.t[:, nn, 0:n],
                                                                     ALU.mult, ALU.add), reads=[p.b, der.b, xt.b], writes=[xt.b])
                    S.dma("sp", XTv[:, :, t0:t0 + n], xt.t[:, :, 0:n], reads=[xt.b])
            phase_end(f"merge{l}")

        def phase_final():
            with ExitStack() as ph:
                def sb(nm, shape, dt):
                    return Tl(ph.enter_context(nc.sbuf_tensor(uname(nm), list(shape), dt)))
                xin_ring = Ring([sb(f"fx{i}", [128, 8, 512], F32) for i in range(2)])
                xo_ring = Ring([sb(f"fo{i}", [128, D], F32) for i in range(3)])
                ev = 0
                for g in range(8):
                    xi = xin_ring.next()
                    t0 = NCTX + g * 512
                    S.dma("sp", xi.t[:], XTv[:, :, t0:t0 + 512], writes=[xi.b])
                    for j in range(4):
                        xo = xo_ring.next()
                        for half in range(2):
                            p = ps_ring.next()
                            for q in range(4):
                                c = half * 4 + q
                                S.op("pe", lambda e: e.transpose(p.t[:, q * 128:(q + 1) * 128], xi.t[:, c, j * 128:(j + 1) * 128],
                                                                 ident.t[:]), reads=[xi.b, ident.b], writes=[p.b])
                            if ev % 2 == 0:
                                S.op("act", lambda e: e.copy(xo.t[:, half * 512:(half + 1) * 512], p.t[:]), reads=[p.b], writes=[xo.b])
                            else:
                                S.op("dve", lambda e: e.tensor_copy(xo.t[:, half * 512:(half + 1) * 512], p.t[:]), reads=[p.b],
                                     writes=[xo.b])
                            ev += 1
                        r0 = g * 512 + j * 128
                        S.dma("sp", out_d[r0:r0 + 128, :], xo.t[:], reads=[xo.b])
            phase_end("final")

        table = {"prep": phase_prep, "final": phase_final}
        order = ["prep"]
        for l in range(2):
            table[f"ffn{l}0"] = (lambda l=l: phase_ffn(l, 0, True, f"ffn{l}0"))
            table[f"inproj{l}"] = (lambda l=l: phase_inproj(l))
            table[f"gla{l}"] = (lambda l=l: phase_gla(l))
            table[f"nat{l}"] = (lambda l=l: phase_nat(l))
            table[f"gqa{l}"] = (lambda l=l: phase_gqa(l))
            table[f"merge{l}"] = (lambda l=l: phase_merge(l))
            table[f"ffn{l}1"] = (lambda l=l: phase_ffn(l, 1, l == 0, f"ffn{l}1"))
            order += [f"ffn{l}0", f"inproj{l}", f"gla{l}", f"nat{l}", f"gqa{l}", f"merge{l}", f"ffn{l}1"]
        order.append("final")
        for name in (phases or order):
            table[name]()
        S.barrier()
        print("instructions:", S.n_ins, {k: v for k, v in S.cnt.items()})
    return nc


def host_constants():
    c = {}
    c["ident"] = np.eye(128, dtype=np.float32)
    perm = np.zeros((128, 128), np.float32)
    for m in range(128):
        blk, r = divmod(m, 64)
        src = blk * 64 + (r + 32) % 64
        perm[src, m] = 1.0
    c["perm"] = perm
    half = 32
    freqs = (10000.0 ** (-np.arange(half, dtype=np.float32) / half)).astype(np.float32)
    tt = np.arange(NLAT)
    rows = (tt // 64).astype(np.float32)
    cols = (tt % 64).astype(np.float32)
    C = np.ones((128, T), np.float32)
    Sg = np.zeros((128, T), np.float32)
    for blk, pos in enumerate((rows, cols)):
        ang = pos[None, :] * freqs[:, None]
        cs, sn = np.cos(ang).astype(np.float32), np.sin(ang).astype(np.float32)
        C[blk * 64:blk * 64 + 32, NCTX:] = cs
        C[blk * 64 + 32:blk * 64 + 64, NCTX:] = cs
        Sg[blk * 64:blk * 64 + 32, NCTX:] = -sn
        Sg[blk * 64 + 32:blk * 64 + 64, NCTX:] = sn
    c["ropec"] = C
    c["ropes"] = Sg
    i = np.arange(128)
    a = -1.0 / 16.0
    tri = np.zeros((4, 128, 128), np.float32)
    tri[0] = a * (i[:, None] <= i[None, :])
    tri[1] = a * (i[:, None] >= i[None, :])
    tri[2] = a * (i[:, None] > i[None, :])
    tri[3] = a * (i[:, None] < i[None, :])
    c["tri"] = tri
    gm = np.zeros((2, 128, 128), np.float32)
    gm[0] = (i[:, None] <= i[None, :])
    gm[1] = (i[:, None] >= i[None, :])
    c["gmask"] = gm
    kc = np.arange(64)
    qc = np.arange(64)
    cstart = np.clip(qc - 8, 0, 48)
    colok = (kc[:, None] >= cstart[None, :]) & (kc[:, None] < cstart[None, :] + 16)
    m = np.where(colok, 0.0, -1e30).astype(np.float32)
    natm = np.zeros((128, NSLOT, 64), np.float32)
    for s in range(NSLOT):
        natm[0:64, s] = m
        natm[64:128, s] = m
    natm[0:64, 14] = -1e30
    natm[64:128, 15] = -1e30
    c["natm"] = natm.reshape(128, NSLOT * 64)
    return c


def gather_nat_bias(rpb):
    p = np.arange(128)
    half = p // 64
    kc = p % 64
    qc = np.arange(64)
    dc = np.clip(kc[:, None] - qc[None, :] + 15, 0, 30)
    out = np.empty((2, 2, 128, NSLOT, 8, 64), np.float32)
    for s in range(NSLOT):
        dr = np.clip(SLOT_DR0[s] + half + 7, 0, 14)
        for hg in range(2):
            for h in range(8):
                out[:, hg, :, s, h, :] = rpb[:, hg * 8 + h][:, dr[:, None], dc]
    return out.reshape(2, 2, 128, NSLOT * 512)


_CACHE = {}


def kernel(x, c, ctx, c_ctx, w_mod, b_mod, norm_g, ffn_w_in, ffn_w_out, w_in, gla_fg_w2, gla_fg_b,
           gla_norm_g, nat_q_norm, nat_k_norm, nat_rpb, gqa_q_norm, gqa_k_norm, w_branch, w_out):
    f = lambda a: np.ascontiguousarray(np.asarray(a, dtype=np.float32))
    if "nc" not in _CACHE:
        _CACHE["nc"] = build_program()
    nc = _CACHE["nc"]
    consts = host_constants()
    shared = dict(
        w_mod=f(w_mod), b_mod=f(b_mod).reshape(2, 72, 128), norm_g=f(norm_g).reshape(2, 24, 128),
        ffn_w_in=f(ffn_w_in), ffn_w_out=f(ffn_w_out), w_in=f(w_in), gla_fg_w2=f(gla_fg_w2),
        gla_fg_b=f(gla_fg_b).reshape(2, 2, 1, 512), gla_norm_g=f(gla_norm_g).reshape(2, 2, 128),
        nat_q_norm=f(nat_q_norm).reshape(2, 1, 64), nat_k_norm=f(nat_k_norm).reshape(2, 1, 64),
        gqa_q_norm=f(gqa_q_norm).reshape(2, 1, 128), gqa_k_norm=f(gqa_k_norm).reshape(2, 1, 128),
        w_branch=f(w_branch), w_out=f(w_out), natb=gather_nat_bias(f(nat_rpb)), **consts)
    x = f(x)
    ctx = f(ctx)
    c = f(c)
    c_ctx = f(c_ctx)
    in_maps = []
    for b in range(8):
        m = dict(shared)
        m["x"] = x[b]
        m["ctx"] = ctx[b]
        m["cc"] = np.ascontiguousarray(np.stack([c[b], c_ctx]).reshape(16, 128))
        in_maps.append(m)
    res = run_bass_kernel_spmd(nc, in_maps, core_ids=list(range(8)))
    return np.stack([np.asarray(r["out"], dtype=np.float32) for r in res.results], axis=0)
```

```python
import numpy as np
from contextlib import ExitStack
import concourse.bass as bass
import concourse.mybir as mybir
from concourse.bass_utils import run_bass_kernel_spmd

F32 = mybir.dt.float32
BF16 = mybir.dt.bfloat16
AF = mybir.ActivationFunctionType
ALU = mybir.AluOpType

NCTX = 256
NLAT = 4096
T = NCTX + NLAT
D = 1024
DFF = 2816
NCH = T // 128
EPS = 1e-6
C_GLA = 128

O_GQ, O_GK, O_GV, O_GG, O_FG = 0, 512, 1024, 2048, 3072
O_NQ, O_NK, O_NV = 3104, 4128, 5152
O_QQ, O_QK, O_QV = 6176, 7200, 7456
O_G0 = 7712
W_IN = 10784

NSLOT = 16
SLOT_DR0 = [d for d in range(-7, 7)] + [-5, 3]


class Buf:
    __slots__ = ("w", "r")

    def __init__(self):
        self.w = None
        self.r = {}


class Sched:
    def __init__(self, nc, es, n_dma_sems=14):
        self.nc = nc
        self.eng = {"pe": nc.tensor, "act": nc.scalar, "dve": nc.vector, "pool": nc.gpsimd, "sp": nc.sync}
        self.sem = {k: es.enter_context(nc.semaphore("sem_" + k)) for k in self.eng}
        self.cnt = {k: 0 for k in self.eng}
        self.seen = {k: {} for k in self.eng}
        self.dsem, self.dval, self.dnext = {}, {}, {}
        for q in ("sp", "pool"):
            self.dsem[q] = [es.enter_context(nc.semaphore(f"dsem_{q}{i}")) for i in range(n_dma_sems)]
            self.dval[q] = [0] * n_dma_sems
            self.dnext[q] = 0
        self.n_ins = 0

    def _semobj(self, key):
        if isinstance(key, str):
            return self.sem[key]
        return self.dsem[key[0]][key[1]]

    def _wait(self, e, tok, raw=False):
        if tok is None:
            return
        key, val = tok
        if key == e and e == "pe":
            return
        if self.seen[e].get(key, 0) >= val:
            return
        self.seen[e][key] = val
        self.eng[e].wait_ge(self._semobj(key), val)

    def _deps(self, e, reads, writes):
        for b in reads:
            self._wait(e, b.w, raw=True)
        for b in writes:
            self._wait(e, b.w)
            for tok in b.r.values():
                self._wait(e, tok)

    def _mark(self, tok, reads, writes):
        for b in reads:
            b.r[tok[0]] = tok
        for b in writes:
            b.w = tok
            b.r = {}

    def op(self, e, fn, reads=(), writes=()):
        self._deps(e, reads, writes)
        ins = fn(self.eng[e])
        self.cnt[e] += 1
        ins.then_inc(self.sem[e], 1)
        self._mark((e, self.cnt[e]), reads, writes)
        self.n_ins += 1

    def dma(self, q, out, in_, reads=(), writes=()):
        i = self.dnext[q]
        self.dnext[q] = (i + 1) % len(self.dsem[q])
        key = (q, i)
        if self.dval[q][i] > 0:
            self._wait(q, (key, self.dval[q][i]))
        self._deps(q, reads, writes)
        self.dval[q][i] += 16
        self.eng[q].dma_start(out=out, in_=in_).then_inc(self.dsem[q][i], 16)
        self._mark((key, self.dval[q][i]), reads, writes)
        self.n_ins += 1

    def barrier(self, engines=None):
        for e in (engines or self.eng):
            for o in self.eng:
                if o != e and self.cnt[o] > 0:
                    self._wait(e, (o, self.cnt[o]))
            for q in self.dsem:
                for i, v in enumerate(self.dval[q]):
                    if v > 0:
                        self._wait(e, ((q, i), v))


class Tl:
    __slots__ = ("t", "b")

    def __init__(self, t):
        self.t = t
        self.b = Buf()


class Ring:
    def __init__(self, items):
        self.items = items
        self.i = 0

    def next(self):
        x = self.items[self.i]
        self.i = (self.i + 1) % len(self.items)
        return x


def build_program(phases=None, dump=()):
    nc = bass.Bass("TRN2", target_bir_lowering=False)

    def din(name, shape, dt=F32):
        return nc.dram_tensor(name, list(shape), dt, kind="ExternalInput").ap()

    def dscr(name, shape, dt):
        kind = "ExternalOutput" if name in dump else "Internal"
        return nc.dram_tensor(name, list(shape), dt, kind=kind).ap()

    x_d = din("x", [NLAT, D])
    ctx_d = din("ctx", [NCTX, D])
    cc_d = din("cc", [16, 128])
    w_mod = din("w_mod", [2, D, 9 * D])
    b_mod = din("b_mod", [2, 72, 128])
    norm_g = din("norm_g", [2, 24, 128])
    ffn_wi = din("ffn_w_in", [2, 2, D, 2 * DFF])
    ffn_wo = din("ffn_w_out", [2, 2, DFF, D])
    w_in = din("w_in", [2, D, W_IN])
    fg_w2 = din("gla_fg_w2", [2, 2, 16, 512])
    fg_b = din("gla_fg_b", [2, 2, 1, 512])
    gla_g = din("gla_norm_g", [2, 2, 128])
    nat_qg = din("nat_q_norm", [2, 1, 64])
    nat_kg = din("nat_k_norm", [2, 1, 64])
    gqa_qg = din("gqa_q_norm", [2, 1, 128])
    gqa_kg = din("gqa_k_norm", [2, 1, 128])
    w_br = din("w_branch", [2, 3, D, D])
    w_o = din("w_out", [2, D, D])
    natb = din("natb", [2, 2, 128, NSLOT * 512])
    natm = din("natm", [128, NSLOT * 64])
    ropec = din("ropec", [128, T])
    ropes = din("ropes", [128, T])
    ident_d = din("ident", [128, 128])
    perm_d = din("perm", [128, 128])
    tri_d = din("tri", [4, 128, 128])
    msk_d = din("gmask", [2, 128, 128])

    out_d = nc.dram_tensor("out", [NLAT, D], F32, kind="ExternalOutput").ap()

    XT = dscr("XT", [D, T], F32)
    GQT = dscr("GQT", [512, T], BF16)
    GKT = dscr("GKT", [512, T], BF16)
    GGT = dscr("GGT", [D, T], BF16)
    SPF = dscr("SPF", [T, 512], F32)
    SPB = dscr("SPB", [T, 512], F32)
    GKt = dscr("GKt", [T, 512], BF16)
    GVt = dscr("GVt", [T, 1024], BF16)
    NQT = dscr("NQT", [D, T], BF16)
    NKT = dscr("NKT", [D, T], BF16)
    NVt = dscr("NVt", [T, D], BF16)
    QQT = dscr("QQT", [D, T], BF16)
    QKT = dscr("QKT", [256, T], BF16)
    QVt = dscr("QVt", [T, 256], BF16)
    SGT = dscr("SGT", [3, D, T], BF16)
    YT = dscr("YT", [3, D, T], BF16)
    UTD = dscr("UTD", [D, T], BF16)

    es = ExitStack()
    with es:
        S = Sched(nc, es)

        uid = [0]

        def uname(name):
            uid[0] += 1
            return f"s{uid[0]}_{name}"

        def gsb(name, shape, dt):
            return Tl(es.enter_context(nc.sbuf_tensor(uname(name), list(shape), dt)))

        PS = [Tl(es.enter_context(nc.psum_tensor(f"ps{i}", [128, 512], F32))) for i in range(8)]

        ident = gsb("ident", [128, 128], F32)
        permb = gsb("permb", [128, 128], BF16)
        onesD = gsb("onesD", [128, 128], BF16)
        ones128 = gsb("ones128", [128, 128], BF16)
        ones256 = gsb("ones256", [128, 128], BF16)
        blk64 = gsb("blk64", [128, 128], BF16)
        ones1 = gsb("ones1", [128, 128], BF16)
        onesf = gsb("onesf", [1, 128], F32)
        DER = [[gsb(f"der{l}{k}", [128, 9, 8], F32) for k in range(2)] for l in range(2)]
        GQG = [gsb(f"gqg{l}", [128, 2], F32) for l in range(2)]
        NG = [gsb(f"ng{l}", [128, 2], F32) for l in range(2)]
        GLG = [gsb(f"glg{l}", [128, 2], F32) for l in range(2)]

        S.dma("sp", ident.t[:], ident_d, writes=[ident.b])
        S.dma("pool", permb.t[:], perm_d, writes=[permb.b])
        S.op("dve", lambda e: e.memset(onesD.t[:], 1.0 / 1024), writes=[onesD.b])
        S.op("dve", lambda e: e.memset(ones128.t[:], 1.0 / 128), writes=[ones128.b])
        S.op("dve", lambda e: e.memset(ones256.t[:], 1.0 / 256), writes=[ones256.b])
        S.op("dve", lambda e: e.memset(ones1.t[:], 1.0), writes=[ones1.b])
        S.op("dve", lambda e: e.memset(onesf.t[:], 1.0), writes=[onesf.b])
        S.op("dve", lambda e: e.memset(blk64.t[:], 0.0), writes=[blk64.b])
        S.op("dve", lambda e: e.memset(blk64.t[0:64, 0:64], 1.0 / 64), writes=[blk64.b])
        S.op("dve", lambda e: e.memset(blk64.t[64:128, 64:128], 1.0 / 64), writes=[blk64.b])

        ps_ring = Ring(PS)

        def mm(out, lhsT, rhs, start, stop, reads, writes):
            S.op("pe", lambda e: e.matmul(out, lhsT, rhs, start=start, stop=stop), reads=reads, writes=writes)

        def phase_end(name):
            S.barrier()

        def tiles(n, with_ctx=True):
            res = []
            if with_ctx:
                res.append((0, NCTX, 1))
            for t0 in range(NCTX, T, n):
                res.append((t0, n, 0))
            return res

        def phase_prep():
            with ExitStack() as ph:
                def sb(name, shape, dt):
                    return Tl(ph.enter_context(nc.sbuf_tensor(uname(name), list(shape), dt)))

                rows = sb("rows", [128, 128], F32)
                cT = sb("cT", [128, 16], F32)
                sT = sb("sT", [128, 16], F32)
                bT = [sb(f"bT{l}", [128, 72], F32) for l in range(2)]
                gT = [sb(f"gT{l}", [128, 24], F32) for l in range(2)]
                tmpv = sb("tmpv", [128, 4], F32)

                def loadT(dst_ap, dst_buf, src_rows, R, extra_src=None):
                    S.dma("sp", rows.t[0:R, :], src_rows, writes=[rows.b])
                    p = ps_ring.next()
                    mm(p.t[:, 0:R], rows.t[0:R, :], ident.t[0:R, 0:R], True, True, [rows.b, ident.b], [p.b])
                    S.op("dve", lambda e: e.tensor_copy(dst_ap, p.t[:, 0:R]), reads=[p.b], writes=[dst_buf])

                loadT(cT.t[:], cT.b, cc_d, 16)
                S.op("act", lambda e: e.activation(out=sT.t[:], in_=cT.t[:], func=AF.Silu), reads=[cT.b], writes=[sT.b])
                for l in range(2):
                    loadT(bT[l].t[:], bT[l].b, b_mod[l], 72)
                    loadT(gT[l].t[:], gT[l].b, norm_g[l], 24)
                    loadT(GLG[l].t[:], GLG[l].b, gla_g[l], 2)
                    loadT(tmpv.t[:, 0:1], tmpv.b, gqa_qg[l], 1)
                    S.op("dve", lambda e: e.tensor_scalar(GQG[l].t[:, 0:1], tmpv.t[:, 0:1], 128.0 ** -0.5, 0.0, ALU.mult, ALU.add),
                         reads=[tmpv.b], writes=[GQG[l].b])
                    loadT(GQG[l].t[:, 1:2], GQG[l].b, gqa_kg[l], 1)
                    S.dma("sp", rows.t[0:1, 0:64], nat_qg[l], writes=[rows.b])
                    S.dma("sp", rows.t[0:1, 64:128], nat_qg[l], writes=[rows.b])
                    p = ps_ring.next()
                    mm(p.t[:, 0:1], rows.t[0:1, :], ident.t[0:1, 0:1], True, True, [rows.b, ident.b], [p.b])
                    S.op("dve", lambda e: e.tensor_scalar(NG[l].t[:, 0:1], p.t[:, 0:1], 0.125, 0.0, ALU.mult, ALU.add),
                         reads=[p.b], writes=[NG[l].b])
                    S.dma("sp", rows.t[0:1, 0:64], nat_kg[l], writes=[rows.b])
                    S.dma("sp", rows.t[0:1, 64:128], nat_kg[l], writes=[rows.b])
                    p = ps_ring.next()
                    mm(p.t[:, 0:1], rows.t[0:1, :], ident.t[0:1, 0:1], True, True, [rows.b, ident.b], [p.b])
                    S.op("dve", lambda e: e.tensor_copy(NG[l].t[:, 1:2], p.t[:, 0:1]), reads=[p.b], writes=[NG[l].b])

                sTb = sb("sTb", [128, 16], BF16)
                S.op("dve", lambda e: e.tensor_copy(sTb.t[:], sT.t[:]), reads=[sT.b], writes=[sTb.b])
                wm_ring = Ring([sb(f"wm{i}", [128, 8, 1152], BF16) for i in range(3)])
                for l in range(2):
                    pm = ps_ring.next()
                    src = w_mod[l].rearrange("(k p) n -> p k n", p=128)
                    for g in range(8):
                        wm = wm_ring.next()
                        S.dma("pool", wm.t[:], src[:, :, g * 1152:(g + 1) * 1152], writes=[wm.b])
                        for cc in range(9):
                            nn = g * 9 + cc
                            for k in range(8):
                                mm(pm.t[:, nn * 2:nn * 2 + 2], wm.t[:, k, cc * 128:(cc + 1) * 128],
                                   sTb.t[:].rearrange("p (a k) -> p k a", a=2)[:, k, :], k == 0, k == 7,
                                   [wm.b, sTb.b], [pm.b])
                    for kind in range(2):
                        der = DER[l][kind]
                        S.op("dve", lambda e: e.tensor_tensor(
                            der.t[:].rearrange("p j c -> p (j c)"),
                            pm.t[:, 0:144].rearrange("p (n a) -> p n a", a=2)[:, :, kind],
                            bT[l].t[:], ALU.add), reads=[pm.b, bT[l].b], writes=[der.b])
                        for s in range(3):
                            S.op("dve", lambda e: e.scalar_tensor_tensor(
                                der.t[:, 3 * s + 1, :], der.t[:, 3 * s + 1, :], 1.0, gT[l].t[:, s * 8:(s + 1) * 8],
                                ALU.add, ALU.mult), reads=[der.b, gT[l].b], writes=[der.b])
                            if s != 1:
                                S.op("dve", lambda e: e.tensor_scalar(der.t[:, 3 * s + 2, :], der.t[:, 3 * s + 2, :], 0.5, 0.0,
                                                                     ALU.mult, ALU.add), reads=[der.b], writes=[der.b])

                xin_ring = Ring([sb(f"xin{i}", [128, D], F32) for i in range(3)])
                xst_ring = Ring([sb(f"xst{i}", [128, 8, 512], F32) for i in range(2)])
                groups = [(0, 2)] + [(2 + 4 * g, 4) for g in range(8)]
                ev = 0
                for (c0, ncn) in groups:
                    st = xst_ring.next()
                    for j in range(ncn):
                        ch = c0 + j
                        xi = xin_ring.next()
                        srcx = ctx_d[ch * 128:(ch + 1) * 128, :] if ch < 2 else x_d[(ch - 2) * 128:(ch - 1) * 128, :]
                        S.dma("sp", xi.t[:], srcx, writes=[xi.b])
                        for half in range(2):
                            p = ps_ring.next()
                            for q in range(4):
                                c = half * 4 + q
                                S.op("pe", lambda e: e.transpose(p.t[:, q * 128:(q + 1) * 128], xi.t[:, c * 128:(c + 1) * 128],
                                                                 ident.t[:]), reads=[xi.b, ident.b], writes=[p.b])
                            eng = "act" if ev % 2 == 0 else "dve"
                            ev += 1
                            dst = st.t[:, half * 4:(half + 1) * 4, j * 128:(j + 1) * 128]
                            srcp = p.t[:].rearrange("p (q t) -> p q t", q=4)
                            if eng == "act":
                                S.op("act", lambda e: e.copy(dst, srcp), reads=[p.b], writes=[st.b])
                            else:
                                S.op("dve", lambda e: e.tensor_copy(dst, srcp), reads=[p.b], writes=[st.b])
                    n = ncn * 128
                    S.dma("sp", XT.rearrange("(c p) t -> p c t", p=128)[:, :, c0 * 128:c0 * 128 + n], st.t[:, :, 0:n],
                          reads=[st.b])
            phase_end("prep")

        def norm_mod(l, kind, s, xt, n, sq, rstd_ring, tmp_ring, u_ap_fn, u_buf, aff="act", part="all"):
            der = DER[l][kind]
            if part in ("all", "sq"):
                S.op("act", lambda e: e.activation(out=sq.t[:, :, 0:n], in_=xt.t[:, :, 0:n], func=AF.Square),
                     reads=[xt.b], writes=[sq.b])
            if part == "sq":
                return
            p = ps_ring.next()
            for c in range(8):
                mm(p.t[:, 0:n], onesD.t[:], sq.t[:, c, 0:n], c == 0, c == 7, [onesD.b, sq.b], [p.b])
            rstd = rstd_ring.next()
            S.op("act", lambda e: e.activation(out=rstd.t[:, 0:n], in_=p.t[:, 0:n], func=AF.Ln, bias=EPS),
                 reads=[p.b], writes=[rstd.b])
            S.op("act", lambda e: e.activation(out=rstd.t[:, 0:n], in_=rstd.t[:, 0:n], func=AF.Exp, scale=-0.5),
                 reads=[rstd.b], writes=[rstd.b])
            for c in range(8):
                tmp = tmp_ring.next()
                S.op("dve", lambda e: e.scalar_tensor_tensor(tmp.t[:, 0:n], xt.t[:, c, 0:n], der.t[:, 3 * s + 1, c:c + 1],
                                                             rstd.t[:, 0:n], ALU.mult, ALU.mult),
                     reads=[xt.b, der.b, rstd.b], writes=[tmp.b])
                if aff == "act":
                    S.op("act", lambda e: e.activation(out=u_ap_fn(c), in_=tmp.t[:, 0:n], func=AF.Identity,
                                                       bias=der.t[:, 3 * s, c:c + 1]),
                         reads=[tmp.b, der.b], writes=[u_buf])
                else:
                    S.op("pool", lambda e: e.tensor_scalar(u_ap_fn(c), tmp.t[:, 0:n], 1.0, der.t[:, 3 * s, c:c + 1], ALU.mult, ALU.add),
                         reads=[tmp.b, der.b], writes=[u_buf])

        XTv = XT.rearrange("(c p) t -> p c t", p=128)

        def phase_ffn(l, i, with_ctx, name):
            NT = 256
            s = 0 if i == 0 else 2
            with ExitStack() as ph:
                def sb(nm, shape, dt):
                    return Tl(ph.enter_context(nc.sbuf_tensor(uname(nm), list(shape), dt)))
                WU = sb("WU", [128, 8, 2 * DFF], BF16)
                WD = sb("WD", [128, 22, D], BF16)
                BWU = [Buf() for _ in range(11)]
                BWD = [Buf() for _ in range(11)]
                srcu = ffn_wi[l, i].rearrange("(k p) n -> p k n", p=128)
                srcd = ffn_wo[l, i].rearrange("(k p) n -> p k n", p=128)
                order = [0, 5, 1, 6, 2, 7, 3, 8, 4, 9, 10]
                for r in order:
                    S.dma("pool", WU.t[:, :, r * 512:(r + 1) * 512], srcu[:, :, r * 512:(r + 1) * 512], writes=[BWU[r]])
                for r in range(11):
                    S.dma("pool", WD.t[:, 2 * r:2 * r + 2, :], srcd[:, 2 * r:2 * r + 2, :], writes=[BWD[r]])
                xt_ring = Ring([sb(f"xt{j}", [128, 8, NT], F32) for j in range(2)])
                sq = sb("sq", [128, 8, NT], BF16)
                rstd_ring = Ring([sb(f"rstd{j}", [128, NT], F32) for j in range(2)])
                tmp_ring = Ring([sb(f"tmp{j}", [128, NT], F32) for j in range(2)])
                u_ring = Ring([sb(f"u{j}", [128, 8, NT], BF16) for j in range(2)])
                sl_ring = Ring([sb(f"sl{j}", [128, NT], F32) for j in range(3)])
                g_ring = Ring([sb(f"g{j}", [128, 22, NT], BF16) for j in range(2)])
                u2 = sb("u2", [128, 8, NT], BF16) if i == 0 else None
                tmp_ring2 = Ring([sb(f"tmpb{j}", [128, NT], F32) for j in range(2)])
                UTDv = UTD.rearrange("(c p) t -> p c t", p=128)

                def do_norm2(ti, part):
                    t0, n, kind = tl[ti]
                    norm_mod(l, kind, 1, xts[ti], n, sq, rstd_ring, tmp_ring2, lambda c: u2.t[:, c, 0:n], u2.b, aff="pool", part=part)
                    if part != "sq":
                        S.dma("sp", UTDv[:, :, t0:t0 + n], u2.t[:, :, 0:n], reads=[u2.b])
                tl = tiles(NT, with_ctx)

                def load(ti):
                    t0, n, kind = tl[ti]
                    xt = xt_ring.next()
                    S.dma("sp", xt.t[:, :, 0:n], XTv[:, :, t0:t0 + n], writes=[xt.b])
                    return xt
                xts = [None] * len(tl)
                us = [None] * len(tl)
                xts[0] = load(0)

                def do_norm(ti, part):
                    t0, n, kind = tl[ti]
                    if part != "rest":
                        us[ti] = u_ring.next()
                    u = us[ti]
                    norm_mod(l, kind, s, xts[ti], n, sq, rstd_ring, tmp_ring, lambda c: u.t[:, c, 0:n], u.b, part=part)
                do_norm(0, "all")
                for ti, (t0, n, kind) in enumerate(tl):
                    xt = xts[ti]
                    u = us[ti]
                    g = g_ring.next()
                    for j in range(22):
                        if j == 2 and i == 0 and ti > 0:
                            do_norm2(ti - 1, "sq")
                        if j == 6:
                            if i == 0 and ti > 0:
                                do_norm2(ti - 1, "rest")
                            if ti + 1 < len(tl):
                                xts[ti + 1] = load(ti + 1)
                        if j == 15 and ti + 1 < len(tl):
                            do_norm(ti + 1, "sq")
                        p = ps_ring.next()
                        ra = (j * 128) // 512
                        rb = (DFF + j * 128) // 512
                        for k in range(8):
                            mm(p.t[:, 0:n], WU.t[:, k, j * 128:(j + 1) * 128], u.t[:, k, 0:n], k == 0, k == 7,
                               [BWU[ra], u.b], [p.b])
                        for k in range(8):
                            mm(p.t[:, 256:256 + n], WU.t[:, k, DFF + j * 128:DFF + (j + 1) * 128], u.t[:, k, 0:n], k == 0, k == 7,
                               [BWU[rb], u.b], [p.b])
                        sl = sl_ring.next()
                        S.op("act", lambda e: e.activation(out=sl.t[:, 0:n], in_=p.t[:, 0:n], func=AF.Silu),
                             reads=[p.b], writes=[sl.b])
                        S.op("dve", lambda e: e.tensor_tensor(g.t[:, j, 0:n], sl.t[:, 0:n], p.t[:, 256:256 + n], ALU.mult),
                             reads=[sl.b, p.b], writes=[g.b])
                    if ti + 1 < len(tl):
                        do_norm(ti + 1, "rest")
                    der = DER[l][kind]
                    for nn in range(8):
                        p = ps_ring.next()
                        for j in range(22):
                            mm(p.t[:, 0:n], WD.t[:, j, nn * 128:(nn + 1) * 128], g.t[:, j, 0:n], j == 0, j == 21,
                               [BWD[j // 2], g.b], [p.b])
                        S.op("dve", lambda e: e.scalar_tensor_tensor(xt.t[:, nn, 0:n], p.t[:, 0:n], der.t[:, 3 * s + 2, nn:nn + 1],
                                                                     xt.t[:, nn, 0:n], ALU.mult, ALU.add),
                             reads=[p.b, der.b, xt.b], writes=[xt.b])
                    S.dma("sp", XTv[:, :, t0:t0 + n], xt.t[:, :, 0:n], reads=[xt.b])
                if i == 0:
                    do_norm2(len(tl) - 1, "all")
            phase_end(name)

        def phase_inproj(l):
            tl = tiles(512, True)
            with ExitStack() as ph:
                def sb(nm, shape, dt):
                    return Tl(ph.enter_context(nc.sbuf_tensor(uname(nm), list(shape), dt)))
                UT = sb("UT", [128, 8, T], BF16)
                BUT = [Buf() for _ in tl]
                UTDv = UTD.rearrange("(c p) t -> p c t", p=128)
                for ti, (t0, n, kind) in enumerate(tl):
                    S.dma("sp", UT.t[:, :, t0:t0 + n], UTDv[:, :, t0:t0 + n], writes=[BUT[ti]])
                w_ring = Ring([sb(f"iw{j}", [128, 8, 1024], BF16) for j in range(2)])
                FG = [sb(f"fg{d}", [16, T], F32) for d in range(2)]
                W2 = [sb(f"w2{d}", [16, 512], F32) for d in range(2)]
                B2 = [sb(f"b2{d}", [1, 512], F32) for d in range(2)]
                st_ring = Ring([sb(f"ist{j}", [128, 512], BF16) for j in range(4)])
                sqb_ring = Ring([sb(f"isqb{j}", [128, 512], BF16) for j in range(2)])
                rs_ring = Ring([sb(f"irs{j}", [128, 512], F32) for j in range(2)])
                qn_ring = Ring([sb(f"iqn{j}", [128, 512], BF16) for j in range(2)])
                t1_ring = Ring([sb(f"it1{j}", [128, 512], F32) for j in range(2)])
                t2_ring = Ring([sb(f"it2{j}", [128, 512], F32) for j in range(2)])
                rc_ring = Ring([sb(f"irc{j}", [128, 512], F32) for j in range(2)])
                rsn_ring = Ring([sb(f"irsn{j}", [128, 512], F32) for j in range(2)])
                e_ring = Ring([sb(f"ie{j}", [128, 512], F32) for j in range(2)])
                sp_ring = Ring([sb(f"isp{j}", [128, 512], F32) for j in range(2)])
                wsrc = w_in[l].rearrange("(k p) n -> p k n", p=128)
                for d in range(2):
                    S.dma("sp", W2[d].t[:], fg_w2[l, d], writes=[W2[d].b])
                    S.dma("sp", B2[d].t[:], fg_b[l, d], writes=[B2[d].b])
                ev = [0]

                def evac(dst, src, reads, writes):
                    if ev[0] % 2 == 0:
                        S.op("act", lambda e: e.copy(dst, src), reads=reads, writes=writes)
                    else:
                        S.op("dve", lambda e: e.tensor_copy(dst, src), reads=reads, writes=writes)
                    ev[0] += 1

                def loadw(col0, ncols):
                    W = w_ring.next()
                    S.dma("pool", W.t[:, :, 0:ncols], wsrc[:, :, col0:col0 + ncols], writes=[W.b])
                    return W

                def fm_group(col0, nchunks, epi, m=128):
                    W = loadw(col0, nchunks * m)
                    for ti, (t0, n, kind) in enumerate(tl):
                        for ch in range(nchunks):
                            p = ps_ring.next()
                            for k in range(8):
                                mm(p.t[0:m, 0:n], W.t[:, k, ch * m:(ch + 1) * m], UT.t[:, k, t0:t0 + n], k == 0, k == 7,
                                   [W.b, BUT[ti]], [p.b])
                            epi(ch, p, t0, n, ti)

                def store_fm(dst, ch, st, t0, n):
                    S.dma("sp", dst[ch * 128:(ch + 1) * 128, t0:t0 + n], st.t[:, 0:n], reads=[st.b])

                def epi_copy(dst):
                    def f(ch, p, t0, n, ti):
                        st = st_ring.next()
                        evac(st.t[:, 0:n], p.t[:, 0:n], [p.b], [st.b])
                        store_fm(dst, ch, st, t0, n)
                    return f

                def epi_act(dst, func):
                    def f(ch, p, t0, n, ti):
                        st = st_ring.next()
                        S.op("act", lambda e: e.activation(out=st.t[:, 0:n], in_=p.t[:, 0:n], func=func), reads=[p.b], writes=[st.b])
                        store_fm(dst, ch, st, t0, n)
                    return f

                pA_ring = Ring([PS[0], PS[1], PS[2], PS[3]])
                pB_ring = Ring([PS[4], PS[5]])
                pC_ring = Ring([PS[6], PS[7]])
                qn_ring3 = Ring(qn_ring.items + [sb("iqn2", [128, 512], BF16)])

                def fm_group_pipe(col0, nchunks, stages):
                    W = loadw(col0, nchunks * 128)
                    items = []
                    for ti, (t0, n, kind) in enumerate(tl):
                        for ch in range(nchunks):
                            items.append({"ch": ch, "t0": t0, "n": n, "ti": ti})

                    def st0(it):
                        p = pA_ring.next()
                        n, t0 = it["n"], it["t0"]
                        for k in range(8):
                            mm(p.t[:, 0:n], W.t[:, k, it["ch"] * 128:(it["ch"] + 1) * 128], UT.t[:, k, t0:t0 + n], k == 0, k == 7,
                               [W.b, BUT[it["ti"]]], [p.b])
                        it["p"] = p
                    allst = [st0] + stages
                    ns = len(allst)
                    for step in range(len(items) + ns - 1):
                        for si in range(ns - 1, -1, -1):
                            i = step - si
                            if 0 <= i < len(items):
                                allst[si](items[i])

                def st_stats(ones_t):
                    def f(it):
                        p, n = it["p"], it["n"]
                        sqb = sqb_ring.next()
                        S.op("act", lambda e: e.activation(out=sqb.t[:, 0:n], in_=p.t[:, 0:n], func=AF.Square), reads=[p.b], writes=[sqb.b])
                        p2 = pB_ring.next()
                        mm(p2.t[:, 0:n], ones_t.t[:], sqb.t[:, 0:n], True, True, [ones_t.b, sqb.b], [p2.b])
                        it["p2"] = p2
                    return f

                def st_norm(gain_t, col, out_ring, dst=None, perm=False):
                    def f(it):
                        p, p2, n = it["p"], it["p2"], it["n"]
                        rs = rs_ring.next()
                        S.op("act", lambda e: e.activation(out=rs.t[:, 0:n], in_=p2.t[:, 0:n], func=AF.Ln, bias=EPS), reads=[p2.b], writes=[rs.b])
                        S.op("act", lambda e: e.activation(out=rs.t[:, 0:n], in_=rs.t[:, 0:n], func=AF.Exp, scale=-0.5), reads=[rs.b], writes=[rs.b])
                        o = out_ring.next()
                        S.op("dve", lambda e: e.scalar_tensor_tensor(o.t[:, 0:n], p.t[:, 0:n], gain_t.t[:, col:col + 1], rs.t[:, 0:n], ALU.mult, ALU.mult),
                             reads=[p.b, rs.b, gain_t.b], writes=[o.b])
                        it["o"] = o
                        if perm:
                            p3 = pC_ring.next()
                            mm(p3.t[:, 0:n], permb.t[:], o.t[:, 0:n], True, True, [permb.b, o.b], [p3.b])
                            it["p3"] = p3
                        else:
                            store_fm(dst, it["ch"], o, it["t0"], n)
                    return f

                rope_tiles = {}

                def get_rope(ti, t0, n):
                    if ti not in rope_tiles:
                        rc = rc_ring.next()
                        rsn = rsn_ring.next()
                        S.dma("sp", rc.t[:, 0:n], ropec[:, t0:t0 + n], writes=[rc.b])
                        S.dma("sp", rsn.t[:, 0:n], ropes[:, t0:t0 + n], writes=[rsn.b])
                        rope_tiles.clear()
                        rope_tiles[ti] = (rc, rsn)
                    return rope_tiles[ti]

                def st_rope(dst):
                    def f(it):
                        qn, p3, n, t0 = it["o"], it["p3"], it["n"], it["t0"]
                        rc, rsn = get_rope(it["ti"], t0, n)
                        t1 = t1_ring.next()
                        t2 = t2_ring.next()
                        S.op("pool", lambda e: e.tensor_tensor(t1.t[:, 0:n], qn.t[:, 0:n], rc.t[:, 0:n], ALU.mult),
                             reads=[qn.b, rc.b], writes=[t1.b])
                        S.op("dve", lambda e: e.tensor_tensor(t2.t[:, 0:n], p3.t[:, 0:n], rsn.t[:, 0:n], ALU.mult),
                             reads=[p3.b, rsn.b], writes=[t2.b])
                        st = st_ring.next()
                        S.op("dve", lambda e: e.tensor_tensor(st.t[:, 0:n], t1.t[:, 0:n], t2.t[:, 0:n], ALU.add),
                             reads=[t1.b, t2.b], writes=[st.b])
                        store_fm(dst, it["ch"], st, t0, n)
                    return f

                def tm_group(col0, ncols, dst):
                    W = loadw(col0, ncols)
                    w = min(512, ncols)
                    for cidx in range(NCH):
                        ti = 0 if cidx < 2 else 1 + (cidx - 2) // 4
                        for piece in range(ncols // w):
                            p = ps_ring.next()
                            for k in range(8):
                                mm(p.t[:, 0:w], UT.t[:, k, cidx * 128:(cidx + 1) * 128], W.t[:, k, piece * w:(piece + 1) * w],
                                   k == 0, k == 7, [W.b, BUT[ti]], [p.b])
                            st = st_ring.next()
                            evac(st.t[:, 0:w], p.t[:, 0:w], [p.b], [st.b])
                            S.dma("sp", dst[cidx * 128:(cidx + 1) * 128, piece * w:(piece + 1) * w], st.t[:, 0:w], reads=[st.b])

                fm_group(O_GQ, 4, epi_copy(GQT))
                fm_group(O_GK, 4, epi_copy(GKT))
                tm_group(O_GK, 512, GKt)
                tm_group(O_GV, 1024, GVt)
                fm_group(O_GG, 8, epi_act(GGT, AF.Silu))

                def epi_fg(ch, p, t0, n, ti):
                    evac(FG[ch].t[0:16, t0:t0 + n], p.t[0:16, 0:n], [p.b], [FG[ch].b])
                fm_group(O_FG, 2, epi_fg, m=16)
                for cidx in range(NCH):
                    for d in range(2):
                        p = ps_ring.next()
                        mm(p.t[:, :], FG[d].t[0:16, cidx * 128:(cidx + 1) * 128], W2[d].t[:], True, False, [FG[d].b, W2[d].b], [p.b])
                        mm(p.t[:, :], onesf.t[0:1, :], B2[d].t[:], False, True, [onesf.b, B2[d].b], [p.b])
                        ee = e_ring.next()
                        S.op("act", lambda e: e.activation(out=ee.t[:], in_=p.t[:], func=AF.Exp, scale=-1.0), reads=[p.b], writes=[ee.b])
                        spt = sp_ring.next()
                        S.op("act", lambda e: e.activation(out=spt.t[:], in_=ee.t[:], func=AF.Ln, bias=1.0), reads=[ee.b], writes=[spt.b])
                        S.dma("sp", (SPF if d == 0 else SPB)[cidx * 128:(cidx + 1) * 128, :], spt.t[:], reads=[spt.b])
                fm_group_pipe(O_NQ, 8, [st_stats(blk64), st_norm(NG[l], 0, st_ring, dst=NQT)])
                fm_group_pipe(O_NK, 8, [st_stats(blk64), st_norm(NG[l], 1, st_ring, dst=NKT)])
                tm_group(O_NV, 1024, NVt)
                fm_group_pipe(O_QQ, 8, [st_stats(ones128), st_norm(GQG[l], 0, qn_ring3, perm=True), st_rope(QQT)])
                fm_group_pipe(O_QK, 2, [st_stats(ones128), st_norm(GQG[l], 1, qn_ring3, perm=True), st_rope(QKT)])
                tm_group(O_QV, 256, QVt)
                for i in range(3):
                    fm_group(O_G0 + i * 1024, 8, epi_act(SGT[i], AF.Sigmoid))
            phase_end(f"inproj{l}")

        def phase_gqa(l):
            with ExitStack() as ph:
                def sb(nm, shape, dt):
                    return Tl(ph.enter_context(nc.sbuf_tensor(uname(nm), list(shape), dt)))
                KT = sb("qKT", [128, 2, T], BF16)
                V = sb("qV", [128, NCH, 256], BF16)
                S.dma("sp", KT.t[:], QKT.rearrange("(g p) t -> p g t", p=128), writes=[KT.b])
                S.dma("sp", V.t[:], QVt.rearrange("(s p) c -> p s c", p=128), writes=[V.b])
                q_ring = Ring([sb(f"qq{j}", [128, 512], BF16) for j in range(3)])
                pt_ring = Ring([sb(f"qpt{j}", [128, 512], BF16) for j in range(4)])
                rd_ring = Ring([sb(f"qrd{j}", [128, 512], F32) for j in range(2)])
                yo_ring = Ring([sb(f"qyo{j}", [128, 512], BF16) for j in range(3)])
                acc_ring = Ring([(PS[0], PS[1]), (PS[2], PS[3])])
                s_ring = Ring([PS[4], PS[5], PS[6], PS[7]])
                jobs = []
                for h in range(8):
                    if l == 0:
                        jobs.append((h, 0, NCTX, 2))
                    for t0 in range(NCTX, T, 512):
                        jobs.append((h, t0, 512, NCH))
                LOOK = 2
                items = []
                for ji, (h, t0, n, nkc) in enumerate(jobs):
                    for sc in range(nkc):
                        items.append((ji, sc))
                jst = {}
                pend = []

                def qk(ji, sc):
                    h, t0, n, nkc = jobs[ji]
                    g = h // 4
                    if sc == 0:
                        q = q_ring.next()
                        S.dma("sp", q.t[:, 0:n], QQT[h * 128:(h + 1) * 128, t0:t0 + n], writes=[q.b])
                        jst[ji] = {"q": q}
                    q = jst[ji]["q"]
                    ps_ = s_ring.next()
                    mm(ps_.t[:, 0:n], KT.t[:, g, sc * 128:(sc + 1) * 128], q.t[:, 0:n], True, True, [KT.b, q.b], [ps_.b])
                    return ps_

                def proc(ji, sc, ps_):
                    h, t0, n, nkc = jobs[ji]
                    g = h // 4
                    st = jst[ji]
                    if sc == 0:
                        st["po"], st["pd"] = acc_ring.next()
                    po, pd = st["po"], st["pd"]
                    pt = pt_ring.next()
                    S.op("act", lambda e: e.activation(out=pt.t[:, 0:n], in_=ps_.t[:, 0:n], func=AF.Exp), reads=[ps_.b], writes=[pt.b])
                    mm(po.t[:, 0:n], V.t[:, sc, g * 128:(g + 1) * 128], pt.t[:, 0:n], sc == 0, sc == nkc - 1, [V.b, pt.b], [po.b])
                    mm(pd.t[:, 0:n], ones1.t[:], pt.t[:, 0:n], sc == 0, sc == nkc - 1, [ones1.b, pt.b], [pd.b])
                    if sc == nkc - 1:
                        rd = rd_ring.next()
                        S.op("dve", lambda e: e.reciprocal(rd.t[:, 0:n], pd.t[:, 0:n]), reads=[pd.b], writes=[rd.b])
                        yo = yo_ring.next()
                        S.op("dve", lambda e: e.tensor_tensor(yo.t[:, 0:n], po.t[:, 0:n], rd.t[:, 0:n], ALU.mult), reads=[po.b, rd.b], writes=[yo.b])
                        S.dma("sp", YT[2][h * 128:(h + 1) * 128, t0:t0 + n], yo.t[:, 0:n], reads=[yo.b])
                        del jst[ji]

                for (ji, sc) in items:
                    pend.append((ji, sc, qk(ji, sc)))
                    if len(pend) > LOOK:
                        proc(*pend.pop(0))
                while pend:
                    proc(*pend.pop(0))
            phase_end(f"gqa{l}")

        def phase_nat(l):
            with ExitStack() as ph:
                def sb(nm, shape, dt):
                    return Tl(ph.enter_context(nc.sbuf_tensor(uname(nm), list(shape), dt)))
                KT = sb("nKT", [128, 4, T], BF16)
                QA = sb("nQA", [128, 4, T], BF16)
                QB = sb("nQB", [128, 4, T], BF16)
                S.op("pool", lambda e: e.memset(QA.t[64:128, :, :], 0.0), writes=[QA.b])
                S.op("pool", lambda e: e.memset(QB.t[0:64, :, :], 0.0), writes=[QB.b])
                V = sb("nV", [128, NCH, 512], BF16)
                TB = sb("nTB", [128, NSLOT, 512], BF16)
                identb = sb("nident", [128, 128], BF16)
                S.dma("pool", identb.t[:], ident_d, writes=[identb.b])
                MSK = sb("nMSK", [128, NSLOT, 64], F32)
                S.dma("sp", MSK.t[:], natm.rearrange("p (s q) -> p s q", q=64), writes=[MSK.b])
                sc_ring = Ring([sb(f"nsc{j}", [128, 512], F32) for j in range(2)])
                pt_ring = Ring([sb(f"npt{j}", [128, 512], BF16) for j in range(4)])
                rd_ring = Ring([sb(f"nrd{j}", [128, 512], F32) for j in range(3)])
                stg_ring = Ring([sb(f"nstg{j}", [128, 4, 512], BF16) for j in range(2)])
                acc_ring = Ring([(PS[0], PS[1]), (PS[2], PS[3])])
                s_ring = Ring([PS[4], PS[5], PS[6], PS[7]])
                for hg in range(2):
                    S.dma("sp", KT.t[:], NKT[hg * 512:(hg + 1) * 512, :].rearrange("(c p) t -> p c t", p=128), writes=[KT.b])
                    qsrc = NQT[hg * 512:(hg + 1) * 512, :].rearrange("(c p) t -> p c t", p=128)
                    S.dma("sp", QA.t[0:64, :, :], qsrc[0:64], writes=[QA.b])
                    S.dma("sp", QB.t[64:128, :, :], qsrc[64:128], writes=[QB.b])
                    S.dma("sp", V.t[:], NVt[:, hg * 512:(hg + 1) * 512].rearrange("(s p) c -> p s c", p=128), writes=[V.b])
                    S.dma("pool", TB.t[:].rearrange("p s c -> p (s c)"), natb[l, hg], writes=[TB.b])
                    for sl in range(NSLOT):
                        S.op("dve", lambda e: e.tensor_tensor(
                            TB.t[:, sl, :].rearrange("p (h q) -> p h q", h=8), TB.t[:, sl, :].rearrange("p (h q) -> p h q", h=8),
                            MSK.t[:, sl, :].unsqueeze(1).broadcast_to([128, 8, 64]), ALU.add),
                            reads=[TB.b, MSK.b], writes=[TB.b])
                    blocks = []
                    if l == 0:
                        blocks.append((0, [(b * 64, [(0, None), (1, None)]) for b in range(4)]))
                    for rg in range(8):
                        rows_ = []
                        for r in range(rg * 8, rg * 8 + 8):
                            rs = min(max(r - 4, 0), 56)
                            ch = []
                            if rs % 2 == 0:
                                for j in range(4):
                                    m = rs // 2 + j
                                    ch.append((2 + m, 2 * m - r + 7))
                            else:
                                m0 = (rs - 1) // 2
                                for j in range(5):
                                    m = m0 + j
                                    dr0 = 2 * m - r
                                    slot = 14 if j == 0 else (15 if j == 4 else dr0 + 7)
                                    ch.append((2 + m, slot))
                            ch += [(0, None), (1, None)]
                            rows_.append((NCTX + r * 64, ch))
                        blocks.append((NCTX + rg * 512, rows_))
                    import os as _os
                    _lim = int(_os.environ.get("NAT_LIM", "99"))
                    LOOK = 3
                    jobs = []
                    for (bt0, rows_) in blocks[:_lim]:
                        for ri, (q0, chunks) in enumerate(rows_):
                            jobs.append((bt0, q0, chunks, ri == 0, ri == len(rows_) - 1, len(rows_) * 64))
                    items = [(ji, ci) for ji, jb in enumerate(jobs) for ci in range(len(jb[2]))]
                    jst = {}
                    cur_stg = [None]

                    def qk(ji, ci):
                        bt0, q0, chunks, first, lastrow, nb = jobs[ji]
                        chunk, slot = chunks[ci]
                        ps_ = s_ring.next()
                        if slot is not None:
                            S.op("pe", lambda e: e.matmul(ps_.t[:, :], identb.t[:], TB.t[:, slot, :], start=True, stop=False, skip_group_check=True),
                                 reads=[identb.b, TB.b], writes=[ps_.b])
                        for hh in range(8):
                            cc, half = hh // 2, hh % 2
                            Qh = QA if half == 0 else QB
                            S.op("pe", lambda e: e.matmul(ps_.t[:, hh * 64:(hh + 1) * 64], KT.t[:, cc, chunk * 128:(chunk + 1) * 128],
                                                          Qh.t[:, cc, q0:q0 + 64], start=(slot is None), stop=True, skip_group_check=True),
                                 reads=[KT.b, Qh.b], writes=[ps_.b])
                        return ps_

                    def proc(ji, ci, ps_):
                        bt0, q0, chunks, first, lastrow, nb = jobs[ji]
                        chunk, slot = chunks[ci]
                        last = len(chunks) - 1
                        if ci == 0:
                            jst[ji] = acc_ring.next()
                            for dfr in list(deferred):
                                if dfr[2] is jst[ji][0]:
                                    dfr[1]()
                                    deferred.remove(dfr)
                            if first:
                                cur_stg[0] = stg_ring.next()
                        po, pd = jst[ji]
                        stg = cur_stg[0]
                        pt = pt_ring.next()
                        S.op("act", lambda e: e.activation(out=pt.t[:], in_=ps_.t[:], func=AF.Exp), reads=[ps_.b], writes=[pt.b])
                        for hh in range(8):
                            cc = hh // 2
                            S.op("pe", lambda e: e.matmul(po.t[:, hh * 64:(hh + 1) * 64], V.t[:, chunk, cc * 128:(cc + 1) * 128],
                                                          pt.t[:, hh * 64:(hh + 1) * 64], start=(ci == 0 and hh == 0),
                                                          stop=(ci == last and hh == 7), skip_group_check=True),
                                 reads=[V.b, pt.b], writes=[po.b])
                        mm(pd.t[:, :], ones1.t[:], pt.t[:], ci == 0, ci == last, [ones1.b, pt.b], [pd.b])
                        if ci == last:
                            rd = rd_ring.next()

                            def fin_a(pd=pd, rd=rd):
                                S.op("act", lambda e: e.activation(out=rd.t[:], in_=pd.t[:, :], func=AF.Ln), reads=[pd.b], writes=[rd.b])
                                S.op("act", lambda e: e.activation(out=rd.t[:], in_=rd.t[:], func=AF.Exp, scale=-1.0), reads=[rd.b], writes=[rd.b])

                            def fin_b(po=po, rd=rd, stg=stg, q0=q0, bt0=bt0, lastrow=lastrow, nb=nb):
                                o0 = q0 - bt0
                                for hf in range(2):
                                    S.op("dve", lambda e: e.tensor_tensor(
                                        stg.t[hf * 64:(hf + 1) * 64, :, o0:o0 + 64],
                                        po.t[hf * 64:(hf + 1) * 64, :].rearrange("p (c h q) -> p c h q", c=4, h=2)[:, :, hf, :],
                                        rd.t[hf * 64:(hf + 1) * 64, :].rearrange("p (c h q) -> p c h q", c=4, h=2)[:, :, hf, :], ALU.mult),
                                        reads=[po.b, rd.b], writes=[stg.b])
                                if lastrow:
                                    S.dma("sp", YT[1][hg * 512:(hg + 1) * 512, bt0:bt0 + nb].rearrange("(c p) t -> p c t", p=128),
                                          stg.t[:, :, 0:nb], reads=[stg.b])
                            deferred.append([2, fin_a, po])
                            deferred.append([4, fin_b, po])
                            del jst[ji]
                        for dfr in list(deferred):
                            dfr[0] -= 1
                            if dfr[0] < 0:
                                dfr[1]()
                                deferred.remove(dfr)

                    deferred = []
                    pend = []
                    for (ji, ci) in items:
                        pend.append((ji, ci, qk(ji, ci)))
                        if len(pend) > LOOK:
                            proc(*pend.pop(0))
                    while pend:
                        proc(*pend.pop(0))
                    for dfr in deferred:
                        dfr[1]()
            phase_end(f"nat{l}")

        def phase_gla(l):
            with ExitStack() as ph:
                def sb(nm, shape, dt):
                    return Tl(ph.enter_context(nc.sbuf_tensor(uname(nm), list(shape), dt)))
                TRI = sb("gTRI", [128, 4, 128], F32)
                MSKG = sb("gMSK", [128, 2, 128], F32)
                S.dma("sp", TRI.t[:], tri_d.rearrange("a p t -> p a t"), writes=[TRI.b])
                S.dma("sp", MSKG.t[:], msk_d.rearrange("a p t -> p a t"), writes=[MSKG.b])
                qT = sb("gqT", [128, T], BF16)
                kT = sb("gkT", [128, T], BF16)
                ktok = sb("gktok", [128, NCH, 128], BF16)
                vtok = sb("gvtok", [128, NCH, 256], BF16)
                sp_ = [sb(f"gsp{d}", [128, NCH, 128], F32) for d in range(2)]
                O = sb("gO", [128, 2, T], F32)
                st32 = [sb(f"gst32{d}", [128, 256], F32) for d in range(2)]
                st16 = [sb(f"gst16{d}", [128, 256], BF16) for d in range(2)]
                Ep_ring = Ring([sb(f"gEp{j}", [128, 128], F32) for j in range(7)])
                En_ring = Ring([sb(f"gEn{j}", [128, 128], F32) for j in range(3)])
                Dk_ring = Ring([sb(f"gDk{j}", [128, 128], F32) for j in range(3)])
                qe_ring = Ring([sb(f"gqe{j}", [128, 128], BF16) for j in range(6)])
                kin_ring = Ring([sb(f"gkin{j}", [128, 128], BF16) for j in range(4)])
                kend_ring = Ring([sb(f"gkend{j}", [128, 128], BF16) for j in range(4)])
                att_ring = Ring([sb(f"gatt{j}", [128, 128], BF16) for j in range(4)])
                fsq_ring = Ring([sb(f"gfsq{j}", [128, 2, 512], BF16) for j in range(2)])
                frs_ring = Ring([sb(f"gfrs{j}", [128, 512], F32) for j in range(2)])
                fgg_ring = Ring([sb(f"gfgg{j}", [128, 2, 512], BF16) for j in range(2)])
                ftm_ring = Ring([sb(f"gftm{j}", [128, 512], F32) for j in range(2)])
                fy_ring = Ring([sb(f"gfy{j}", [128, 2, 512], BF16) for j in range(2)])
                orders = [list(range(NCH)), [1, 0] + list(range(NCH - 1, 1, -1))]
                gd_ring = Ring([PS[0], PS[1]])
                as_ring = Ring([PS[2], PS[3], PS[4]])
                o_ring = Ring([PS[5], PS[6], PS[7]])
                for h in range(4):
                    S.dma("sp", qT.t[:], GQT[h * 128:(h + 1) * 128, :], writes=[qT.b])
                    S.dma("sp", kT.t[:], GKT[h * 128:(h + 1) * 128, :], writes=[kT.b])
                    S.dma("sp", ktok.t[:], GKt[:, h * 128:(h + 1) * 128].rearrange("(s p) c -> p s c", p=128), writes=[ktok.b])
                    S.dma("sp", vtok.t[:], GVt[:, h * 256:(h + 1) * 256].rearrange("(s p) c -> p s c", p=128), writes=[vtok.b])
                    S.dma("sp", sp_[0].t[:], SPF[:, h * 128:(h + 1) * 128].rearrange("(s p) c -> p s c", p=128), writes=[sp_[0].b])
                    S.dma("sp", sp_[1].t[:], SPB[:, h * 128:(h + 1) * 128].rearrange("(s p) c -> p s c", p=128), writes=[sp_[1].b])
                    S.op("pool", lambda e: e.memset(O.t[:], 0.0), writes=[O.b])
                    for d in range(2):
                        S.op("pool", lambda e: e.memset(st32[d].t[:], 0.0), writes=[st32[d].b])
                        S.op("pool", lambda e: e.memset(st16[d].t[:], 0.0), writes=[st16[d].b])
                    items = []
                    for step in range(NCH):
                        for d in range(2):
                            items.append({"d": d, "c": orders[d][step]})

                    def g0(it):
                        d, c = it["d"], it["c"]
                        spc = sp_[d].t[:, c, :]
                        gd = gd_ring.next()
                        mm(gd.t[:, 0:128], spc, TRI.t[:, d, :], True, True, [sp_[d].b, TRI.b], [gd.b])
                        mm(gd.t[:, 128:256], TRI.t[:, 2 + d, :], spc, True, True, [sp_[d].b, TRI.b], [gd.b])
                        it["gd"] = gd

                    def g1(it):
                        gd = it["gd"]
                        Ep, En, Dk = Ep_ring.next(), En_ring.next(), Dk_ring.next()
                        S.op("act", lambda e: e.activation(out=Ep.t[:], in_=gd.t[:, 0:128], func=AF.Exp), reads=[gd.b], writes=[Ep.b])
                        S.op("act", lambda e: e.activation(out=En.t[:], in_=gd.t[:, 0:128], func=AF.Exp, scale=-1.0), reads=[gd.b], writes=[En.b])
                        S.op("act", lambda e: e.activation(out=Dk.t[:], in_=gd.t[:, 128:256], func=AF.Exp), reads=[gd.b], writes=[Dk.b])
                        it.update(Ep=Ep, En=En, Dk=Dk)

                    def g2(it):
                        c, Ep, En, Dk = it["c"], it["Ep"], it["En"], it["Dk"]
                        tk = slice(c * 128, (c + 1) * 128)
                        qe, kin, kend = qe_ring.next(), kin_ring.next(), kend_ring.next()
                        S.op("dve", lambda e: e.scalar_tensor_tensor(qe.t[:], qT.t[:, tk], 128.0 ** -0.5, Ep.t[:], ALU.mult, ALU.mult),
                             reads=[qT.b, Ep.b], writes=[qe.b])
                        S.op("pool", lambda e: e.tensor_tensor(kin.t[:], kT.t[:, tk], En.t[:], ALU.mult), reads=[kT.b, En.b], writes=[kin.b])
                        S.op("pool", lambda e: e.tensor_tensor(kend.t[:], ktok.t[:, c, :], Dk.t[:], ALU.mult), reads=[ktok.b, Dk.b], writes=[kend.b])
                        it.update(qe=qe, kin=kin, kend=kend)

                    def g3(it):
                        c = it["c"]
                        a_s = as_ring.next()
                        mm(a_s.t[:, 0:128], it["kin"].t[:], it["qe"].t[:], True, True, [it["kin"].b, it["qe"].b], [a_s.b])
                        mm(a_s.t[:, 128:384], it["kend"].t[:], vtok.t[:, c, :], True, True, [it["kend"].b, vtok.b], [a_s.b])
                        it["as"] = a_s

                    def g4(it):
                        d, a_s, Ep = it["d"], it["as"], it["Ep"]
                        att = att_ring.next()
                        S.op("dve", lambda e: e.tensor_tensor(att.t[:], a_s.t[:, 0:128], MSKG.t[:, d, :], ALU.mult), reads=[a_s.b, MSKG.b], writes=[att.b])
                        it["att"] = att

                    def g5(it):
                        d, c, qe, att, a_s, Ep = it["d"], it["c"], it["qe"], it["att"], it["as"], it["Ep"]
                        pO = o_ring.next()
                        for vc in range(2):
                            mm(pO.t[:, vc * 128:(vc + 1) * 128], vtok.t[:, c, vc * 128:(vc + 1) * 128], att.t[:], True, False,
                               [vtok.b, att.b], [pO.b])
                            mm(pO.t[:, vc * 128:(vc + 1) * 128], st16[d].t[:, vc * 128:(vc + 1) * 128], qe.t[:], False, True,
                               [st16[d].b, qe.b], [pO.b])
                        it["pO"] = pO
                        egl = Ep.t[:, 127:128] if d == 0 else Ep.t[:, 0:1]
                        S.op("dve", lambda e: e.scalar_tensor_tensor(st32[d].t[:], st32[d].t[:], egl, a_s.t[:, 128:384], ALU.mult, ALU.add),
                             reads=[st32[d].b, Ep.b, a_s.b], writes=[st32[d].b])

                    def g6(it):
                        d, c, pO = it["d"], it["c"], it["pO"]
                        tk = slice(c * 128, (c + 1) * 128)
                        S.op("act", lambda e: e.copy(st16[d].t[:], st32[d].t[:]), reads=[st32[d].b], writes=[st16[d].b])
                        S.op("dve", lambda e: e.tensor_tensor(O.t[:, :, tk], O.t[:, :, tk], pO.t[:, 0:256].rearrange("p (v t) -> p v t", v=2), ALU.add),
                             reads=[pO.b, O.b], writes=[O.b])

                    gst = [g0, g1, g2, g3, g4, g5, g6]
                    NS = len(gst)
                    for step in range(len(items) + NS - 1):
                        for si in range(NS - 1, -1, -1):
                            i = step - si
                            if 0 <= i < len(items):
                                gst[si](items[i])
                    for (t0, n, kind) in tiles(512, l == 0):
                        fsq = fsq_ring.next()
                        S.op("act", lambda e: e.activation(out=fsq.t[:, :, 0:n], in_=O.t[:, :, t0:t0 + n], func=AF.Square), reads=[O.b], writes=[fsq.b])
                        pM = ps_ring.next()
                        for vc in range(2):
                            mm(pM.t[:, 0:n], ones256.t[:], fsq.t[:, vc, 0:n], vc == 0, vc == 1, [ones256.b, fsq.b], [pM.b])
                        frs = frs_ring.next()
                        S.op("act", lambda e: e.activation(out=frs.t[:, 0:n], in_=pM.t[:, 0:n], func=AF.Ln, bias=EPS), reads=[pM.b], writes=[frs.b])
                        S.op("act", lambda e: e.activation(out=frs.t[:, 0:n], in_=frs.t[:, 0:n], func=AF.Exp, scale=-0.5), reads=[frs.b], writes=[frs.b])
                        fgg = fgg_ring.next()
                        S.dma("sp", fgg.t[:, :, 0:n], GGT[h * 256:(h + 1) * 256, t0:t0 + n].rearrange("(v p) t -> p v t", p=128), writes=[fgg.b])
                        fy = fy_ring.next()
                        for vc in range(2):
                            ftm = ftm_ring.next()
                            S.op("dve", lambda e: e.scalar_tensor_tensor(ftm.t[:, 0:n], O.t[:, vc, t0:t0 + n], GLG[l].t[:, vc:vc + 1], frs.t[:, 0:n],
                                                                         ALU.mult, ALU.mult), reads=[O.b, GLG[l].b, frs.b], writes=[ftm.b])
                            S.op("pool", lambda e: e.tensor_tensor(fy.t[:, vc, 0:n], ftm.t[:, 0:n], fgg.t[:, vc, 0:n], ALU.mult),
                                 reads=[ftm.b, fgg.b], writes=[fy.b])
                        S.dma("sp", YT[0][h * 256:(h + 1) * 256, t0:t0 + n].rearrange("(v p) t -> p v t", p=128), fy.t[:, :, 0:n], reads=[fy.b])
            phase_end(f"gla{l}")

        def phase_merge(l):
            NT = 256
            with ExitStack() as ph:
                def sb(nm, shape, dt):
                    return Tl(ph.enter_context(nc.sbuf_tensor(uname(nm), list(shape), dt)))
                WB = [sb(f"mWB{i}", [128, 8, D], BF16) for i in range(3)]
                WO = sb("mWO", [128, 8, D], BF16)
                for i in range(3):
                    S.dma("pool", WB[i].t[:], w_br[l, i].rearrange("(k p) n -> p k n", p=128), writes=[WB[i].b])
                S.dma("pool", WO.t[:], w_o[l].rearrange("(k p) n -> p k n", p=128), writes=[WO.b])
                y_ring = Ring([sb(f"my{j}", [128, 8, NT], BF16) for j in range(9)])
                sg_ring = Ring([sb(f"msg{j}", [128, 8, NT], BF16) for j in range(9)])
                xt_ring = Ring([sb(f"mxt{j}", [128, 8, NT], F32) for j in range(3)])
                z_ring = Ring([sb(f"mz{j}", [128, 8, NT], BF16) for j in range(2)])
                ta_ring = Ring([sb(f"mta{j}", [128, NT], F32) for j in range(2)])
                tb_ring = Ring([sb(f"mtb{j}", [128, NT], F32) for j in range(2)])
                tc_ring = Ring([sb(f"mtc{j}", [128, NT], F32) for j in range(2)])
                tl = tiles(NT, l == 0)
                st = [dict() for _ in tl]

                def loads(ti):
                    t0, n, kind = tl[ti]
                    ys, sgs = [], []
                    for i in range(3):
                        y = y_ring.next()
                        S.dma("sp", y.t[:, :, 0:n], YT[i].rearrange("(c p) t -> p c t", p=128)[:, :, t0:t0 + n], writes=[y.b])
                        sg = sg_ring.next()
                        S.dma("sp", sg.t[:, :, 0:n], SGT[i].rearrange("(c p) t -> p c t", p=128)[:, :, t0:t0 + n], writes=[sg.b])
                        ys.append(y)
                        sgs.append(sg)
                    xt = xt_ring.next()
                    S.dma("sp", xt.t[:, :, 0:n], XTv[:, :, t0:t0 + n], writes=[xt.b])
                    st[ti].update(ys=ys, sgs=sgs, xt=xt)

                def branch(ti):
                    t0, n, kind = tl[ti]
                    ys, sgs = st[ti]["ys"], st[ti]["sgs"]
                    z = z_ring.next()
                    st[ti]["z"] = z
                    for nn in range(8):
                        pp = []
                        for i in range(3):
                            p = ps_ring.next()
                            for k in range(8):
                                mm(p.t[:, 0:n], WB[i].t[:, k, nn * 128:(nn + 1) * 128], ys[i].t[:, k, 0:n], k == 0, k == 7,
                                   [WB[i].b, ys[i].b], [p.b])
                            pp.append(p)
                        ta, tb, tc = ta_ring.next(), tb_ring.next(), tc_ring.next()
                        for (tt_, i) in ((ta, 0), (tb, 1), (tc, 2)):
                            S.op("dve", lambda e: e.tensor_tensor(tt_.t[:, 0:n], pp[i].t[:, 0:n], sgs[i].t[:, nn, 0:n], ALU.mult),
                                 reads=[pp[i].b, sgs[i].b], writes=[tt_.b])
                        S.op("pool", lambda e: e.tensor_tensor(ta.t[:, 0:n], ta.t[:, 0:n], tb.t[:, 0:n], ALU.add), reads=[ta.b, tb.b], writes=[ta.b])
                        S.op("pool", lambda e: e.tensor_tensor(z.t[:, nn, 0:n], ta.t[:, 0:n], tc.t[:, 0:n], ALU.add), reads=[ta.b, tc.b], writes=[z.b])

                def outproj(ti):
                    t0, n, kind = tl[ti]
                    der = DER[l][kind]
                    z, xt = st[ti]["z"], st[ti]["xt"]
                    for nn in range(8):
                        p = ps_ring.next()
                        for k in range(8):
                            mm(p.t[:, 0:n], WO.t[:, k, nn * 128:(nn + 1) * 128], z.t[:, k, 0:n], k == 0, k == 7, [WO.b, z.b], [p.b])
                        S.op("dve", lambda e: e.scalar_tensor_tensor(xt.t[:, nn, 0:n], p.t[:, 0:n], der.t[:, 5, nn:nn + 1], xt.t[:, nn, 0:n],
                                                                     ALU.mult, ALU.add), reads=[p.b, der.b, xt.b], writes=[xt.b])
                    S.dma("sp", XTv[:, :, t0:t0 + n], xt.t[:, :, 0:n], reads=[xt.b])
                    st[ti].clear()

                loads(0)
                if len(tl) > 1:
                    loads(1)
                branch(0)
                for ti in range(len(tl)):
                    if ti + 2 < len(tl):
                        loads(ti + 2)
                    if ti + 1 < len(tl):
                        branch(ti + 1)
                    outproj(ti)
            phase_end(f"merge{l}")

        def phase_final():
            with ExitStack() as ph:
                def sb(nm, shape, dt):
                    return Tl(ph.enter_context(nc.sbuf_tensor(uname(nm), list(shape), dt)))
                xin_ring = Ring([sb(f"fx{i}", [128, 8, 512], F32) for i in range(2)])
                xo_ring = Ring([sb(f"fo{i}", [128, D], F32) for i in range(3)])
                ev = 0
                for g in range(8):
                    xi = xin_ring.next()
                    t0 = NCTX + g * 512
                    S.dma("sp", xi.t[:], XTv[:, :, t0:t0 + 512], writes=[xi.b])
                    for j in range(4):
                        xo = xo_ring.next()
                        for half in range(2):
                            p = ps_ring.next()
                            for q in range(4):
                                c = half * 4 + q
                                S.op("pe", lambda e: e.transpose(p.t[:, q * 128:(q + 1) * 128], xi.t[:, c, j * 128:(j + 1) * 128],
                                                                 ident.t[:]), reads=[xi.b, ident.b], writes=[p.b])
                            if ev % 2 == 0:
                                S.op("act", lambda e: e.copy(xo.t[:, half * 512:(half + 1) * 512], p.t[:]), reads=[p.b], writes=[xo.b])
                            else:
                                S.op("dve", lambda e: e.tensor_copy(xo.t[:, half * 512:(half + 1) * 512], p.t[:]), reads=[p.b],
                                     writes=[xo.b])
                            ev += 1
                        r0 = g * 512 + j * 128
                        S.dma("sp", out_d[r0:r0 + 128, :], xo.t[:], reads=[xo.b])
            phase_end("final")

        table = {"prep": phase_prep, "final": phase_final}
        order = ["prep"]
        for l in range(2):
            table[f"ffn{l}0"] = (lambda l=l: phase_ffn(l, 0, True, f"ffn{l}0"))
            table[f"inproj{l}"] = (lambda l=l: phase_inproj(l))
            table[f"gla{l}"] = (lambda l=l: phase_gla(l))
            table[f"nat{l}"] = (lambda l=l: phase_nat(l))
            table[f"gqa{l}"] = (lambda l=l: phase_gqa(l))
            table[f"merge{l}"] = (lambda l=l: phase_merge(l))
            table[f"ffn{l}1"] = (lambda l=l: phase_ffn(l, 1, l == 0, f"ffn{l}1"))
            order += [f"ffn{l}0", f"inproj{l}", f"gla{l}", f"nat{l}", f"gqa{l}", f"merge{l}", f"ffn{l}1"]
        order.append("final")
        for name in (phases or order):
            table[name]()
        S.barrier()
        print("instructions:", S.n_ins, {k: v for k, v in S.cnt.items()})
    return nc


def host_constants():
    c = {}
    c["ident"] = np.eye(128, dtype=np.float32)
    perm = np.zeros((128, 128), np.float32)
    for m in range(128):
        blk, r = divmod(m, 64)
        src = blk * 64 + (r + 32) % 64
        perm[src, m] = 1.0
    c["perm"] = perm
    half = 32
    freqs = (10000.0 ** (-np.arange(half, dtype=np.float32) / half)).astype(np.float32)
    tt = np.arange(NLAT)
    rows = (tt // 64).astype(np.float32)
    cols = (tt % 64).astype(np.float32)
    C = np.ones((128, T), np.float32)
    Sg = np.zeros((128, T), np.float32)
    for blk, pos in enumerate((rows, cols)):
        ang = pos[None, :] * freqs[:, None]
        cs, sn = np.cos(ang).astype(np.float32), np.sin(ang).astype(np.float32)
        C[blk * 64:blk * 64 + 32, NCTX:] = cs
        C[blk * 64 + 32:blk * 64 + 64, NCTX:] = cs
        Sg[blk * 64:blk * 64 + 32, NCTX:] = -sn
        Sg[blk * 64 + 32:blk * 64 + 64, NCTX:] = sn
    c["ropec"] = C
    c["ropes"] = Sg
    i = np.arange(128)
    a = -1.0 / 16.0
    tri = np.zeros((4, 128, 128), np.float32)
    tri[0] = a * (i[:, None] <= i[None, :])
    tri[1] = a * (i[:, None] >= i[None, :])
    tri[2] = a * (i[:, None] > i[None, :])
    tri[3] = a * (i[:, None] < i[None, :])
    c["tri"] = tri
    gm = np.zeros((2, 128, 128), np.float32)
    gm[0] = (i[:, None] <= i[None, :])
    gm[1] = (i[:, None] >= i[None, :])
    c["gmask"] = gm
    kc = np.arange(64)
    qc = np.arange(64)
    cstart = np.clip(qc - 8, 0, 48)
    colok = (kc[:, None] >= cstart[None, :]) & (kc[:, None] < cstart[None, :] + 16)
    m = np.where(colok, 0.0, -1e30).astype(np.float32)
    natm = np.zeros((128, NSLOT, 64), np.float32)
    for s in range(NSLOT):
        natm[0:64, s] = m
        natm[64:128, s] = m
    natm[0:64, 14] = -1e30
    natm[64:128, 15] = -1e30
    c["natm"] = natm.reshape(128, NSLOT * 64)
    return c


def gather_nat_bias(rpb):
    p = np.arange(128)
    half = p // 64
    kc = p % 64
    qc = np.arange(64)
    dc = np.clip(kc[:, None] - qc[None, :] + 15, 0, 30)
    out = np.empty((2, 2, 128, NSLOT, 8, 64), np.float32)
    for s in range(NSLOT):
        dr = np.clip(SLOT_DR0[s] + half + 7, 0, 14)
        for hg in range(2):
            for h in range(8):
                out[:, hg, :, s, h, :] = rpb[:, hg * 8 + h][:, dr[:, None], dc]
    return out.reshape(2, 2, 128, NSLOT * 512)


_CACHE = {}


def kernel(x, c, ctx, c_ctx, w_mod, b_mod, norm_g, ffn_w_in, ffn_w_out, w_in, gla_fg_w2, gla_fg_b,
           gla_norm_g, nat_q_norm, nat_k_norm, nat_rpb, gqa_q_norm, gqa_k_norm, w_branch, w_out):
    f = lambda a: np.ascontiguousarray(np.asarray(a, dtype=np.float32))
    if "nc" not in _CACHE:
        _CACHE["nc"] = build_program()
    nc = _CACHE["nc"]
    consts = host_constants()
    shared = dict(
        w_mod=f(w_mod), b_mod=f(b_mod).reshape(2, 72, 128), norm_g=f(norm_g).reshape(2, 24, 128),
        ffn_w_in=f(ffn_w_in), ffn_w_out=f(ffn_w_out), w_in=f(w_in), gla_fg_w2=f(gla_fg_w2),
        gla_fg_b=f(gla_fg_b).reshape(2, 2, 1, 512), gla_norm_g=f(gla_norm_g).reshape(2, 2, 128),
        nat_q_norm=f(nat_q_norm).reshape(2, 1, 64), nat_k_norm=f(nat_k_norm).reshape(2, 1, 64),
        gqa_q_norm=f(gqa_q_norm).reshape(2, 1, 128), gqa_k_norm=f(gqa_k_norm).reshape(2, 1, 128),
        w_branch=f(w_branch), w_out=f(w_out), natb=gather_nat_bias(f(nat_rpb)), **consts)
    x = f(x)
    ctx = f(ctx)
    c = f(c)
    c_ctx = f(c_ctx)
    in_maps = []
    for b in range(8):
        m = dict(shared)
        m["x"] = x[b]
        m["ctx"] = ctx[b]
        m["cc"] = np.ascontiguousarray(np.stack([c[b], c_ctx]).reshape(16, 128))
        in_maps.append(m)
    res = run_bass_kernel_spmd(nc, in_maps, core_ids=list(range(8)))
    return np.stack([np.asarray(r["out"], dtype=np.float32) for r in res.results], axis=0)
```

```python
import numpy as np
from contextlib import ExitStack
import concourse.bass as bass
import concourse.mybir as mybir
from concourse.bass_utils import run_bass_kernel_spmd

F32 = mybir.dt.float32
BF16 = mybir.dt.bfloat16
AF = mybir.ActivationFunctionType
ALU = mybir.AluOpType

NCTX = 256
NLAT = 4096
T = NCTX + NLAT
D = 1024
DFF = 2816
NCH = T // 128
EPS = 1e-6
C_GLA = 128

O_GQ, O_GK, O_GV, O_GG, O_FG = 0, 512, 1024, 2048, 3072
O_NQ, O_NK, O_NV = 3104, 4128, 5152
O_QQ, O_QK, O_QV = 6176, 7200, 7456
O_G0 = 7712
W_IN = 10784

NSLOT = 16
SLOT_DR0 = [d for d in range(-7, 7)] + [-5, 3]


class Buf:
    __slots__ = ("w", "r")

    def __init__(self):
        self.w = None
        self.r = {}


class Sched:
    def __init__(self, nc, es, n_dma_sems=14):
        self.nc = nc
        self.eng = {"pe": nc.tensor, "act": nc.scalar, "dve": nc.vector, "pool": nc.gpsimd, "sp": nc.sync}
        self.sem = {k: es.enter_context(nc.semaphore("sem_" + k)) for k in self.eng}
        self.cnt = {k: 0 for k in self.eng}
        self.seen = {k: {} for k in self.eng}
        self.dsem, self.dval, self.dnext = {}, {}, {}
        for q in ("sp", "pool"):
            self.dsem[q] = [es.enter_context(nc.semaphore(f"dsem_{q}{i}")) for i in range(n_dma_sems)]
            self.dval[q] = [0] * n_dma_sems
            self.dnext[q] = 0
        self.n_ins = 0

    def _semobj(self, key):
        if isinstance(key, str):
            return self.sem[key]
        return self.dsem[key[0]][key[1]]

    def _wait(self, e, tok, raw=False):
        if tok is None:
            return
        key, val = tok
        if key == e and e == "pe":
            return
        if self.seen[e].get(key, 0) >= val:
            return
        self.seen[e][key] = val
        self.eng[e].wait_ge(self._semobj(key), val)

    def _deps(self, e, reads, writes):
        for b in reads:
            self._wait(e, b.w, raw=True)
        for b in writes:
            self._wait(e, b.w)
            for tok in b.r.values():
                self._wait(e, tok)

    def _mark(self, tok, reads, writes):
        for b in reads:
            b.r[tok[0]] = tok
        for b in writes:
            b.w = tok
            b.r = {}

    def op(self, e, fn, reads=(), writes=()):
        self._deps(e, reads, writes)
        ins = fn(self.eng[e])
        self.cnt[e] += 1
        ins.then_inc(self.sem[e], 1)
        self._mark((e, self.cnt[e]), reads, writes)
        self.n_ins += 1

    def dma(self, q, out, in_, reads=(), writes=()):
        i = self.dnext[q]
        self.dnext[q] = (i + 1) % len(self.dsem[q])
        key = (q, i)
        if self.dval[q][i] > 0:
            self._wait(q, (key, self.dval[q][i]))
        self._deps(q, reads, writes)
        self.dval[q][i] += 16
        self.eng[q].dma_start(out=out, in_=in_).then_inc(self.dsem[q][i], 16)
        self._mark((key, self.dval[q][i]), reads, writes)
        self.n_ins += 1

    def barrier(self, engines=None):
        for e in (engines or self.eng):
            for o in self.eng:
                if o != e and self.cnt[o] > 0:
                    self._wait(e, (o, self.cnt[o]))
            for q in self.dsem:
                for i, v in enumerate(self.dval[q]):
                    if v > 0:
                        self._wait(e, ((q, i), v))


class Tl:
    __slots__ = ("t", "b")

    def __init__(self, t):
        self.t = t
        self.b = Buf()


class Ring:
    def __init__(self, items):
        self.items = items
        self.i = 0

    def next(self):
        x = self.items[self.i]
        self.i = (self.i + 1) % len(self.items)
        return x


def build_program(phases=None, dump=()):
    nc = bass.Bass("TRN2", target_bir_lowering=False)

    def din(name, shape, dt=F32):
        return nc.dram_tensor(name, list(shape), dt, kind="ExternalInput").ap()

    def dscr(name, shape, dt):
        kind = "ExternalOutput" if name in dump else "Internal"
        return nc.dram_tensor(name, list(shape), dt, kind=kind).ap()

    x_d = din("x", [NLAT, D])
    ctx_d = din("ctx", [NCTX, D])
    cc_d = din("cc", [16, 128])
    w_mod = din("w_mod", [2, D, 9 * D])
    b_mod = din("b_mod", [2, 72, 128])
    norm_g = din("norm_g", [2, 24, 128])
    ffn_wi = din("ffn_w_in", [2, 2, D, 2 * DFF])
    ffn_wo = din("ffn_w_out", [2, 2, DFF, D])
    w_in = din("w_in", [2, D, W_IN])
    fg_w2 = din("gla_fg_w2", [2, 2, 16, 512])
    fg_b = din("gla_fg_b", [2, 2, 1, 512])
    gla_g = din("gla_norm_g", [2, 2, 128])
    nat_qg = din("nat_q_norm", [2, 1, 64])
    nat_kg = din("nat_k_norm", [2, 1, 64])
    gqa_qg = din("gqa_q_norm", [2, 1, 128])
    gqa_kg = din("gqa_k_norm", [2, 1, 128])
    w_br = din("w_branch", [2, 3, D, D])
    w_o = din("w_out", [2, D, D])
    natb = din("natb", [2, 2, 128, NSLOT * 512])
    natm = din("natm", [128, NSLOT * 64])
    ropec = din("ropec", [128, T])
    ropes = din("ropes", [128, T])
    ident_d = din("ident", [128, 128])
    perm_d = din("perm", [128, 128])
    tri_d = din("tri", [4, 128, 128])
    msk_d = din("gmask", [2, 128, 128])

    out_d = nc.dram_tensor("out", [NLAT, D], F32, kind="ExternalOutput").ap()

    XT = dscr("XT", [D, T], F32)
    GQT = dscr("GQT", [512, T], BF16)
    GKT = dscr("GKT", [512, T], BF16)
    GGT = dscr("GGT", [D, T], BF16)
    SPF = dscr("SPF", [T, 512], F32)
    SPB = dscr("SPB", [T, 512], F32)
    GKt = dscr("GKt", [T, 512], BF16)
    GVt = dscr("GVt", [T, 1024], BF16)
    NQT = dscr("NQT", [D, T], BF16)
    NKT = dscr("NKT", [D, T], BF16)
    NVt = dscr("NVt", [T, D], BF16)
    QQT = dscr("QQT", [D, T], BF16)
    QKT = dscr("QKT", [256, T], BF16)
    QVt = dscr("QVt", [T, 256], BF16)
    SGT = dscr("SGT", [3, D, T], BF16)
    YT = dscr("YT", [3, D, T], BF16)
    UTD = dscr("UTD", [D, T], BF16)

    es = ExitStack()
    with es:
        S = Sched(nc, es)

        uid = [0]

        def uname(name):
            uid[0] += 1
            return f"s{uid[0]}_{name}"

        def gsb(name, shape, dt):
            return Tl(es.enter_context(nc.sbuf_tensor(uname(name), list(shape), dt)))

        PS = [Tl(es.enter_context(nc.psum_tensor(f"ps{i}", [128, 512], F32))) for i in range(8)]

        ident = gsb("ident", [128, 128], F32)
        permb = gsb("permb", [128, 128], BF16)
        onesD = gsb("onesD", [128, 128], BF16)
        ones128 = gsb("ones128", [128, 128], BF16)
        ones256 = gsb("ones256", [128, 128], BF16)
        blk64 = gsb("blk64", [128, 128], BF16)
        ones1 = gsb("ones1", [128, 128], BF16)
        onesf = gsb("onesf", [1, 128], F32)
        DER = [[gsb(f"der{l}{k}", [128, 9, 8], F32) for k in range(2)] for l in range(2)]
        GQG = [gsb(f"gqg{l}", [128, 2], F32) for l in range(2)]
        NG = [gsb(f"ng{l}", [128, 2], F32) for l in range(2)]
        GLG = [gsb(f"glg{l}", [128, 2], F32) for l in range(2)]

        S.dma("sp", ident.t[:], ident_d, writes=[ident.b])
        S.dma("pool", permb.t[:], perm_d, writes=[permb.b])
        S.op("dve", lambda e: e.memset(onesD.t[:], 1.0 / 1024), writes=[onesD.b])
        S.op("dve", lambda e: e.memset(ones128.t[:], 1.0 / 128), writes=[ones128.b])
        S.op("dve", lambda e: e.memset(ones256.t[:], 1.0 / 256), writes=[ones256.b])
        S.op("dve", lambda e: e.memset(ones1.t[:], 1.0), writes=[ones1.b])
        S.op("dve", lambda e: e.memset(onesf.t[:], 1.0), writes=[onesf.b])
        S.op("dve", lambda e: e.memset(blk64.t[:], 0.0), writes=[blk64.b])
        S.op("dve", lambda e: e.memset(blk64.t[0:64, 0:64], 1.0 / 64), writes=[blk64.b])
        S.op("dve", lambda e: e.memset(blk64.t[64:128, 64:128], 1.0 / 64), writes=[blk64.b])

        ps_ring = Ring(PS)

        def mm(out, lhsT, rhs, start, stop, reads, writes):
            S.op("pe", lambda e: e.matmul(out, lhsT, rhs, start=start, stop=stop), reads=reads, writes=writes)

        def phase_end(name):
            S.barrier()

        def tiles(n, with_ctx=True):
            res = []
            if with_ctx:
                res.append((0, NCTX, 1))
            for t0 in range(NCTX, T, n):
                res.append((t0, n, 0))
            return res

        def phase_prep():
            with ExitStack() as ph:
                def sb(name, shape, dt):
                    return Tl(ph.enter_context(nc.sbuf_tensor(uname(name), list(shape), dt)))

                rows = sb("rows", [128, 128], F32)
                cT = sb("cT", [128, 16], F32)
                sT = sb("sT", [128, 16], F32)
                bT = [sb(f"bT{l}", [128, 72], F32) for l in range(2)]
                gT = [sb(f"gT{l}", [128, 24], F32) for l in range(2)]
                tmpv = sb("tmpv", [128, 4], F32)

                def loadT(dst_ap, dst_buf, src_rows, R, extra_src=None):
                    S.dma("sp", rows.t[0:R, :], src_rows, writes=[rows.b])
                    p = ps_ring.next()
                    mm(p.t[:, 0:R], rows.t[0:R, :], ident.t[0:R, 0:R], True, True, [rows.b, ident.b], [p.b])
                    S.op("dve", lambda e: e.tensor_copy(dst_ap, p.t[:, 0:R]), reads=[p.b], writes=[dst_buf])

                loadT(cT.t[:], cT.b, cc_d, 16)
                S.op("act", lambda e: e.activation(out=sT.t[:], in_=cT.t[:], func=AF.Silu), reads=[cT.b], writes=[sT.b])
                for l in range(2):
                    loadT(bT[l].t[:], bT[l].b, b_mod[l], 72)
                    loadT(gT[l].t[:], gT[l].b, norm_g[l], 24)
                    loadT(GLG[l].t[:], GLG[l].b, gla_g[l], 2)
                    loadT(tmpv.t[:, 0:1], tmpv.b, gqa_qg[l], 1)
                    S.op("dve", lambda e: e.tensor_scalar(GQG[l].t[:, 0:1], tmpv.t[:, 0:1], 128.0 ** -0.5, 0.0, ALU.mult, ALU.add),
                         reads=[tmpv.b], writes=[GQG[l].b])
                    loadT(GQG[l].t[:, 1:2], GQG[l].b, gqa_kg[l], 1)
                    S.dma("sp", rows.t[0:1, 0:64], nat_qg[l], writes=[rows.b])
                    S.dma("sp", rows.t[0:1, 64:128], nat_qg[l], writes=[rows.b])
                    p = ps_ring.next()
                    mm(p.t[:, 0:1], rows.t[0:1, :], ident.t[0:1, 0:1], True, True, [rows.b, ident.b], [p.b])
                    S.op("dve", lambda e: e.tensor_scalar(NG[l].t[:, 0:1], p.t[:, 0:1], 0.125, 0.0, ALU.mult, ALU.add),
                         reads=[p.b], writes=[NG[l].b])
                    S.dma("sp", rows.t[0:1, 0:64], nat_kg[l], writes=[rows.b])
                    S.dma("sp", rows.t[0:1, 64:128], nat_kg[l], writes=[rows.b])
                    p = ps_ring.next()
                    mm(p.t[:, 0:1], rows.t[0:1, :], ident.t[0:1, 0:1], True, True, [rows.b, ident.b], [p.b])
                    S.op("dve", lambda e: e.tensor_copy(NG[l].t[:, 1:2], p.t[:, 0:1]), reads=[p.b], writes=[NG[l].b])

                sTb = sb("sTb", [128, 16], BF16)
                S.op("dve", lambda e: e.tensor_copy(sTb.t[:], sT.t[:]), reads=[sT.b], writes=[sTb.b])
                wm_ring = Ring([sb(f"wm{i}", [128, 8, 1152], BF16) for i in range(3)])
                for l in range(2):
                    pm = ps_ring.next()
                    src = w_mod[l].rearrange("(k p) n -> p k n", p=128)
                    for g in range(8):
                        wm = wm_ring.next()
                        S.dma("pool", wm.t[:], src[:, :, g * 1152:(g + 1) * 1152], writes=[wm.b])
                        for cc in range(9):
                            nn = g * 9 + cc
                            for k in range(8):
                                mm(pm.t[:, nn * 2:nn * 2 + 2], wm.t[:, k, cc * 128:(cc + 1) * 128],
                                   sTb.t[:].rearrange("p (a k) -> p k a", a=2)[:, k, :], k == 0, k == 7,
                                   [wm.b, sTb.b], [pm.b])
                    for kind in range(2):
                        der = DER[l][kind]
                        S.op("dve", lambda e: e.tensor_tensor(
                            der.t[:].rearrange("p j c -> p (j c)"),
                            pm.t[:, 0:144].rearrange("p (n a) -> p n a", a=2)[:, :, kind],
                            bT[l].t[:], ALU.add), reads=[pm.b, bT[l].b], writes=[der.b])
                        for s in range(3):
                            S.op("dve", lambda e: e.scalar_tensor_tensor(
                                der.t[:, 3 * s + 1, :], der.t[:, 3 * s + 1, :], 1.0, gT[l].t[:, s * 8:(s + 1) * 8],
                                ALU.add, ALU.mult), reads=[der.b, gT[l].b], writes=[der.b])
                            if s != 1:
                                S.op("dve", lambda e: e.tensor_scalar(der.t[:, 3 * s + 2, :], der.t[:, 3 * s + 2, :], 0.5, 0.0,
                                                                     ALU.mult, ALU.add), reads=[der.b], writes=[der.b])

                xin_ring = Ring([sb(f"xin{i}", [128, D], F32) for i in range(3)])
                xst_ring = Ring([sb(f"xst{i}", [128, 8, 512], F32) for i in range(2)])
                groups = [(0, 2)] + [(2 + 4 * g, 4) for g in range(8)]
                ev = 0
                for (c0, ncn) in groups:
                    st = xst_ring.next()
                    for j in range(ncn):
                        ch = c0 + j
                        xi = xin_ring.next()
                        srcx = ctx_d[ch * 128:(ch + 1) * 128, :] if ch < 2 else x_d[(ch - 2) * 128:(ch - 1) * 128, :]
                        S.dma("sp", xi.t[:], srcx, writes=[xi.b])
                        for half in range(2):
                            p = ps_ring.next()
                            for q in range(4):
                                c = half * 4 + q
                                S.op("pe", lambda e: e.transpose(p.t[:, q * 128:(q + 1) * 128], xi.t[:, c * 128:(c + 1) * 128],
                                                                 ident.t[:]), reads=[xi.b, ident.b], writes=[p.b])
                            eng = "act" if ev % 2 == 0 else "dve"
                            ev += 1
                            dst = st.t[:, half * 4:(half + 1) * 4, j * 128:(j + 1) * 128]
                            srcp = p.t[:].rearrange("p (q t) -> p q t", q=4)
                            if eng == "act":
                                S.op("act", lambda e: e.copy(dst, srcp), reads=[p.b], writes=[st.b])
                            else:
                                S.op("dve", lambda e: e.tensor_copy(dst, srcp), reads=[p.b], writes=[st.b])
                    n = ncn * 128
                    S.dma("sp", XT.rearrange("(c p) t -> p c t", p=128)[:, :, c0 * 128:c0 * 128 + n], st.t[:, :, 0:n],
                          reads=[st.b])
            phase_end("prep")

        def norm_mod(l, kind, s, xt, n, sq, rstd_ring, tmp_ring, u_ap_fn, u_buf, aff="act", part="all"):
            der = DER[l][kind]
            if part in ("all", "sq"):
                S.op("act", lambda e: e.activation(out=sq.t[:, :, 0:n], in_=xt.t[:, :, 0:n], func=AF.Square),
                     reads=[xt.b], writes=[sq.b])
            if part == "sq":
                return
            p = ps_ring.next()
            for c in range(8):
                mm(p.t[:, 0:n], onesD.t[:], sq.t[:, c, 0:n], c == 0, c == 7, [onesD.b, sq.b], [p.b])
            rstd = rstd_ring.next()
            S.op("act", lambda e: e.activation(out=rstd.t[:, 0:n], in_=p.t[:, 0:n], func=AF.Ln, bias=EPS),
                 reads=[p.b], writes=[rstd.b])
            S.op("act", lambda e: e.activation(out=rstd.t[:, 0:n], in_=rstd.t[:, 0:n], func=AF.Exp, scale=-0.5),
                 reads=[rstd.b], writes=[rstd.b])
            for c in range(8):
                tmp = tmp_ring.next()
                S.op("dve", lambda e: e.scalar_tensor_tensor(tmp.t[:, 0:n], xt.t[:, c, 0:n], der.t[:, 3 * s + 1, c:c + 1],
                                                             rstd.t[:, 0:n], ALU.mult, ALU.mult),
                     reads=[xt.b, der.b, rstd.b], writes=[tmp.b])
                if aff == "act":
                    S.op("act", lambda e: e.activation(out=u_ap_fn(c), in_=tmp.t[:, 0:n], func=AF.Identity,
                                                       bias=der.t[:, 3 * s, c:c + 1]),
                         reads=[tmp.b, der.b], writes=[u_buf])
                else:
                    S.op("pool", lambda e: e.tensor_scalar(u_ap_fn(c), tmp.t[:, 0:n], 1.0, der.t[:, 3 * s, c:c + 1], ALU.mult, ALU.add),
                         reads=[tmp.b, der.b], writes=[u_buf])

        XTv = XT.rearrange("(c p) t -> p c t", p=128)

        def phase_ffn(l, i, with_ctx, name):
            NT = 256
            s = 0 if i == 0 else 2
            with ExitStack() as ph:
                def sb(nm, shape, dt):
                    return Tl(ph.enter_context(nc.sbuf_tensor(uname(nm), list(shape), dt)))
                WU = sb("WU", [128, 8, 2 * DFF], BF16)
                WD = sb("WD", [128, 22, D], BF16)
                BWU = [Buf() for _ in range(11)]
                BWD = [Buf() for _ in range(11)]
                srcu = ffn_wi[l, i].rearrange("(k p) n -> p k n", p=128)
                srcd = ffn_wo[l, i].rearrange("(k p) n -> p k n", p=128)
                order = [0, 5, 1, 6, 2, 7, 3, 8, 4, 9, 10]
                for r in order:
                    S.dma("pool", WU.t[:, :, r * 512:(r + 1) * 512], srcu[:, :, r * 512:(r + 1) * 512], writes=[BWU[r]])
                for r in range(11):
                    S.dma("pool", WD.t[:, 2 * r:2 * r + 2, :], srcd[:, 2 * r:2 * r + 2, :], writes=[BWD[r]])
                xt_ring = Ring([sb(f"xt{j}", [128, 8, NT], F32) for j in range(2)])
                sq = sb("sq", [128, 8, NT], BF16)
                rstd_ring = Ring([sb(f"rstd{j}", [128, NT], F32) for j in range(2)])
                tmp_ring = Ring([sb(f"tmp{j}", [128, NT], F32) for j in range(2)])
                u_ring = Ring([sb(f"u{j}", [128, 8, NT], BF16) for j in range(2)])
                sl_ring = Ring([sb(f"sl{j}", [128, NT], F32) for j in range(3)])
                g_ring = Ring([sb(f"g{j}", [128, 22, NT], BF16) for j in range(2)])
                last = (l == 1 and i == 1)
                u2 = sb("u2", [128, 8, NT], BF16) if i == 0 else None
                tmp_ring2 = Ring([sb(f"tmpb{j}", [128, NT], F32) for j in range(2)]) if i == 0 else None
                xo_ring = Ring([sb(f"xo{j}", [128, D], F32) for j in range(2)]) if last else None
                evc = [0]

                def do_outT(ti):
                    t0, n, kind = tl[ti]
                    xt = xts[ti]
                    for sub in range(n // 128):
                        xo = xo_ring.next()
                        for half in range(2):
                            p = ps_ring.next()
                            for q in range(4):
                                c = half * 4 + q
                                S.op("pe", lambda e: e.transpose(p.t[:, q * 128:(q + 1) * 128], xt.t[:, c, sub * 128:(sub + 1) * 128],
                                                                 ident.t[:]), reads=[xt.b, ident.b], writes=[p.b])
                            if evc[0] % 2 == 0:
                                S.op("act", lambda e: e.copy(xo.t[:, half * 512:(half + 1) * 512], p.t[:]), reads=[p.b], writes=[xo.b])
                            else:
                                S.op("dve", lambda e: e.tensor_copy(xo.t[:, half * 512:(half + 1) * 512], p.t[:]), reads=[p.b], writes=[xo.b])
                            evc[0] += 1
                        r0 = t0 - NCTX + sub * 128
                        S.dma("sp", out_d[r0:r0 + 128, :], xo.t[:], reads=[xo.b])
                UTDv = UTD.rearrange("(c p) t -> p c t", p=128)

                def do_norm2(ti, part):
                    t0, n, kind = tl[ti]
                    norm_mod(l, kind, 1, xts[ti], n, sq, rstd_ring, tmp_ring2, lambda c: u2.t[:, c, 0:n], u2.b, aff="pool", part=part)
                    if part != "sq":
                        S.dma("sp", UTDv[:, :, t0:t0 + n], u2.t[:, :, 0:n], reads=[u2.b])
                tl = tiles(NT, with_ctx)

                def load(ti):
                    t0, n, kind = tl[ti]
                    xt = xt_ring.next()
                    S.dma("sp", xt.t[:, :, 0:n], XTv[:, :, t0:t0 + n], writes=[xt.b])
                    return xt
                xts = [None] * len(tl)
                us = [None] * len(tl)
                xts[0] = load(0)

                def do_norm(ti, part):
                    t0, n, kind = tl[ti]
                    if part != "rest":
                        us[ti] = u_ring.next()
                    u = us[ti]
                    norm_mod(l, kind, s, xts[ti], n, sq, rstd_ring, tmp_ring, lambda c: u.t[:, c, 0:n], u.b, part=part)
                do_norm(0, "all")
                for ti, (t0, n, kind) in enumerate(tl):
                    xt = xts[ti]
                    u = us[ti]
                    g = g_ring.next()
                    for j in range(22):
                        if j == 2 and i == 0 and ti > 0:
                            do_norm2(ti - 1, "sq")
                        if j == 6:
                            if i == 0 and ti > 0:
                                do_norm2(ti - 1, "rest")
                            if last and ti > 0:
                                do_outT(ti - 1)
                            if ti + 1 < len(tl):
                                xts[ti + 1] = load(ti + 1)
                        if j == 15 and ti + 1 < len(tl):
                            do_norm(ti + 1, "sq")
                        p = ps_ring.next()
                        ra = (j * 128) // 512
                        rb = (DFF + j * 128) // 512
                        for k in range(8):
                            mm(p.t[:, 0:n], WU.t[:, k, j * 128:(j + 1) * 128], u.t[:, k, 0:n], k == 0, k == 7,
                               [BWU[ra], u.b], [p.b])
                        for k in range(8):
                            mm(p.t[:, 256:256 + n], WU.t[:, k, DFF + j * 128:DFF + (j + 1) * 128], u.t[:, k, 0:n], k == 0, k == 7,
                               [BWU[rb], u.b], [p.b])
                        sl = sl_ring.next()
                        S.op("act", lambda e: e.activation(out=sl.t[:, 0:n], in_=p.t[:, 0:n], func=AF.Silu),
                             reads=[p.b], writes=[sl.b])
                        S.op("dve", lambda e: e.tensor_tensor(g.t[:, j, 0:n], sl.t[:, 0:n], p.t[:, 256:256 + n], ALU.mult),
                             reads=[sl.b, p.b], writes=[g.b])
                    if ti + 1 < len(tl):
                        do_norm(ti + 1, "rest")
                    der = DER[l][kind]
                    for nn in range(8):
                        p = ps_ring.next()
                        for j in range(22):
                            mm(p.t[:, 0:n], WD.t[:, j, nn * 128:(nn + 1) * 128], g.t[:, j, 0:n], j == 0, j == 21,
                               [BWD[j // 2], g.b], [p.b])
                        S.op("dve", lambda e: e.scalar_tensor_tensor(xt.t[:, nn, 0:n], p.t[:, 0:n], der.t[:, 3 * s + 2, nn:nn + 1],
                                                                     xt.t[:, nn, 0:n], ALU.mult, ALU.add),
                             reads=[p.b, der.b, xt.b], writes=[xt.b])
                    if not last:
                        S.dma("sp", XTv[:, :, t0:t0 + n], xt.t[:, :, 0:n], reads=[xt.b])
                if i == 0:
                    do_norm2(len(tl) - 1, "all")
                if last:
                    do_outT(len(tl) - 1)
            phase_end(name)

        def phase_inproj(l):
            tl = tiles(512, True)
            with ExitStack() as ph:
                def sb(nm, shape, dt):
                    return Tl(ph.enter_context(nc.sbuf_tensor(uname(nm), list(shape), dt)))
                UT = sb("UT", [128, 8, T], BF16)
                BUT = [Buf() for _ in tl]
                UTDv = UTD.rearrange("(c p) t -> p c t", p=128)
                for ti, (t0, n, kind) in enumerate(tl):
                    S.dma("sp", UT.t[:, :, t0:t0 + n], UTDv[:, :, t0:t0 + n], writes=[BUT[ti]])
                w_ring = Ring([sb(f"iw{j}", [128, 8, 1024], BF16) for j in range(2)])
                FG = [sb(f"fg{d}", [16, T], F32) for d in range(2)]
                W2 = [sb(f"w2{d}", [16, 512], F32) for d in range(2)]
                B2 = [sb(f"b2{d}", [1, 512], F32) for d in range(2)]
                st_ring = Ring([sb(f"ist{j}", [128, 512], BF16) for j in range(4)])
                sqb_ring = Ring([sb(f"isqb{j}", [128, 512], BF16) for j in range(2)])
                rs_ring = Ring([sb(f"irs{j}", [128, 512], F32) for j in range(2)])
                qn_ring = Ring([sb(f"iqn{j}", [128, 512], BF16) for j in range(2)])
                t1_ring = Ring([sb(f"it1{j}", [128, 512], F32) for j in range(2)])
                t2_ring = Ring([sb(f"it2{j}", [128, 512], F32) for j in range(2)])
                rc_ring = Ring([sb(f"irc{j}", [128, 512], F32) for j in range(2)])
                rsn_ring = Ring([sb(f"irsn{j}", [128, 512], F32) for j in range(2)])
                e_ring = Ring([sb(f"ie{j}", [128, 512], F32) for j in range(2)])
                sp_ring = Ring([sb(f"isp{j}", [128, 512], F32) for j in range(2)])
                wsrc = w_in[l].rearrange("(k p) n -> p k n", p=128)
                for d in range(2):
                    S.dma("sp", W2[d].t[:], fg_w2[l, d], writes=[W2[d].b])
                    S.dma("sp", B2[d].t[:], fg_b[l, d], writes=[B2[d].b])
                ev = [0]

                def evac(dst, src, reads, writes):
                    if ev[0] % 2 == 0:
                        S.op("act", lambda e: e.copy(dst, src), reads=reads, writes=writes)
                    else:
                        S.op("dve", lambda e: e.tensor_copy(dst, src), reads=reads, writes=writes)
                    ev[0] += 1

                def loadw(col0, ncols):
                    W = w_ring.next()
                    S.dma("pool", W.t[:, :, 0:ncols], wsrc[:, :, col0:col0 + ncols], writes=[W.b])
                    return W

                def fm_group(col0, nchunks, epi, m=128):
                    W = loadw(col0, nchunks * m)
                    for ti, (t0, n, kind) in enumerate(tl):
                        for ch in range(nchunks):
                            p = ps_ring.next()
                            for k in range(8):
                                mm(p.t[0:m, 0:n], W.t[:, k, ch * m:(ch + 1) * m], UT.t[:, k, t0:t0 + n], k == 0, k == 7,
                                   [W.b, BUT[ti]], [p.b])
                            epi(ch, p, t0, n, ti)

                def store_fm(dst, ch, st, t0, n):
                    S.dma("sp", dst[ch * 128:(ch + 1) * 128, t0:t0 + n], st.t[:, 0:n], reads=[st.b])

                def epi_copy(dst):
                    def f(ch, p, t0, n, ti):
                        st = st_ring.next()
                        evac(st.t[:, 0:n], p.t[:, 0:n], [p.b], [st.b])
                        store_fm(dst, ch, st, t0, n)
                    return f

                def epi_act(dst, func):
                    def f(ch, p, t0, n, ti):
                        st = st_ring.next()
                        S.op("act", lambda e: e.activation(out=st.t[:, 0:n], in_=p.t[:, 0:n], func=func), reads=[p.b], writes=[st.b])
                        store_fm(dst, ch, st, t0, n)
                    return f

                pA_ring = Ring([PS[0], PS[1], PS[2], PS[3]])
                pB_ring = Ring([PS[4], PS[5]])
                pC_ring = Ring([PS[6], PS[7]])
                qn_ring3 = Ring(qn_ring.items + [sb("iqn2", [128, 512], BF16)])
                xc_ring = Ring([sb(f"ixc{j}", [128, 512], F32) for j in range(3)])

                def fm_group_pipe(col0, nchunks, stages):
                    W = loadw(col0, nchunks * 128)
                    items = []
                    for ti, (t0, n, kind) in enumerate(tl):
                        for ch in range(nchunks):
                            items.append({"ch": ch, "t0": t0, "n": n, "ti": ti})

                    def st0(it):
                        p = pA_ring.next()
                        n, t0 = it["n"], it["t0"]
                        for k in range(8):
                            mm(p.t[:, 0:n], W.t[:, k, it["ch"] * 128:(it["ch"] + 1) * 128], UT.t[:, k, t0:t0 + n], k == 0, k == 7,
                               [W.b, BUT[it["ti"]]], [p.b])
                        it["p"] = p
                    allst = [st0] + stages
                    ns = len(allst)
                    for step in range(len(items) + ns - 1):
                        for si in range(ns - 1, -1, -1):
                            i = step - si
                            if 0 <= i < len(items):
                                allst[si](items[i])

                def st_stats(ones_t):
                    def f(it):
                        p, n = it["p"], it["n"]
                        sqb = sqb_ring.next()
                        S.op("act", lambda e: e.activation(out=sqb.t[:, 0:n], in_=p.t[:, 0:n], func=AF.Square), reads=[p.b], writes=[sqb.b])
                        p2 = pB_ring.next()
                        mm(p2.t[:, 0:n], ones_t.t[:], sqb.t[:, 0:n], True, True, [ones_t.b, sqb.b], [p2.b])
                        it["p2"] = p2
                    return f

                def st_norm(gain_t, col, out_ring, dst=None, perm=False):
                    def f(it):
                        p, p2, n = it["p"], it["p2"], it["n"]
                        rs = rs_ring.next()
                        S.op("act", lambda e: e.activation(out=rs.t[:, 0:n], in_=p2.t[:, 0:n], func=AF.Ln, bias=EPS), reads=[p2.b], writes=[rs.b])
                        S.op("act", lambda e: e.activation(out=rs.t[:, 0:n], in_=rs.t[:, 0:n], func=AF.Exp, scale=-0.5), reads=[rs.b], writes=[rs.b])
                        o = out_ring.next()
                        S.op("dve", lambda e: e.scalar_tensor_tensor(o.t[:, 0:n], p.t[:, 0:n], gain_t.t[:, col:col + 1], rs.t[:, 0:n], ALU.mult, ALU.mult),
                             reads=[p.b, rs.b, gain_t.b], writes=[o.b])
                        it["o"] = o
                        if perm:
                            p3 = pC_ring.next()
                            mm(p3.t[:, 0:n], permb.t[:], o.t[:, 0:n], True, True, [permb.b, o.b], [p3.b])
                            it["p3"] = p3
                        else:
                            store_fm(dst, it["ch"], o, it["t0"], n)
                    return f

                rope_tiles = {}

                def get_rope(ti, t0, n):
                    if ti not in rope_tiles:
                        rc = rc_ring.next()
                        rsn = rsn_ring.next()
                        S.dma("sp", rc.t[:, 0:n], ropec[:, t0:t0 + n], writes=[rc.b])
                        S.dma("sp", rsn.t[:, 0:n], ropes[:, t0:t0 + n], writes=[rsn.b])
                        rope_tiles.clear()
                        rope_tiles[ti] = (rc, rsn)
                    return rope_tiles[ti]

                def st_rope(dst):
                    def f(it):
                        qn, p3, n, t0 = it["o"], it["p3"], it["n"], it["t0"]
                        rc, rsn = get_rope(it["ti"], t0, n)
                        t1 = t1_ring.next()
                        t2 = t2_ring.next()
                        S.op("pool", lambda e: e.tensor_tensor(t1.t[:, 0:n], qn.t[:, 0:n], rc.t[:, 0:n], ALU.mult),
                             reads=[qn.b, rc.b], writes=[t1.b])
                        S.op("dve", lambda e: e.tensor_tensor(t2.t[:, 0:n], p3.t[:, 0:n], rsn.t[:, 0:n], ALU.mult),
                             reads=[p3.b, rsn.b], writes=[t2.b])
                        st = st_ring.next()
                        S.op("dve", lambda e: e.tensor_tensor(st.t[:, 0:n], t1.t[:, 0:n], t2.t[:, 0:n], ALU.add),
                             reads=[t1.b, t2.b], writes=[st.b])
                        store_fm(dst, it["ch"], st, t0, n)
                    return f

                def tm_group(col0, ncols, dst):
                    W = loadw(col0, ncols)
                    w = min(512, ncols)
                    for cidx in range(NCH):
                        ti = 0 if cidx < 2 else 1 + (cidx - 2) // 4
                        for piece in range(ncols // w):
                            p = ps_ring.next()
                            for k in range(8):
                                mm(p.t[:, 0:w], UT.t[:, k, cidx * 128:(cidx + 1) * 128], W.t[:, k, piece * w:(piece + 1) * w],
                                   k == 0, k == 7, [W.b, BUT[ti]], [p.b])
                            st = st_ring.next()
                            evac(st.t[:, 0:w], p.t[:, 0:w], [p.b], [st.b])
                            S.dma("sp", dst[cidx * 128:(cidx + 1) * 128, piece * w:(piece + 1) * w], st.t[:, 0:w], reads=[st.b])

                fm_group(O_GQ, 4, epi_copy(GQT))
                fm_group(O_GK, 4, epi_copy(GKT))
                tm_group(O_GK, 512, GKt)
                tm_group(O_GV, 1024, GVt)
                fm_group(O_GG, 8, epi_act(GGT, AF.Silu))

                def epi_fg(ch, p, t0, n, ti):
                    evac(FG[ch].t[0:16, t0:t0 + n], p.t[0:16, 0:n], [p.b], [FG[ch].b])
                fm_group(O_FG, 2, epi_fg, m=16)
                for cidx in range(NCH):
                    for d in range(2):
                        p = ps_ring.next()
                        mm(p.t[:, :], FG[d].t[0:16, cidx * 128:(cidx + 1) * 128], W2[d].t[:], True, False, [FG[d].b, W2[d].b], [p.b])
                        mm(p.t[:, :], onesf.t[0:1, :], B2[d].t[:], False, True, [onesf.b, B2[d].b], [p.b])
                        ee = e_ring.next()
                        S.op("act", lambda e: e.activation(out=ee.t[:], in_=p.t[:], func=AF.Exp, scale=-1.0), reads=[p.b], writes=[ee.b])
                        spt = sp_ring.next()
                        S.op("act", lambda e: e.activation(out=spt.t[:], in_=ee.t[:], func=AF.Ln, bias=1.0), reads=[ee.b], writes=[spt.b])
                        S.dma("sp", (SPF if d == 0 else SPB)[cidx * 128:(cidx + 1) * 128, :], spt.t[:], reads=[spt.b])
                fm_group_pipe(O_NQ, 8, [st_stats(blk64), st_norm(NG[l], 0, st_ring, dst=NQT)])
                fm_group_pipe(O_NK, 8, [st_stats(blk64), st_norm(NG[l], 1, st_ring, dst=NKT)])
                tm_group(O_NV, 1024, NVt)
                fm_group_pipe(O_QQ, 8, [st_stats(ones128), st_norm(GQG[l], 0, qn_ring3, perm=True), st_rope(QQT)])
                fm_group_pipe(O_QK, 2, [st_stats(ones128), st_norm(GQG[l], 1, qn_ring3, perm=True), st_rope(QKT)])
                tm_group(O_QV, 256, QVt)
                for i in range(3):
                    fm_group(O_G0 + i * 1024, 8, epi_act(SGT[i], AF.Sigmoid))
            phase_end(f"inproj{l}")

        def phase_gqa(l):
            with ExitStack() as ph:
                def sb(nm, shape, dt):
                    return Tl(ph.enter_context(nc.sbuf_tensor(uname(nm), list(shape), dt)))
                KT = sb("qKT", [128, 2, T], BF16)
                V = sb("qV", [128, NCH, 256], BF16)
                S.dma("sp", KT.t[:], QKT.rearrange("(g p) t -> p g t", p=128), writes=[KT.b])
                S.dma("sp", V.t[:], QVt.rearrange("(s p) c -> p s c", p=128), writes=[V.b])
                q_ring = Ring([sb(f"qq{j}", [128, 512], BF16) for j in range(3)])
                pt_ring = Ring([sb(f"qpt{j}", [128, 512], BF16) for j in range(4)])
                rd_ring = Ring([sb(f"qrd{j}", [128, 512], F32) for j in range(2)])
                yo_ring = Ring([sb(f"qyo{j}", [128, 512], BF16) for j in range(3)])
                acc_ring = Ring([(PS[0], PS[1]), (PS[2], PS[3])])
                s_ring = Ring([PS[4], PS[5], PS[6], PS[7]])
                jobs = []
                for h in range(8):
                    if l == 0:
                        jobs.append((h, 0, NCTX, 2))
                    for t0 in range(NCTX, T, 512):
                        jobs.append((h, t0, 512, NCH))
                LOOK = 2
                items = []
                for ji, (h, t0, n, nkc) in enumerate(jobs):
                    for sc in range(nkc):
                        items.append((ji, sc))
                jst = {}
                pend = []

                def qk(ji, sc):
                    h, t0, n, nkc = jobs[ji]
                    g = h // 4
                    if sc == 0:
                        q = q_ring.next()
                        S.dma("sp", q.t[:, 0:n], QQT[h * 128:(h + 1) * 128, t0:t0 + n], writes=[q.b])
                        jst[ji] = {"q": q}
                    q = jst[ji]["q"]
                    ps_ = s_ring.next()
                    mm(ps_.t[:, 0:n], KT.t[:, g, sc * 128:(sc + 1) * 128], q.t[:, 0:n], True, True, [KT.b, q.b], [ps_.b])
                    return ps_

                def proc(ji, sc, ps_):
                    h, t0, n, nkc = jobs[ji]
                    g = h // 4
                    st = jst[ji]
                    if sc == 0:
                        st["po"], st["pd"] = acc_ring.next()
                    po, pd = st["po"], st["pd"]
                    pt = pt_ring.next()
                    S.op("act", lambda e: e.activation(out=pt.t[:, 0:n], in_=ps_.t[:, 0:n], func=AF.Exp), reads=[ps_.b], writes=[pt.b])
                    mm(po.t[:, 0:n], V.t[:, sc, g * 128:(g + 1) * 128], pt.t[:, 0:n], sc == 0, sc == nkc - 1, [V.b, pt.b], [po.b])
                    mm(pd.t[:, 0:n], ones1.t[:], pt.t[:, 0:n], sc == 0, sc == nkc - 1, [ones1.b, pt.b], [pd.b])
                    if sc == nkc - 1:
                        rd = rd_ring.next()
                        S.op("dve", lambda e: e.reciprocal(rd.t[:, 0:n], pd.t[:, 0:n]), reads=[pd.b], writes=[rd.b])
                        yo = yo_ring.next()
                        S.op("dve", lambda e: e.tensor_tensor(yo.t[:, 0:n], po.t[:, 0:n], rd.t[:, 0:n], ALU.mult), reads=[po.b, rd.b], writes=[yo.b])
                        S.dma("sp", YT[2][h * 128:(h + 1) * 128, t0:t0 + n], yo.t[:, 0:n], reads=[yo.b])
                        del jst[ji]

                for (ji, sc) in items:
                    pend.append((ji, sc, qk(ji, sc)))
                    if len(pend) > LOOK:
                        proc(*pend.pop(0))
                while pend:
                    proc(*pend.pop(0))
            phase_end(f"gqa{l}")

        def phase_nat(l):
            with ExitStack() as ph:
                def sb(nm, shape, dt):
                    return Tl(ph.enter_context(nc.sbuf_tensor(uname(nm), list(shape), dt)))
                KT = sb("nKT", [128, 4, T], BF16)
                QA = sb("nQA", [128, 4, T], BF16)
                QB = sb("nQB", [128, 4, T], BF16)
                S.op("pool", lambda e: e.memset(QA.t[64:128, :, :], 0.0), writes=[QA.b])
                S.op("pool", lambda e: e.memset(QB.t[0:64, :, :], 0.0), writes=[QB.b])
                V = sb("nV", [128, NCH, 512], BF16)
                TB = sb("nTB", [128, NSLOT, 512], BF16)
                identb = sb("nident", [128, 128], BF16)
                S.dma("pool", identb.t[:], ident_d, writes=[identb.b])
                MSK = sb("nMSK", [128, NSLOT, 64], F32)
                S.dma("sp", MSK.t[:], natm.rearrange("p (s q) -> p s q", q=64), writes=[MSK.b])
                sc_ring = Ring([sb(f"nsc{j}", [128, 512], F32) for j in range(2)])
                pt_ring = Ring([sb(f"npt{j}", [128, 512], BF16) for j in range(4)])
                rd_ring = Ring([sb(f"nrd{j}", [128, 512], F32) for j in range(3)])
                stg_ring = Ring([sb(f"nstg{j}", [128, 4, 512], BF16) for j in range(2)])
                acc_ring = Ring([(PS[0], PS[1]), (PS[2], PS[3])])
                s_ring = Ring([PS[4], PS[5], PS[6], PS[7]])
                for hg in range(2):
                    S.dma("sp", KT.t[:], NKT[hg * 512:(hg + 1) * 512, :].rearrange("(c p) t -> p c t", p=128), writes=[KT.b])
                    qsrc = NQT[hg * 512:(hg + 1) * 512, :].rearrange("(c p) t -> p c t", p=128)
                    S.dma("sp", QA.t[0:64, :, :], qsrc[0:64], writes=[QA.b])
                    S.dma("sp", QB.t[64:128, :, :], qsrc[64:128], writes=[QB.b])
                    S.dma("sp", V.t[:], NVt[:, hg * 512:(hg + 1) * 512].rearrange("(s p) c -> p s c", p=128), writes=[V.b])
                    S.dma("pool", TB.t[:].rearrange("p s c -> p (s c)"), natb[l, hg], writes=[TB.b])
                    for sl in range(NSLOT):
                        S.op("dve", lambda e: e.tensor_tensor(
                            TB.t[:, sl, :].rearrange("p (h q) -> p h q", h=8), TB.t[:, sl, :].rearrange("p (h q) -> p h q", h=8),
                            MSK.t[:, sl, :].unsqueeze(1).broadcast_to([128, 8, 64]), ALU.add),
                            reads=[TB.b, MSK.b], writes=[TB.b])
                    blocks = []
                    if l == 0:
                        blocks.append((0, [(b * 64, [(0, None), (1, None)]) for b in range(4)]))
                    for rg in range(8):
                        rows_ = []
                        for r in range(rg * 8, rg * 8 + 8):
                            rs = min(max(r - 4, 0), 56)
                            ch = []
                            if rs % 2 == 0:
                                for j in range(4):
                                    m = rs // 2 + j
                                    ch.append((2 + m, 2 * m - r + 7))
                            else:
                                m0 = (rs - 1) // 2
                                for j in range(5):
                                    m = m0 + j
                                    dr0 = 2 * m - r
                                    slot = 14 if j == 0 else (15 if j == 4 else dr0 + 7)
                                    ch.append((2 + m, slot))
                            ch += [(0, None), (1, None)]
                            rows_.append((NCTX + r * 64, ch))
                        blocks.append((NCTX + rg * 512, rows_))
                    LOOK = 3
                    jobs = []
                    for (bt0, rows_) in blocks:
                        for ri, (q0, chunks) in enumerate(rows_):
                            jobs.append((bt0, q0, chunks, ri == 0, ri == len(rows_) - 1, len(rows_) * 64))
                    items = [(ji, ci) for ji, jb in enumerate(jobs) for ci in range(len(jb[2]))]
                    jst = {}
                    cur_stg = [None]

                    def qk(ji, ci):
                        bt0, q0, chunks, first, lastrow, nb = jobs[ji]
                        chunk, slot = chunks[ci]
                        ps_ = s_ring.next()
                        if slot is not None:
                            S.op("pe", lambda e: e.matmul(ps_.t[:, :], identb.t[:], TB.t[:, slot, :], start=True, stop=False, skip_group_check=True),
                                 reads=[identb.b, TB.b], writes=[ps_.b])
                        for hh in range(8):
                            cc, half = hh // 2, hh % 2
                            Qh = QA if half == 0 else QB
                            S.op("pe", lambda e: e.matmul(ps_.t[:, hh * 64:(hh + 1) * 64], KT.t[:, cc, chunk * 128:(chunk + 1) * 128],
                                                          Qh.t[:, cc, q0:q0 + 64], start=(slot is None), stop=True, skip_group_check=True),
                                 reads=[KT.b, Qh.b], writes=[ps_.b])
                        return ps_

                    def proc(ji, ci, ps_):
                        bt0, q0, chunks, first, lastrow, nb = jobs[ji]
                        chunk, slot = chunks[ci]
                        last = len(chunks) - 1
                        if ci == 0:
                            jst[ji] = acc_ring.next()
                            for dfr in list(deferred):
                                if dfr[2] is jst[ji][0]:
                                    dfr[1]()
                                    deferred.remove(dfr)
                            if first:
                                cur_stg[0] = stg_ring.next()
                        po, pd = jst[ji]
                        stg = cur_stg[0]
                        pt = pt_ring.next()
                        S.op("act", lambda e: e.activation(out=pt.t[:], in_=ps_.t[:], func=AF.Exp), reads=[ps_.b], writes=[pt.b])
                        for hh in range(8):
                            cc = hh // 2
                            S.op("pe", lambda e: e.matmul(po.t[:, hh * 64:(hh + 1) * 64], V.t[:, chunk, cc * 128:(cc + 1) * 128],
                                                          pt.t[:, hh * 64:(hh + 1) * 64], start=(ci == 0 and hh == 0),
                                                          stop=(ci == last and hh == 7), skip_group_check=True),
                                 reads=[V.b, pt.b], writes=[po.b])
                        mm(pd.t[:, :], ones1.t[:], pt.t[:], ci == 0, ci == last, [ones1.b, pt.b], [pd.b])
                        if ci == last:
                            rd = rd_ring.next()

                            def fin_a(pd=pd, rd=rd):
                                S.op("act", lambda e: e.activation(out=rd.t[:], in_=pd.t[:, :], func=AF.Ln), reads=[pd.b], writes=[rd.b])
                                S.op("act", lambda e: e.activation(out=rd.t[:], in_=rd.t[:], func=AF.Exp, scale=-1.0), reads=[rd.b], writes=[rd.b])

                            def fin_b(po=po, rd=rd, stg=stg, q0=q0, bt0=bt0, lastrow=lastrow, nb=nb):
                                o0 = q0 - bt0
                                for hf in range(2):
                                    S.op("dve", lambda e: e.tensor_tensor(
                                        stg.t[hf * 64:(hf + 1) * 64, :, o0:o0 + 64],
                                        po.t[hf * 64:(hf + 1) * 64, :].rearrange("p (c h q) -> p c h q", c=4, h=2)[:, :, hf, :],
                                        rd.t[hf * 64:(hf + 1) * 64, :].rearrange("p (c h q) -> p c h q", c=4, h=2)[:, :, hf, :], ALU.mult),
                                        reads=[po.b, rd.b], writes=[stg.b])
                                if lastrow:
                                    S.dma("sp", YT[1][hg * 512:(hg + 1) * 512, bt0:bt0 + nb].rearrange("(c p) t -> p c t", p=128),
                                          stg.t[:, :, 0:nb], reads=[stg.b])
                            deferred.append([2, fin_a, po])
                            deferred.append([4, fin_b, po])
                            del jst[ji]
                        for dfr in list(deferred):
                            dfr[0] -= 1
                            if dfr[0] < 0:
                                dfr[1]()
                                deferred.remove(dfr)

                    deferred = []
                    pend = []
                    for (ji, ci) in items:
                        pend.append((ji, ci, qk(ji, ci)))
                        if len(pend) > LOOK:
                            proc(*pend.pop(0))
                    while pend:
                        proc(*pend.pop(0))
                    for dfr in deferred:
                        dfr[1]()
            phase_end(f"nat{l}")

        def phase_gla(l):
            with ExitStack() as ph:
                def sb(nm, shape, dt):
                    return Tl(ph.enter_context(nc.sbuf_tensor(uname(nm), list(shape), dt)))
                TRI = sb("gTRI", [128, 4, 128], F32)
                MSKG = sb("gMSK", [128, 2, 128], F32)
                S.dma("sp", TRI.t[:], tri_d.rearrange("a p t -> p a t"), writes=[TRI.b])
                S.dma("sp", MSKG.t[:], msk_d.rearrange("a p t -> p a t"), writes=[MSKG.b])
                qT = sb("gqT", [128, T], BF16)
                kT = sb("gkT", [128, T], BF16)
                ktok = sb("gktok", [128, NCH, 128], BF16)
                vtok = sb("gvtok", [128, NCH, 256], BF16)
                sp_ = [sb(f"gsp{d}", [128, NCH, 128], F32) for d in range(2)]
                O_ring = Ring([sb(f"gO{j}", [128, 2, T], F32) for j in range(2)])
                st32 = [sb(f"gst32{d}", [128, 256], F32) for d in range(2)]
                st16 = [sb(f"gst16{d}", [128, 256], BF16) for d in range(2)]
                Ep_ring = Ring([sb(f"gEp{j}", [128, 128], F32) for j in range(7)])
                En_ring = Ring([sb(f"gEn{j}", [128, 128], F32) for j in range(3)])
                Dk_ring = Ring([sb(f"gDk{j}", [128, 128], F32) for j in range(3)])
                qe_ring = Ring([sb(f"gqe{j}", [128, 128], BF16) for j in range(6)])
                kin_ring = Ring([sb(f"gkin{j}", [128, 128], BF16) for j in range(4)])
                kend_ring = Ring([sb(f"gkend{j}", [128, 128], BF16) for j in range(4)])
                att_ring = Ring([sb(f"gatt{j}", [128, 128], BF16) for j in range(4)])
                fsq_ring = Ring([sb(f"gfsq{j}", [128, 2, 512], BF16) for j in range(2)])
                frs_ring = Ring([sb(f"gfrs{j}", [128, 512], F32) for j in range(2)])
                fgg_ring = Ring([sb(f"gfgg{j}", [128, 2, 512], BF16) for j in range(2)])
                ftm_ring = Ring([sb(f"gftm{j}", [128, 512], F32) for j in range(2)])
                fy_ring = Ring([sb(f"gfy{j}", [128, 2, 512], BF16) for j in range(2)])
                orders = [list(range(NCH)), [1, 0] + list(range(NCH - 1, 1, -1))]
                gd_ring = Ring([PS[0], PS[1]])
                as_ring = Ring([PS[2], PS[3], PS[4]])
                o_ring = Ring([PS[5], PS[6], PS[7]])
                def head_loads(h):
                    S.dma("sp", sp_[0].t[:], SPF[:, h * 128:(h + 1) * 128].rearrange("(s p) c -> p s c", p=128), writes=[sp_[0].b])
                    S.dma("sp", sp_[1].t[:], SPB[:, h * 128:(h + 1) * 128].rearrange("(s p) c -> p s c", p=128), writes=[sp_[1].b])
                    S.dma("sp", qT.t[:], GQT[h * 128:(h + 1) * 128, :], writes=[qT.b])
                    S.dma("sp", kT.t[:], GKT[h * 128:(h + 1) * 128, :], writes=[kT.b])
                    S.dma("sp", ktok.t[:], GKt[:, h * 128:(h + 1) * 128].rearrange("(s p) c -> p s c", p=128), writes=[ktok.b])
                    S.dma("sp", vtok.t[:], GVt[:, h * 256:(h + 1) * 256].rearrange("(s p) c -> p s c", p=128), writes=[vtok.b])

                head_loads(0)
                for h in range(4):
                    O = O_ring.next()
                    touched = set()
                    for d in range(2):
                        S.op("pool", lambda e: e.memset(st32[d].t[:], 0.0), writes=[st32[d].b])
                        S.op("pool", lambda e: e.memset(st16[d].t[:], 0.0), writes=[st16[d].b])
                    items = []
                    for step in range(NCH):
                        for d in range(2):
                            items.append({"d": d, "c": orders[d][step]})

                    def g0(it):
                        d, c = it["d"], it["c"]
                        spc = sp_[d].t[:, c, :]
                        gd = gd_ring.next()
                        mm(gd.t[:, 0:128], spc, TRI.t[:, d, :], True, True, [sp_[d].b, TRI.b], [gd.b])
                        mm(gd.t[:, 128:256], TRI.t[:, 2 + d, :], spc, True, True, [sp_[d].b, TRI.b], [gd.b])
                        it["gd"] = gd

                    def g1(it):
                        gd = it["gd"]
                        Ep, En, Dk = Ep_ring.next(), En_ring.next(), Dk_ring.next()
                        S.op("act", lambda e: e.activation(out=Ep.t[:], in_=gd.t[:, 0:128], func=AF.Exp), reads=[gd.b], writes=[Ep.b])
                        S.op("act", lambda e: e.activation(out=En.t[:], in_=gd.t[:, 0:128], func=AF.Exp, scale=-1.0), reads=[gd.b], writes=[En.b])
                        S.op("act", lambda e: e.activation(out=Dk.t[:], in_=gd.t[:, 128:256], func=AF.Exp), reads=[gd.b], writes=[Dk.b])
                        it.update(Ep=Ep, En=En, Dk=Dk)

                    def g2(it):
                        c, Ep, En, Dk = it["c"], it["Ep"], it["En"], it["Dk"]
                        tk = slice(c * 128, (c + 1) * 128)
                        qe, kin, kend = qe_ring.next(), kin_ring.next(), kend_ring.next()
                        S.op("dve", lambda e: e.scalar_tensor_tensor(qe.t[:], qT.t[:, tk], 128.0 ** -0.5, Ep.t[:], ALU.mult, ALU.mult),
                             reads=[qT.b, Ep.b], writes=[qe.b])
                        S.op("pool", lambda e: e.tensor_tensor(kin.t[:], kT.t[:, tk], En.t[:], ALU.mult), reads=[kT.b, En.b], writes=[kin.b])
                        S.op("pool", lambda e: e.tensor_tensor(kend.t[:], ktok.t[:, c, :], Dk.t[:], ALU.mult), reads=[ktok.b, Dk.b], writes=[kend.b])
                        it.update(qe=qe, kin=kin, kend=kend)

                    def g3(it):
                        c = it["c"]
                        a_s = as_ring.next()
                        mm(a_s.t[:, 0:128], it["kin"].t[:], it["qe"].t[:], True, True, [it["kin"].b, it["qe"].b], [a_s.b])
                        mm(a_s.t[:, 128:384], it["kend"].t[:], vtok.t[:, c, :], True, True, [it["kend"].b, vtok.b], [a_s.b])
                        it["as"] = a_s

                    def g4(it):
                        d, a_s, Ep = it["d"], it["as"], it["Ep"]
                        att = att_ring.next()
                        S.op("dve", lambda e: e.tensor_tensor(att.t[:], a_s.t[:, 0:128], MSKG.t[:, d, :], ALU.mult), reads=[a_s.b, MSKG.b], writes=[att.b])
                        it["att"] = att

                    def g5(it):
                        d, c, qe, att, a_s, Ep = it["d"], it["c"], it["qe"], it["att"], it["as"], it["Ep"]
                        pO = o_ring.next()
                        for vc in range(2):
                            mm(pO.t[:, vc * 128:(vc + 1) * 128], vtok.t[:, c, vc * 128:(vc + 1) * 128], att.t[:], True, False,
                               [vtok.b, att.b], [pO.b])
                            mm(pO.t[:, vc * 128:(vc + 1) * 128], st16[d].t[:, vc * 128:(vc + 1) * 128], qe.t[:], False, True,
                               [st16[d].b, qe.b], [pO.b])
                        it["pO"] = pO
                        egl = Ep.t[:, 127:128] if d == 0 else Ep.t[:, 0:1]
                        S.op("dve", lambda e: e.scalar_tensor_tensor(st32[d].t[:], st32[d].t[:], egl, a_s.t[:, 128:384], ALU.mult, ALU.add),
                             reads=[st32[d].b, Ep.b, a_s.b], writes=[st32[d].b])

                    def g6(it):
                        d, c, pO = it["d"], it["c"], it["pO"]
                        tk = slice(c * 128, (c + 1) * 128)
                        S.op("act", lambda e: e.copy(st16[d].t[:], st32[d].t[:]), reads=[st32[d].b], writes=[st16[d].b])
                        if c in touched:
                            S.op("dve", lambda e: e.tensor_tensor(O.t[:, :, tk], O.t[:, :, tk], pO.t[:, 0:256].rearrange("p (v t) -> p v t", v=2), ALU.add),
                                 reads=[pO.b, O.b], writes=[O.b])
                        else:
                            touched.add(c)
                            S.op("dve", lambda e: e.tensor_copy(O.t[:, :, tk], pO.t[:, 0:256].rearrange("p (v t) -> p v t", v=2)),
                                 reads=[pO.b], writes=[O.b])

                    gst = [g0, g1, g2, g3, g4, g5, g6]
                    NS = len(gst)
                    for step in range(len(items) + NS - 1):
                        for si in range(NS - 1, -1, -1):
                            i = step - si
                            if 0 <= i < len(items):
                                gst[si](items[i])
                    if h + 1 < 4:
                        head_loads(h + 1)
                    for (t0, n, kind) in tiles(512, l == 0):
                        fsq = fsq_ring.next()
                        S.op("act", lambda e: e.activation(out=fsq.t[:, :, 0:n], in_=O.t[:, :, t0:t0 + n], func=AF.Square), reads=[O.b], writes=[fsq.b])
                        pM = ps_ring.next()
                        for vc in range(2):
                            mm(pM.t[:, 0:n], ones256.t[:], fsq.t[:, vc, 0:n], vc == 0, vc == 1, [ones256.b, fsq.b], [pM.b])
                        frs = frs_ring.next()
                        S.op("act", lambda e: e.activation(out=frs.t[:, 0:n], in_=pM.t[:, 0:n], func=AF.Ln, bias=EPS), reads=[pM.b], writes=[frs.b])
                        S.op("act", lambda e: e.activation(out=frs.t[:, 0:n], in_=frs.t[:, 0:n], func=AF.Exp, scale=-0.5), reads=[frs.b], writes=[frs.b])
                        fgg = fgg_ring.next()
                        S.dma("sp", fgg.t[:, :, 0:n], GGT[h * 256:(h + 1) * 256, t0:t0 + n].rearrange("(v p) t -> p v t", p=128), writes=[fgg.b])
                        fy = fy_ring.next()
                        for vc in range(2):
                            ftm = ftm_ring.next()
                            S.op("dve", lambda e: e.scalar_tensor_tensor(ftm.t[:, 0:n], O.t[:, vc, t0:t0 + n], GLG[l].t[:, vc:vc + 1], frs.t[:, 0:n],
                                                                         ALU.mult, ALU.mult), reads=[O.b, GLG[l].b, frs.b], writes=[ftm.b])
                            S.op("pool", lambda e: e.tensor_tensor(fy.t[:, vc, 0:n], ftm.t[:, 0:n], fgg.t[:, vc, 0:n], ALU.mult),
                                 reads=[ftm.b, fgg.b], writes=[fy.b])
                        S.dma("sp", YT[0][h * 256:(h + 1) * 256, t0:t0 + n].rearrange("(v p) t -> p v t", p=128), fy.t[:, :, 0:n], reads=[fy.b])
            phase_end(f"gla{l}")

        def phase_merge(l):
            NT = 256
            with ExitStack() as ph:
                def sb(nm, shape, dt):
                    return Tl(ph.enter_context(nc.sbuf_tensor(uname(nm), list(shape), dt)))
                WB = [sb(f"mWB{i}", [128, 8, D], BF16) for i in range(3)]
                WO = sb("mWO", [128, 8, D], BF16)
                for i in range(3):
                    S.dma("pool", WB[i].t[:], w_br[l, i].rearrange("(k p) n -> p k n", p=128), writes=[WB[i].b])
                S.dma("pool", WO.t[:], w_o[l].rearrange("(k p) n -> p k n", p=128), writes=[WO.b])
                y_ring = Ring([sb(f"my{j}", [128, 8, NT], BF16) for j in range(9)])
                sg_ring = Ring([sb(f"msg{j}", [128, 8, NT], BF16) for j in range(9)])
                xt_ring = Ring([sb(f"mxt{j}", [128, 8, NT], F32) for j in range(3)])
                z_ring = Ring([sb(f"mz{j}", [128, 8, NT], BF16) for j in range(2)])
                ta_ring = Ring([sb(f"mta{j}", [128, NT], F32) for j in range(2)])
                tb_ring = Ring([sb(f"mtb{j}", [128, NT], F32) for j in range(2)])
                tc_ring = Ring([sb(f"mtc{j}", [128, NT], F32) for j in range(2)])
                tl = tiles(NT, l == 0)
                st = [dict() for _ in tl]

                def loads(ti):
                    t0, n, kind = tl[ti]
                    ys, sgs = [], []
                    for i in range(3):
                        y = y_ring.next()
                        S.dma("sp", y.t[:, :, 0:n], YT[i].rearrange("(c p) t -> p c t", p=128)[:, :, t0:t0 + n], writes=[y.b])
                        sg = sg_ring.next()
                        S.dma("sp", sg.t[:, :, 0:n], SGT[i].rearrange("(c p) t -> p c t", p=128)[:, :, t0:t0 + n], writes=[sg.b])
                        ys.append(y)
                        sgs.append(sg)
                    xt = xt_ring.next()
                    S.dma("sp", xt.t[:, :, 0:n], XTv[:, :, t0:t0 + n], writes=[xt.b])
                    st[ti].update(ys=ys, sgs=sgs, xt=xt)

                def branch(ti):
                    t0, n, kind = tl[ti]
                    ys, sgs = st[ti]["ys"], st[ti]["sgs"]
                    z = z_ring.next()
                    st[ti]["z"] = z
                    for nn in range(8):
                        pp = []
                        for i in range(3):
                            p = ps_ring.next()
                            for k in range(8):
                                mm(p.t[:, 0:n], WB[i].t[:, k, nn * 128:(nn + 1) * 128], ys[i].t[:, k, 0:n], k == 0, k == 7,
                                   [WB[i].b, ys[i].b], [p.b])
                            pp.append(p)
                        ta, tb, tc = ta_ring.next(), tb_ring.next(), tc_ring.next()
                        for (tt_, i) in ((ta, 0), (tb, 1), (tc, 2)):
                            S.op("dve", lambda e: e.tensor_tensor(tt_.t[:, 0:n], pp[i].t[:, 0:n], sgs[i].t[:, nn, 0:n], ALU.mult),
                                 reads=[pp[i].b, sgs[i].b], writes=[tt_.b])
                        S.op("pool", lambda e: e.tensor_tensor(ta.t[:, 0:n], ta.t[:, 0:n], tb.t[:, 0:n], ALU.add), reads=[ta.b, tb.b], writes=[ta.b])
                        S.op("pool", lambda e: e.tensor_tensor(z.t[:, nn, 0:n], ta.t[:, 0:n], tc.t[:, 0:n], ALU.add), reads=[ta.b, tc.b], writes=[z.b])

                def outproj(ti):
                    t0, n, kind = tl[ti]
                    der = DER[l][kind]
                    z, xt = st[ti]["z"], st[ti]["xt"]
                    for nn in range(8):
                        p = ps_ring.next()
                        for k in range(8):
                            mm(p.t[:, 0:n], WO.t[:, k, nn * 128:(nn + 1) * 128], z.t[:, k, 0:n], k == 0, k == 7, [WO.b, z.b], [p.b])
                        S.op("dve", lambda e: e.scalar_tensor_tensor(xt.t[:, nn, 0:n], p.t[:, 0:n], der.t[:, 5, nn:nn + 1], xt.t[:, nn, 0:n],
                                                                     ALU.mult, ALU.add), reads=[p.b, der.b, xt.b], writes=[xt.b])
                    S.dma("sp", XTv[:, :, t0:t0 + n], xt.t[:, :, 0:n], reads=[xt.b])
                    st[ti].clear()

                loads(0)
                if len(tl) > 1:
                    loads(1)
                branch(0)
                for ti in range(len(tl)):
                    if ti + 2 < len(tl):
                        loads(ti + 2)
                    if ti + 1 < len(tl):
                        branch(ti + 1)
                    outproj(ti)
            phase_end(f"merge{l}")

        def phase_final():
            with ExitStack() as ph:
                def sb(nm, shape, dt):
                    return Tl(ph.enter_context(nc.sbuf_tensor(uname(nm), list(shape), dt)))
                xin_ring = Ring([sb(f"fx{i}", [128, 8, 512], F32) for i in range(2)])
                xo_ring = Ring([sb(f"fo{i}", [128, D], F32) for i in range(3)])
                ev = 0
                for g in range(8):
                    xi = xin_ring.next()
                    t0 = NCTX + g * 512
                    S.dma("sp", xi.t[:], XTv[:, :, t0:t0 + 512], writes=[xi.b])
                    for j in range(4):
                        xo = xo_ring.next()
                        for half in range(2):
                            p = ps_ring.next()
                            for q in range(4):
                                c = half * 4 + q
                                S.op("pe", lambda e: e.transpose(p.t[:, q * 128:(q + 1) * 128], xi.t[:, c, j * 128:(j + 1) * 128],
                                                                 ident.t[:]), reads=[xi.b, ident.b], writes=[p.b])
                            if ev % 2 == 0:
                                S.op("act", lambda e: e.copy(xo.t[:, half * 512:(half + 1) * 512], p.t[:]), reads=[p.b], writes=[xo.b])
                            else:
                                S.op("dve", lambda e: e.tensor_copy(xo.t[:, half * 512:(half + 1) * 512], p.t[:]), reads=[p.b],
                                     writes=[xo.b])
                            ev += 1
                        r0 = g * 512 + j * 128
                        S.dma("sp", out_d[r0:r0 + 128, :], xo.t[:], reads=[xo.b])
            phase_end("final")

        table = {"prep": phase_prep, "final": phase_final}
        order = ["prep"]
        for l in range(2):
            table[f"ffn{l}0"] = (lambda l=l: phase_ffn(l, 0, True, f"ffn{l}0"))
            table[f"inproj{l}"] = (lambda l=l: phase_inproj(l))
            table[f"gla{l}"] = (lambda l=l: phase_gla(l))
            table[f"nat{l}"] = (lambda l=l: phase_nat(l))
            table[f"gqa{l}"] = (lambda l=l: phase_gqa(l))
            table[f"merge{l}"] = (lambda l=l: phase_merge(l))
            table[f"ffn{l}1"] = (lambda l=l: phase_ffn(l, 1, l == 0, f"ffn{l}1"))
            order += [f"ffn{l}0", f"inproj{l}", f"gla{l}", f"nat{l}", f"gqa{l}", f"merge{l}", f"ffn{l}1"]
        for name in (phases or order):
            table[name]()
        S.barrier()
        print("instructions:", S.n_ins, {k: v for k, v in S.cnt.items()})
    return nc


def host_constants():
    c = {}
    c["ident"] = np.eye(128, dtype=np.float32)
    perm = np.zeros((128, 128), np.float32)
    for m in range(128):
        blk, r = divmod(m, 64)
        src = blk * 64 + (r + 32) % 64
        perm[src, m] = 1.0
    c["perm"] = perm
    half = 32
    freqs = (10000.0 ** (-np.arange(half, dtype=np.float32) / half)).astype(np.float32)
    tt = np.arange(NLAT)
    rows = (tt // 64).astype(np.float32)
    cols = (tt % 64).astype(np.float32)
    C = np.ones((128, T), np.float32)
    Sg = np.zeros((128, T), np.float32)
    for blk, pos in enumerate((rows, cols)):
        ang = pos[None, :] * freqs[:, None]
        cs, sn = np.cos(ang).astype(np.float32), np.sin(ang).astype(np.float32)
        C[blk * 64:blk * 64 + 32, NCTX:] = cs
        C[blk * 64 + 32:blk * 64 + 64, NCTX:] = cs
        Sg[blk * 64:blk * 64 + 32, NCTX:] = -sn
        Sg[blk * 64 + 32:blk * 64 + 64, NCTX:] = sn
    c["ropec"] = C
    c["ropes"] = Sg
    i = np.arange(128)
    a = -1.0 / 16.0
    tri = np.zeros((4, 128, 128), np.float32)
    tri[0] = a * (i[:, None] <= i[None, :])
    tri[1] = a * (i[:, None] >= i[None, :])
    tri[2] = a * (i[:, None] > i[None, :])
    tri[3] = a * (i[:, None] < i[None, :])
    c["tri"] = tri
    gm = np.zeros((2, 128, 128), np.float32)
    gm[0] = (i[:, None] <= i[None, :])
    gm[1] = (i[:, None] >= i[None, :])
    c["gmask"] = gm
    kc = np.arange(64)
    qc = np.arange(64)
    cstart = np.clip(qc - 8, 0, 48)
    colok = (kc[:, None] >= cstart[None, :]) & (kc[:, None] < cstart[None, :] + 16)
    m = np.where(colok, 0.0, -1e30).astype(np.float32)
    natm = np.zeros((128, NSLOT, 64), np.float32)
    for s in range(NSLOT):
        natm[0:64, s] = m
        natm[64:128, s] = m
    natm[0:64, 14] = -1e30
    natm[64:128, 15] = -1e30
    c["natm"] = natm.reshape(128, NSLOT * 64)
    return c


def gather_nat_bias(rpb):
    p = np.arange(128)
    half = p // 64
    kc = p % 64
    qc = np.arange(64)
    dc = np.clip(kc[:, None] - qc[None, :] + 15, 0, 30)
    out = np.empty((2, 2, 128, NSLOT, 8, 64), np.float32)
    for s in range(NSLOT):
        dr = np.clip(SLOT_DR0[s] + half + 7, 0, 14)
        for hg in range(2):
            for h in range(8):
                out[:, hg, :, s, h, :] = rpb[:, hg * 8 + h][:, dr[:, None], dc]
    return out.reshape(2, 2, 128, NSLOT * 512)


_CACHE = {}


def kernel(x, c, ctx, c_ctx, w_mod, b_mod, norm_g, ffn_w_in, ffn_w_out, w_in, gla_fg_w2, gla_fg_b,
           gla_norm_g, nat_q_norm, nat_k_norm, nat_rpb, gqa_q_norm, gqa_k_norm, w_branch, w_out):
    f = lambda a: np.ascontiguousarray(np.asarray(a, dtype=np.float32))
    if "nc" not in _CACHE:
        _CACHE["nc"] = build_program()
    nc = _CACHE["nc"]
    consts = host_constants()
    shared = dict(
        w_mod=f(w_mod), b_mod=f(b_mod).reshape(2, 72, 128), norm_g=f(norm_g).reshape(2, 24, 128),
        ffn_w_in=f(ffn_w_in), ffn_w_out=f(ffn_w_out), w_in=f(w_in), gla_fg_w2=f(gla_fg_w2),
        gla_fg_b=f(gla_fg_b).reshape(2, 2, 1, 512), gla_norm_g=f(gla_norm_g).reshape(2, 2, 128),
        nat_q_norm=f(nat_q_norm).reshape(2, 1, 64), nat_k_norm=f(nat_k_norm).reshape(2, 1, 64),
        gqa_q_norm=f(gqa_q_norm).reshape(2, 1, 128), gqa_k_norm=f(gqa_k_norm).reshape(2, 1, 128),
        w_branch=f(w_branch), w_out=f(w_out), natb=gather_nat_bias(f(nat_rpb)), **consts)
    x = f(x)
    ctx = f(ctx)
    c = f(c)
    c_ctx = f(c_ctx)
    in_maps = []
    for b in range(8):
        m = dict(shared)
        m["x"] = x[b]
        m["ctx"] = ctx[b]
        m["cc"] = np.ascontiguousarray(np.stack([c[b], c_ctx]).reshape(16, 128))
        in_maps.append(m)
    res = run_bass_kernel_spmd(nc, in_maps, core_ids=list(range(8)))
    return np.stack([np.asarray(r["out"], dtype=np.float32) for r in res.results], axis=0)
```
